# Optimizing a Trainium2 kernel written in Bass

```python
import jax, jax.numpy as jnp
from jax import lax
import numpy as np

D_MODEL = 1024
BATCH = 2
SEQ = 16384
DEPTH = 2

GRID_W = 64
CTX_LEN = 256
N_MIXERS = 2
N_MOD = 9
D_FF = 2816
GMLP_D_FF = 6 * D_MODEL
GMLP_HALF = GMLP_D_FF // 2
CHUNK = 128
GMLP_GROUPS = 8
HEAD_DIM = 128
N_HEADS = D_MODEL // HEAD_DIM
N_KV_HEADS = 2
Q_DIM = N_HEADS * HEAD_DIM
KV_DIM = N_KV_HEADS * HEAD_DIM
ROPE_THETA = 10000.0
Q_BLOCK = 128
LN_EPS = 1e-5
RMS_EPS = 1e-6
ALPHA = (2.0 * DEPTH) ** 0.25
BETA = (8.0 * DEPTH) ** -0.25
N_A = (DEPTH + 1) // 2
N_B = DEPTH // 2

kernel_name = "hybrid_gmlp_gqa_macaron_deepnorm_dit"


def layer_norm(x, g, b):
    xf = x.astype(jnp.float32)
    mu = jnp.mean(xf, axis=-1, keepdims=True)
    var = jnp.mean(jnp.square(xf - mu), axis=-1, keepdims=True)
    return ((xf - mu) * lax.rsqrt(var + LN_EPS) * g + b).astype(x.dtype)


def rms_norm(x, g):
    xf = x.astype(jnp.float32)
    return (xf * lax.rsqrt(jnp.mean(jnp.square(xf), axis=-1, keepdims=True) + RMS_EPS) * g).astype(x.dtype)


def modulation(cond, w_mod, b_mod):
    m = jax.nn.silu(cond) @ w_mod + b_mod
    return jnp.split(m[..., None, :], N_MOD, axis=-1)


def modulate(x, shift, scale):
    return x * (1.0 + scale) + shift


def swiglu(h, w_in, w_out):
    g, u = jnp.split(h @ w_in, 2, axis=-1)
    return (jax.nn.silu(g) * u) @ w_out


def ffn_sublayer(x, shift, scale, gate, w_in, w_out, ln_g, ln_b):
    y = swiglu(modulate(x, shift, scale), w_in, w_out)
    return layer_norm(ALPHA * x + 0.5 * gate * y, ln_g, ln_b)


def gmlp_mix(h, w_in, sgu_g, sgu_b, w_s, b_s, w_out):
    b_, t_, _ = h.shape
    u, v = jnp.split(jax.nn.gelu(h @ w_in, approximate=False), 2, axis=-1)
    v = layer_norm(v, sgu_g, sgu_b)
    v = v.reshape(b_, t_ // CHUNK, CHUNK, GMLP_GROUPS, GMLP_HALF // GMLP_GROUPS)
    v = jnp.einsum('gpq,bnqgc->bnpgc', w_s, v) + b_s.T[None, None, :, :, None]
    return (u * v.reshape(b_, t_, GMLP_HALF)) @ w_out


def rope_tables(rows, cols):
    quarter = HEAD_DIM // 4
    freqs = ROPE_THETA ** (-jnp.arange(quarter, dtype=jnp.float32) / quarter)
    ang = jnp.concatenate([rows.astype(jnp.float32)[:, None] * freqs,
                           cols.astype(jnp.float32)[:, None] * freqs], axis=-1)
    return jnp.cos(ang), jnp.sin(ang)


def apply_rope(x, cos, sin):
    x1, x2 = jnp.split(x, 2, axis=-1)
    out = jnp.concatenate([x1 * cos - x2 * sin, x2 * cos + x1 * sin], axis=-1)
    return out.astype(x.dtype)


def split_heads(x, n):
    b_, t_, _ = x.shape
    return x.reshape(b_, t_, n, HEAD_DIM).transpose(0, 2, 1, 3)


def merge_heads(x):
    b_, h_, t_, d_ = x.shape
    return x.transpose(0, 2, 1, 3).reshape(b_, t_, h_ * d_)


def attend(q, k, v):
    b_, h_, tq, hd = q.shape
    qg = q.reshape(b_, N_KV_HEADS, h_ // N_KV_HEADS, tq, hd)
    s = jnp.einsum('bkgqd,bktd->bkgqt', qg, k, preferred_element_type=jnp.float32) * (hd ** -0.5)
    p = jax.nn.softmax(s, axis=-1)
    o = jnp.einsum('bkgqt,bktd->bkgqd', p.astype(v.dtype), v)
    return o.reshape(b_, h_, tq, hd)


def block_attend(q, k, v):
    b_, h_, t_, hd = q.shape
    nb = t_ // Q_BLOCK
    qb = q.reshape(b_, h_, nb, Q_BLOCK, hd).transpose(2, 0, 1, 3, 4)
    ob = lax.map(lambda qblk: attend(qblk, k, v), qb)
    return ob.transpose(1, 2, 0, 3, 4).reshape(b_, h_, t_, hd)


def gqa_mix(h_lat, h_ctx, w_qkv, q_norm_g, k_norm_g, w_o, cos, sin, with_ctx_out):
    kv_c = h_ctx @ w_qkv[:, Q_DIM:]
    k_c = rms_norm(split_heads(kv_c[..., :KV_DIM], N_KV_HEADS), k_norm_g)
    v_c = split_heads(kv_c[..., KV_DIM:], N_KV_HEADS)
    qkv = h_lat @ w_qkv
    q_l = apply_rope(rms_norm(split_heads(qkv[..., :Q_DIM], N_HEADS), q_norm_g), cos, sin)
    k_l = apply_rope(rms_norm(split_heads(qkv[..., Q_DIM:Q_DIM + KV_DIM], N_KV_HEADS), k_norm_g), cos, sin)
    v_l = split_heads(qkv[..., Q_DIM + KV_DIM:], N_KV_HEADS)
    k_all = jnp.concatenate([k_c, k_l], axis=2)
    v_all = jnp.concatenate([v_c, v_l], axis=2)
    y_lat = merge_heads(block_attend(q_l, k_all, v_all)) @ w_o
    if not with_ctx_out:
        return y_lat, None
    q_c = rms_norm(split_heads(h_ctx @ w_qkv[:, :Q_DIM], N_HEADS), q_norm_g)
    y_ctx = merge_heads(attend(q_c, k_c, v_c)) @ w_o
    return y_lat, y_ctx


def setup_inputs(seed: int = 0) -> dict:
    key = jax.random.key(seed)
    ks = jax.random.split(key, 20)
    f32 = jnp.float32
    nrm = lambda k, shape, s: jax.random.normal(k, shape, f32) * s
    return {
        "x": nrm(ks[0], (BATCH, SEQ, D_MODEL), 1.0),
        "c": nrm(ks[1], (BATCH, D_MODEL), 1.0),
        "ctx": nrm(ks[2], (BATCH, CTX_LEN, D_MODEL), 1.0),
        "c_ctx": nrm(ks[3], (D_MODEL,), 1.0),
        "w_mod": nrm(ks[4], (DEPTH, D_MODEL, N_MOD * D_MODEL), D_MODEL ** -0.5),
        "b_mod": nrm(ks[5], (DEPTH, N_MOD * D_MODEL), 0.02),
        "ln_g": 1.0 + nrm(ks[6], (DEPTH, 3, D_MODEL), 0.02),
        "ln_b": nrm(ks[7], (DEPTH, 3, D_MODEL), 0.02),
        "ffn_w_in": nrm(ks[8], (DEPTH, 2, D_MODEL, 2 * D_FF), D_MODEL ** -0.5),
        "ffn_w_out": nrm(ks[9], (DEPTH, 2, D_FF, D_MODEL), BETA * D_FF ** -0.5),
        "gmlp_w_in": nrm(ks[10], (N_A, D_MODEL, GMLP_D_FF), D_MODEL ** -0.5),
        "gmlp_ln_g": 1.0 + nrm(ks[11], (N_A, GMLP_HALF), 0.02),
        "gmlp_ln_b": nrm(ks[12], (N_A, GMLP_HALF), 0.02),
        "gmlp_w_s": nrm(ks[13], (N_A, GMLP_GROUPS, CHUNK, CHUNK), CHUNK ** -0.5),
        "gmlp_b_s": 1.0 + nrm(ks[14], (N_A, GMLP_GROUPS, CHUNK), 0.02),
        "gmlp_w_out": nrm(ks[15], (N_A, GMLP_HALF, D_MODEL), BETA * GMLP_HALF ** -0.5),
        "attn_w_qkv": nrm(ks[16], (N_B, D_MODEL, Q_DIM + 2 * KV_DIM), D_MODEL ** -0.5),
        "attn_q_norm": 1.0 + nrm(ks[17], (N_B, HEAD_DIM), 0.02),
        "attn_k_norm": 1.0 + nrm(ks[18], (N_B, HEAD_DIM), 0.02),
        "attn_w_o": nrm(ks[19], (N_B, Q_DIM, D_MODEL), BETA * Q_DIM ** -0.5),
    }


def reference(x, c, ctx, c_ctx, w_mod, b_mod, ln_g, ln_b, ffn_w_in, ffn_w_out,
              gmlp_w_in, gmlp_ln_g, gmlp_ln_b, gmlp_w_s, gmlp_b_s, gmlp_w_out,
              attn_w_qkv, attn_q_norm, attn_k_norm, attn_w_o):
    n_tok = x.shape[1]
    ROWS = n_tok // GRID_W
    rows = jnp.repeat(jnp.arange(ROWS, dtype=jnp.int32), GRID_W)
    cols = jnp.tile(jnp.arange(GRID_W, dtype=jnp.int32), ROWS)
    cos, sin = rope_tables(rows, cols)

    x_lat, x_ctx = x, ctx
    for i in range(DEPTH):
        is_attn = (i % N_MIXERS) == 1
        j = i // N_MIXERS
        last = i == DEPTH - 1
        ctx_feeds_mixer = (not last) or is_attn
        m_lat = modulation(c, w_mod[i], b_mod[i])
        m_ctx = modulation(c_ctx, w_mod[i], b_mod[i])

        x_lat = ffn_sublayer(x_lat, m_lat[0], m_lat[1], m_lat[2],
                             ffn_w_in[i, 0], ffn_w_out[i, 0], ln_g[i, 0], ln_b[i, 0])
        if ctx_feeds_mixer:
            x_ctx = ffn_sublayer(x_ctx, m_ctx[0], m_ctx[1], m_ctx[2],
                                 ffn_w_in[i, 0], ffn_w_out[i, 0], ln_g[i, 0], ln_b[i, 0])

        h_lat = modulate(x_lat, m_lat[3], m_lat[4])
        if is_attn:
            h_ctx = modulate(x_ctx, m_ctx[3], m_ctx[4])
            y_lat, y_ctx = gqa_mix(h_lat, h_ctx, attn_w_qkv[j], attn_q_norm[j], attn_k_norm[j],
                                   attn_w_o[j], cos, sin, not last)
        else:
            y_lat = gmlp_mix(h_lat, gmlp_w_in[j], gmlp_ln_g[j], gmlp_ln_b[j],
                             gmlp_w_s[j], gmlp_b_s[j], gmlp_w_out[j])
            y_ctx = None
            if not last:
                h_ctx = modulate(x_ctx, m_ctx[3], m_ctx[4])
                y_ctx = gmlp_mix(h_ctx, gmlp_w_in[j], gmlp_ln_g[j], gmlp_ln_b[j],
                                 gmlp_w_s[j], gmlp_b_s[j], gmlp_w_out[j])
        x_lat = layer_norm(ALPHA * x_lat + m_lat[5] * y_lat, ln_g[i, 1], ln_b[i, 1])
        if not last:
            x_ctx = layer_norm(ALPHA * x_ctx + m_ctx[5] * y_ctx, ln_g[i, 1], ln_b[i, 1])

        x_lat = ffn_sublayer(x_lat, m_lat[6], m_lat[7], m_lat[8],
                             ffn_w_in[i, 1], ffn_w_out[i, 1], ln_g[i, 2], ln_b[i, 2])
        if not last:
            x_ctx = ffn_sublayer(x_ctx, m_ctx[6], m_ctx[7], m_ctx[8],
                                 ffn_w_in[i, 1], ffn_w_out[i, 1], ln_g[i, 2], ln_b[i, 2])
    return x_lat
```

```python
from contextlib import ExitStack
import math
import os
import numpy as np
G1MODE = int(os.environ.get('G1MODE', '9'))
G1GELU = int(os.environ.get('G1GELU', '1'))
import concourse.bass as bass
import concourse.mybir as mybir
from concourse.bass_utils import run_bass_kernel_spmd

F32 = mybir.dt.float32
BF16 = mybir.dt.bfloat16
I32 = mybir.dt.int32
AF = mybir.ActivationFunctionType
ALU = mybir.AluOpType
AX = mybir.AxisListType

ENGS = ["pe", "act", "dve", "pool", "sp"]
SEM_LIMIT = 60000


class Prog:
    def __init__(self):
        self.ops = []
        self.last_w = {}
        self.readers = {}
        self.bar_from = 0

    def add(self, eng, emit, reads=(), writes=(), dma=None, ndma=1, inc=16, nobar=False):
        i = len(self.ops)
        deps = {}
        for r in reads:
            d = self.last_w.get(r)
            if d is not None:
                deps[d] = "raw"
        for w in writes:
            d = self.last_w.get(w)
            if d is not None:
                deps.setdefault(d, "waw")
            for rd in self.readers.get(w, ()):
                if rd != i:
                    deps.setdefault(rd, "war")
        for r in reads:
            self.readers.setdefault(r, []).append(i)
        for w in writes:
            self.last_w[w] = i
            self.readers[w] = []
        self.ops.append(dict(eng=eng, emit=emit, deps=deps, dma=dma, ndma=ndma, inc=inc, nobar=nobar))
        return i

    def barrier(self):
        lo, hi = self.bar_from, len(self.ops)
        last = {}
        for i in range(lo, hi):
            op = self.ops[i]
            if op["nobar"]:
                continue
            if op["dma"] is not None:
                last[("dma", op["dma"])] = i
            else:
                last[("eng", op["eng"])] = i
        deps = {i: "raw" for i in last.values()}
        for e in ENGS:
            self.ops.append(dict(eng=e, emit=lambda e_: None, deps=dict(deps), dma=None, ndma=1, inc=16,
                                 nobar=True, isbar=True))
        self.bar_from = len(self.ops)

    def _needs_wait(self, op, dop, kind):
        if dop["dma"] is not None:
            return True
        if dop["eng"] != op["eng"]:
            return True
        if op["dma"] is not None or op.get("isbar"):
            return True
        if op["eng"] == "pe":
            return False
        return kind == "raw"

    def emit_all(self, nc, block, sems):
        ops = self.ops
        n = len(ops)
        need_sig = [False] * n
        for op in ops:
            for d, kind in op["deps"].items():
                dop = ops[d]
                if dop["dma"] is None and self._needs_wait(op, dop, kind):
                    need_sig[d] = True
        cnt = {e: 0 for e in ENGS}
        dcnt = {}
        for i, op in enumerate(ops):
            if op["dma"] is not None:
                k = op["dma"]
                dcnt[k] = dcnt.get(k, 0) + op["inc"] * op["ndma"]
                op["dval"] = dcnt[k]
            elif need_sig[i]:
                cnt[op["eng"]] += 1
                op["sig"] = cnt[op["eng"]]
        self.stats = dict(cnt=cnt, dma_keys=len(dcnt), nops=n, maxdma=max(dcnt.values()) if dcnt else 0)
        assert max(cnt.values()) < SEM_LIMIT, cnt
        assert self.stats["maxdma"] < SEM_LIMIT, self.stats
        eng_sem = {e: sems(f"prog_{e}") for e in ENGS if cnt[e] > 0}
        dma_sem = {k: sems(f"dma_{k}") for k in dcnt}

        def run_engine(ename, e):
            seen = {}
            for i, op in enumerate(ops):
                if op["eng"] != ename:
                    continue
                waits = {}
                for d, kind in op["deps"].items():
                    dop = ops[d]
                    if not self._needs_wait(op, dop, kind):
                        continue
                    if dop["dma"] is not None:
                        s, v = dma_sem[dop["dma"]], dop["dval"]
                    else:
                        s, v = eng_sem[dop["eng"]], dop["sig"]
                    key = id(s)
                    if waits.get(key, (None, 0))[1] < v:
                        waits[key] = (s, v)
                for key, (s, v) in waits.items():
                    if seen.get(key, 0) < v:
                        e.wait_ge(s, v)
                        seen[key] = v
                r = op["emit"](e)
                if op["dma"] is not None:
                    insts = r if isinstance(r, (list, tuple)) else [r]
                    assert len(insts) == op["ndma"], (len(insts), op["ndma"], op["dma"])
                    for ins in insts:
                        ins.then_inc(dma_sem[op["dma"]], op["inc"])
                elif need_sig[i]:
                    assert r is not None, f"op {i} on {ename} must return an instruction"
                    r.then_inc(eng_sem[ename], 1)

        @block.tensor
        def _(e):
            run_engine("pe", e)

        @block.scalar
        def _(e):
            run_engine("act", e)

        @block.vector
        def _(e):
            run_engine("dve", e)

        @block.gpsimd
        def _(e):
            run_engine("pool", e)

        @block.sync
        def _(e):
            run_engine("sp", e)


D = 1024
NCH = 34
NLAT = 32
DFF = 2816
ALPHA = 4.0 ** 0.25
LN_EPS = 1e-5
RMS_EPS = 1e-6
SCALE = 128.0 ** -0.5
NKC = 130
GROUPS = [[0, 1, 2, 3], [4, 5, 6, 7]]
PHASES = ["P0", "FA00", "FB00", "G1", "G2", "FA01", "FB01", "FA10", "FB10", "QKV", "ATT", "FA11", "FB11"]


def build(stop_after=None):
    nc = bass.Bass("TRN2", target_bir_lowering=False)

    def din(name, shape, dt=F32):
        return nc.dram_tensor(name, list(shape), dt, kind="ExternalInput").ap()

    def dscr(name, shape, dt):
        return nc.dram_tensor(name, list(shape), dt).ap()

    xin = din("xin", [NCH * 128, D])
    cond = din("cond", [128, 8, 2])
    posrc = din("posrc", [128, NLAT, 2])
    freq2 = din("freq2", [128, 64])
    identd = din("ident", [128, 128])
    w_mod = din("w_mod", [2, D, 9 * D])
    bmod2 = din("bmod2", [2, 2, 9 * D])
    ln_g = din("ln_g", [2, 3, D])
    ln_b = din("ln_b", [2, 3, D])
    ffn_w_in = din("ffn_w_in", [2, 2, D, 2 * DFF])
    ffn_w_out = din("ffn_w_out", [2, 2, DFF, D])
    gm_w_in = din("gm_w_in", [D, 6144])
    glng = din("glng", [128, 24])
    glnb = din("glnb", [128, 24])
    wsT_d = din("wsT", [128, 8, 128])
    gbs = din("gbs", [8 * 128])
    gm_w_out = din("gm_w_out", [3072, D])
    w_qkv = din("w_qkv", [D, 1536])
    qn_d = din("qn", [128])
    kn_d = din("kn", [128])
    w_o_d = din("w_o", [D, D])
    out = nc.dram_tensor("out", [NLAT * 128, D], F32, kind="ExternalOutput").ap()

    xa = dscr("xa", [NCH * 128, D], F32)
    xb = dscr("xb", [NCH * 128, D], F32)
    hT_d = dscr("hT_d", [NCH, 128, 24, 128], BF16)
    qT_d = dscr("qT_d", [NLAT, 128, 1024], BF16)
    kin = [nc.dram_tensor(f"kin{i}", [128, 4096], BF16) for i in range(2)]
    vin = [nc.dram_tensor(f"vin{i}", [128, 4096], BF16) for i in range(2)]
    kout = [nc.dram_tensor(f"kout{i}", [512, 4096], BF16) for i in range(2)]
    vout = [nc.dram_tensor(f"vout{i}", [512, 4096], BF16) for i in range(2)]
    kc_d = dscr("kc_d", [2, 128, 256], BF16)
    vc_d = dscr("vc_d", [256, 256], BF16)
    mrow_d = dscr("mrow_d", [2, 2, 9 * D], F32)

    P = Prog()
    root = ExitStack()

    uid = [0]

    def sbuf(es, name, shape, dt):
        uid[0] += 1
        return es.enter_context(nc.sbuf_tensor(f"{name}_{uid[0]}", list(shape), dt))

    def psum(es, name, shape, dt):
        uid[0] += 1
        return es.enter_context(nc.psum_tensor(f"{name}_{uid[0]}", list(shape), dt))

    ident_f = sbuf(root, "ident_f", [128, 128], F32)
    ident_b = sbuf(root, "ident_b", [128, 128], BF16)
    mcol = sbuf(root, "mcol", [128, 2, 72, 2], F32)
    BIG = sbuf(root, "BIG", [128, 77824], BF16)
    R1 = BIG[:, 0:49152]
    R2 = BIG[:, 49152:77824]

    def DMA(eng, key, out_, in_, reads=(), writes=(), nobar=False):
        return P.add(eng, lambda e: e.dma_start(out=out_, in_=in_), reads=reads, writes=writes, dma=key, nobar=nobar)

    DMA("sp", "c_ident", ident_f[:], identd, writes=["ident_f"])
    P.add("dve", lambda e: e.tensor_copy(out=ident_b[:], in_=ident_f[:]), reads=["ident_f"], writes=["ident_b"])

    def done(ph):
        P.barrier()
        return stop_after == ph

    def load_w1(layer, j, nobar=True):
        w1 = R1[:, 0:8 * 5632].rearrange("p (k f) -> p k f", k=8)
        src = ffn_w_in[layer, j].rearrange("(k p) f -> p k f", p=128)
        pieces = [(0, 2048), (2048, 4096), (4096, 5632)]

        def emit(e):
            r = []
            for kc in range(8):
                for lo, hi in pieces:
                    r.append(e.dma_start(out=w1[:, kc, lo:hi], in_=src[:, kc, lo:hi]))
            return r
        P.add("pool", emit, writes=["R1"], dma="wR1", ndma=24, nobar=nobar)
        return w1

    def load_w2(layer, j, nobar=True):
        w2 = R2[:, 0:22 * 1024].rearrange("p (f d) -> p f d", f=22)
        src = ffn_w_out[layer, j].rearrange("(f p) d -> p f d", p=128)

        def emit(e):
            return [e.dma_start(out=w2[:, f, :], in_=src[:, f, :]) for f in range(22)]
        P.add("pool", emit, writes=["R2"], dma="wR2", ndma=22, nobar=nobar)
        return w2

    def prep_chunk(xc_t, xc_res, trp, trp_res, xmT, xm_res, col0, layer, shift_mod, scale_mod, cnd):
        def pe(e):
            r = None
            for kc in range(8):
                r = e.transpose(out=trp[:, kc, :], in_=xc_t[:, kc * 128:(kc + 1) * 128], identity=ident_f[:])
            return r
        P.add("pe", pe, reads=[xc_res, "ident_f"], writes=[trp_res])
        for kc in range(8):
            P.add("act", lambda e, kc=kc: e.activation(
                out=xmT[:, kc, col0:col0 + 128], in_=trp[:, kc, :], func=AF.Identity,
                scale=mcol[:, layer, scale_mod * 8 + kc, cnd:cnd + 1],
                bias=mcol[:, layer, shift_mod * 8 + kc, cnd:cnd + 1]),
                reads=[trp_res, "mcol"], writes=[xm_res])

    class Tail:
        def __init__(self, es, layer, gate_mod, ln_idx, nslots=2):
            self.gate = [sbuf(es, f"gate_bc{c}", [128, D], F32) for c in range(2)]
            self.g_bc = sbuf(es, "g_bc", [128, D], F32)
            self.b_bc = sbuf(es, "b_bc", [128, D], F32)
            self.t1 = [sbuf(es, f"t1_{s}", [128, D], F32) for s in range(nslots)]
            self.st = [sbuf(es, f"st_{s}", [128, 2, 6], F32) for s in range(nslots)]
            self.mv = [sbuf(es, f"mv_{s}", [128, 2], F32) for s in range(nslots)]
            self.rs = [sbuf(es, f"rs_{s}", [128, 2], F32) for s in range(nslots)]
            self.n = nslots
            self.epsc = sbuf(es, "epsc", [128, 1], F32)
            P.add("pool", lambda e: e.memset(self.epsc[:], LN_EPS), writes=["epsc"])
            for c in range(2):
                DMA("sp", f"gatebc{c}", self.gate[c][:],
                    mrow_d[layer, c, gate_mod * D:(gate_mod + 1) * D].partition_broadcast(128),
                    reads=["mrow_d"], writes=[f"gate_bc{c}"])
            DMA("sp", "gbc", self.g_bc[:], ln_g[layer, ln_idx].partition_broadcast(128), writes=["g_bc"])
            DMA("sp", "bbc", self.b_bc[:], ln_b[layer, ln_idx].partition_broadcast(128), writes=["b_bc"])

        def piece(self, s, yap, yres, lo, hi, cnd):
            t1 = self.t1[s]
            P.add("dve", lambda e: e.tensor_tensor(out=t1[:, lo:hi], in0=yap, in1=self.gate[cnd][:, lo:hi], op=ALU.mult),
                  reads=[yres, f"gate_bc{cnd}"], writes=[f"t1_{s}"])

        def rest(self, s, xold, xres, dst, dst_res):
            t1, st, mv, rs = self.t1[s], self.st[s], self.mv[s], self.rs[s]
            r = f"t1_{s}"
            P.add("dve", lambda e: e.scalar_tensor_tensor(out=t1[:], in0=xold, scalar=ALPHA, in1=t1[:],
                                                          op0=ALU.mult, op1=ALU.add), reads=[xres, r], writes=[r])

            def stats(e):
                e.bn_stats(out=st[:, 0, :], in_=t1[:, 0:512])
                return e.bn_stats(out=st[:, 1, :], in_=t1[:, 512:1024])
            P.add("dve", stats, reads=[r], writes=[f"st_{s}"])
            P.add("dve", lambda e: e.bn_aggr(out=mv[:], in_=st[:].rearrange("p a b -> p (a b)")),
                  reads=[f"st_{s}"], writes=[f"mv_{s}"])
            P.add("act", lambda e: e.activation(out=rs[:, 0:1], in_=mv[:, 1:2], func=AF.Sqrt, bias=self.epsc[:, 0:1], scale=1.0),
                  reads=[f"mv_{s}", "epsc"], writes=[f"rsa_{s}"])
            P.add("dve", lambda e: e.reciprocal(out=rs[:, 0:1], in_=rs[:, 0:1]), reads=[f"rsa_{s}"], writes=[f"rsa_{s}"])
            P.add("dve", lambda e: e.tensor_scalar(out=rs[:, 1:2], in0=mv[:, 0:1], scalar1=rs[:, 0:1], scalar2=-1.0,
                                                   op0=ALU.mult, op1=ALU.mult), reads=[f"mv_{s}", f"rsa_{s}"], writes=[f"rsb_{s}"])
            P.add("act", lambda e: e.activation(out=t1[:], in_=t1[:], func=AF.Identity, scale=rs[:, 0:1], bias=rs[:, 1:2]),
                  reads=[r, f"rsa_{s}", f"rsb_{s}"], writes=[r])
            P.add("pool", lambda e: e.tensor_tensor(out=t1[:], in0=t1[:], in1=self.g_bc[:], op=ALU.mult),
                  reads=[r, "g_bc"], writes=[r])
            P.add("pool", lambda e: e.tensor_tensor(out=t1[:], in0=t1[:], in1=self.b_bc[:], op=ALU.add),
                  reads=[r, "b_bc"], writes=[r])
            DMA("sp", f"st_t1_{s}", dst, t1[:], reads=[r], writes=[dst_res])

    with ExitStack() as es:
        scf = sbuf(es, "scf", [128, 8, 2], F32)
        scb = sbuf(es, "scb", [128, 8, 2], BF16)
        mrow_ = R2[0:2, 0:18432].bitcast(F32)
        mrow = [mrow_, mrow_]
        wblk = [sbuf(es, f"wblk{s}", [128, 8, 512], BF16) for s in range(3)]
        brow = [sbuf(es, f"brow{s}", [2, 512], F32) for s in range(3)]
        mps = [psum(es, f"mps{s}", [128, 512], F32) for s in range(2)]
        tps = psum(es, "tps0", [128, 512], F32)
        DMA("sp", "scf", scf[:], cond, writes=["scf"])
        P.add("act", lambda e: e.activation(out=scb[:], in_=scf[:], func=AF.Silu), reads=["scf"], writes=["scb"])
        for layer in range(2):
            for blk in range(18):
                s3, s2 = blk % 3, blk % 2
                DMA("pool", f"wblk{s3}", wblk[s3][:],
                    w_mod[layer, :, blk * 512:(blk + 1) * 512].rearrange("(k p) c -> p k c", p=128),
                    writes=[f"wblk{s3}"])
                DMA("sp", f"brow{s3}", brow[s3][:], bmod2[:, layer, blk * 512:(blk + 1) * 512], writes=[f"brow{s3}"])

                def pe(e, s3=s3, s2=s2):
                    r = None
                    for kc in range(8):
                        r = e.matmul(mps[s2][0:2, :], lhsT=scb[:, kc, :], rhs=wblk[s3][:, kc, :], start=(kc == 0), stop=(kc == 7))
                    return r
                P.add("pe", pe, reads=["scb", f"wblk{s3}"], writes=[f"mps{s2}"])
                P.add("dve", lambda e, s3=s3, s2=s2, blk=blk, layer=layer: e.tensor_tensor(
                    out=mrow[layer][:, blk * 512:(blk + 1) * 512], in0=mps[s2][0:2, :], in1=brow[s3][:], op=ALU.add),
                    reads=[f"mps{s2}", f"brow{s3}"], writes=["R2"])
            for m in (1, 4, 7):
                P.add("dve", lambda e, m=m, layer=layer: e.tensor_scalar(
                    out=mrow[layer][:, m * D:(m + 1) * D], in0=mrow[layer][:, m * D:(m + 1) * D], scalar1=1.0, scalar2=None, op0=ALU.add),
                    reads=["R2"], writes=["R2"])
            for m in (2, 8):
                P.add("dve", lambda e, m=m, layer=layer: e.tensor_scalar(
                    out=mrow[layer][:, m * D:(m + 1) * D], in0=mrow[layer][:, m * D:(m + 1) * D], scalar1=0.5, scalar2=None, op0=ALU.mult),
                    reads=["R2"], writes=["R2"])
            DMA("sp", f"mrowst{layer}", mrow_d[layer], mrow[layer][:], reads=["R2"], writes=["mrow_d"])

            def pe_t(e, layer=layer):
                r = None
                for jj in range(72):
                    r = e.transpose(out=tps[:, 2 * jj:2 * jj + 2], in_=mrow[layer][0:2, jj * 128:(jj + 1) * 128], identity=ident_f[0:2, 0:2])
                return r
            P.add("pe", pe_t, reads=["R2", "ident_f"], writes=["tps0"])
            P.add("dve", lambda e, layer=layer: e.tensor_copy(
                out=mcol[:, layer, :, :].rearrange("p a b -> p (a b)"), in_=tps[:, 0:144]), reads=["tps0"], writes=["mcol"])
        load_w1(0, 0)
        load_w2(0, 0)
        fin = done("P0")

    def phase_FA(layer, j, src, with_ctx, prefetch=None):
        shift_mod, scale_mod = (0, 1) if j == 0 else (6, 7)
        tiles = [(4 * t, 4, 0) for t in range(8)] + ([(32, 2, 1)] if with_ctx else [])
        w1 = R1[:, 0:8 * 5632].rearrange("p (k f) -> p k f", k=8)
        with ExitStack() as es:
            NX = 6
            xc = [sbuf(es, f"xc{s}", [128, D], F32) for s in range(NX)]
            xmT = [sbuf(es, f"xmT{s}", [128, 8, 512], BF16) for s in range(2)]
            sg = [sbuf(es, f"sg{s}", [128, 512], F32) for s in range(2)]
            hto = [sbuf(es, f"hto{s}", [128, 512], BF16) for s in range(4)]
            trp = [psum(es, f"trp{s}", [128, 8, 128], F32) for s in range(2)]
            gu = [psum(es, f"gu{s}", [128, 2, 512], F32) for s in range(2)]
            if prefetch:
                prefetch()
            chunks = [(c0 + ci, cnd) for (c0, n, cnd) in tiles for ci in range(n)]
            nload = [0]

            def load_next():
                if nload[0] < len(chunks):
                    c, _ = chunks[nload[0]]
                    s = nload[0] % NX
                    DMA("sp", f"xc{s}", xc[s][:], src[c * 128:(c + 1) * 128, :], reads=[("x", id(src), c)], writes=[f"xc{s}"])
                    nload[0] += 1
            for _ in range(NX):
                load_next()
            cidx = [0]

            def prep(ti):
                c0, n, cnd = tiles[ti]
                for ci in range(n):
                    k = cidx[0]
                    s = k % NX
                    prep_chunk(xc[s], f"xc{s}", trp[k % 2], f"trp{k % 2}", xmT[ti % 2], f"xmT{ti % 2}", ci * 128,
                               layer, shift_mod, scale_mod, cnd)
                    cidx[0] += 1
                    load_next()
            prep(0)
            for ti, (c0, n, cnd) in enumerate(tiles):
                NT = n * 128
                xm = xmT[ti % 2]
                for f in range(22):
                    b = f % 2

                    def pe(e, f=f, b=b, xm=xm, NT=NT):
                        r = None
                        for half in range(2):
                            col = half * DFF + f * 128
                            for kc in range(8):
                                r = e.matmul(gu[b][:, half, 0:NT], lhsT=w1[:, kc, col:col + 128], rhs=xm[:, kc, 0:NT],
                                             start=(kc == 0), stop=(kc == 7))
                        return r
                    P.add("pe", pe, reads=["R1", f"xmT{ti % 2}"], writes=[f"gu{b}"])
                    P.add("act", lambda e, b=b, NT=NT: e.activation(out=sg[b][:, 0:NT], in_=gu[b][:, 0, 0:NT], func=AF.Silu),
                          reads=[f"gu{b}"], writes=[f"sg{b}"])
                    hs = f % 4
                    P.add("dve", lambda e, b=b, hs=hs, NT=NT: e.tensor_tensor(out=hto[hs][:, 0:NT], in0=sg[b][:, 0:NT],
                                                                              in1=gu[b][:, 1, 0:NT], op=ALU.mult),
                          reads=[f"sg{b}", f"gu{b}"], writes=[f"hto{hs}"])
                    DMA("sp", f"hto{hs}", hT_d[c0:c0 + n, :, f, :].rearrange("c p t -> p c t"),
                        hto[hs][:, 0:NT].rearrange("p (c t) -> p c t", t=128),
                        reads=[f"hto{hs}"], writes=[("hT", c0, f)])
                    if f == 8 and ti + 1 < len(tiles):
                        prep(ti + 1)
            return done(f"FA{layer}{j}")

    def phase_B(name, nf, w2, layer, gate_mod, ln_idx, xsrc, dst, chunk_list, dst_rows, prefetch=None):
        with ExitStack() as es:
            tl = Tail(es, layer, gate_mod, ln_idx)
            xc = [sbuf(es, f"xc{s}", [128, D], F32) for s in range(2)]
            hin = [sbuf(es, f"hin{s}", [128, nf, 128], BF16) for s in range(2)]
            yps = [psum(es, f"yps{s}", [128, D], F32) for s in range(2)]
            if prefetch:
                prefetch()
            nl = [0]

            def load_next():
                if nl[0] < len(chunk_list):
                    c = chunk_list[nl[0]]
                    s = nl[0] % 2
                    DMA("sp", f"xc{s}", xc[s][:], xsrc[c * 128:(c + 1) * 128, :], reads=[("x", id(xsrc), c)], writes=[f"xc{s}"])
                    DMA("sp", f"hin{s}", hin[s][:], hT_d[c, :, 0:nf, :], reads=[("hT", c)], writes=[f"hin{s}"])
                    nl[0] += 1
            for _ in range(2):
                load_next()
            for k, c in enumerate(chunk_list):
                s, b = k % 2, k % 2
                cnd = 0 if c < NLAT else 1
                for half in range(2):
                    def pe(e, s=s, b=b, half=half):
                        r = None
                        for f in range(nf):
                            r = e.matmul(yps[b][:, half * 512:(half + 1) * 512], lhsT=hin[s][:, f, :],
                                         rhs=w2[:, f, half * 512:(half + 1) * 512], start=(f == 0), stop=(f == nf - 1))
                        return r
                    P.add("pe", pe, reads=["R2", f"hin{s}"], writes=[(f"yps{b}", half)])
                    tl.piece(b, yps[b][:, half * 512:(half + 1) * 512], (f"yps{b}", half), half * 512, (half + 1) * 512, cnd)
                r0 = dst_rows(c)
                tl.rest(b, xc[s][:], f"xc{s}", dst[r0:r0 + 128, :], ("x", id(dst), c))
                load_next()
            return done(name)


    def gen_loader(name_prefix, nslots, slots, chunk_ids, src, res_prefix):
        st_ = [0]

        def load_next():
            if st_[0] < len(chunk_ids):
                c = chunk_ids[st_[0]]
                s_ = st_[0] % nslots
                DMA("sp", f"{res_prefix}{s_}", slots[s_][:], src[c * 128:(c + 1) * 128, :],
                    reads=[("x", id(src), c)], writes=[f"{res_prefix}{s_}"])
                st_[0] += 1
        return load_next

    def phase_G1(src, prefetch=None):
        layer, shift_mod, scale_mod = 0, 3, 4
        tiles = [(2 * t, 2, 0) for t in range(16)] + [(32, 2, 1)]
        w_in = R1.rearrange("p (k f) -> p k f", k=8)
        with ExitStack() as es:
            NX = 4
            xc = [sbuf(es, "xc", [128, D], F32) for s_ in range(NX)]
            xmT = [sbuf(es, "xmT", [128, 8, 256], BF16) for s_ in range(2)]
            vh = sbuf(es, "vh", [128, 3072], BF16)
            hho = [sbuf(es, "hho", [128, 24, 128], BF16) for s_ in range(2)]
            wsT = sbuf(es, "wsT", [128, 8, 128], BF16)
            bs_bc = sbuf(es, "bs_bc", [128, 8, 128], F32)
            tmp = [sbuf(es, "tmp", [128, 128], F32) for s_ in range(2)]
            gcol = sbuf(es, "gcol", [128, 24], F32)
            bcol = sbuf(es, "bcol", [128, 24], F32)
            ones_b = sbuf(es, "ones_b", [128, 128], BF16)
            st6 = sbuf(es, "st6", [128, 6, 6], F32)
            mv = sbuf(es, "mvg", [128, 2], F32)
            rs = sbuf(es, "rsg", [128, 2], F32)
            epsc = sbuf(es, "epsg", [128, 1], F32)
            uT = [R2[:, i * 6144:(i + 1) * 6144].rearrange("p (f t) -> p f t", f=24) for i in range(2)]
            v_sb = R2[:, 12288:18432].bitcast(F32)
            Bt = R2[:, 18432:24576].bitcast(F32).rearrange("p (f t) -> p f t", f=24)
            trp = psum(es, "trp", [128, 8, 128], F32)
            ups_ = [psum(es, "ups", [128, 512], F32) for s_ in range(2)]
            vps = [psum(es, "vps", [128, 512], F32) for s_ in range(2)]
            sps = [psum(es, "sps", [128, 4, 128], F32) for s_ in range(2)]
            if prefetch:
                prefetch()
            DMA("pool", "wsT", wsT[:], wsT_d, writes=["wsT"])
            DMA("sp", "gcol", gcol[:], glng, writes=["gcol"])
            DMA("sp", "bcol", bcol[:], glnb, writes=["bcol"])
            DMA("sp", "bsbc", bs_bc[:].rearrange("p g t -> p (g t)"), gbs.partition_broadcast(128), writes=["bs_bc"])
            P.add("pool", lambda e: e.memset(ones_b[:], 1.0), writes=["ones_b"])
            P.add("pool", lambda e: e.memset(epsc[:], LN_EPS), writes=["epsg"])
            for g in range(8):
                P.add("pe", lambda e, g=g: e.matmul(sps[g // 4][:, g % 4, :], lhsT=ones_b[:], rhs=wsT[:, g, :], start=True, stop=True),
                      reads=["ones_b", "wsT"], writes=[f"sps{g // 4}"])
            for cc in range(24):
                g = cc // 3
                P.add("dve", lambda e, cc=cc, g=g: e.scalar_tensor_tensor(
                    out=Bt[:, cc, :], in0=sps[g // 4][:, g % 4, :], scalar=bcol[:, cc:cc + 1], in1=bs_bc[:, g, :],
                    op0=ALU.mult, op1=ALU.add), reads=[f"sps{g // 4}", "bcol", "bs_bc"], writes=["Bt"])
            chunks = [c0 + ci for (c0, n, cnd) in tiles for ci in range(n)]
            load_next = gen_loader("xc", NX, xc, chunks, src, "xc")
            for _ in range(NX):
                load_next()
            kidx = [0]

            def prep(ti):
                c0, n, cnd = tiles[ti]
                for ci in range(n):
                    k = kidx[0]
                    prep_chunk(xc[k % NX], f"xc{k % NX}", trp, "trp", xmT[ti % 2], f"xmT{ti % 2}", ci * 128,
                               layer, shift_mod, scale_mod, cnd)
                    kidx[0] += 1
                    load_next()
            prep(0)
            for ti, (c0, n, cnd) in enumerate(tiles):
                xm = xmT[ti % 2]
                u_ = uT[ti % 2]
                for cc in range(24 if G1MODE >= 2 else 0):
                    def pe(e, cc=cc, xm=xm):
                        r = None
                        for kc in range(8):
                            r = e.matmul(ups_[cc % 2][:, 0:256], lhsT=w_in[:, kc, cc * 128:(cc + 1) * 128], rhs=xm[:, kc, :],
                                         start=(kc == 0), stop=(kc == 7))
                        return r
                    P.add("pe", pe, reads=["R1", f"xmT{ti % 2}"], writes=[f"ups{cc % 2}"])
                    P.add("act", lambda e, cc=cc, u_=u_: e.activation(out=u_[:, cc, :], in_=ups_[cc % 2][:, 0:256], func=(AF.Gelu if G1GELU else AF.Identity)),
                          reads=[f"ups{cc % 2}"], writes=[f"uT{ti % 2}"])
                for ci in range(n):
                    c = c0 + ci
                    if G1MODE < 3:
                        if ci == 0 and ti + 1 < len(tiles):
                            prep(ti + 1)
                        continue
                    for blk in range(6):
                        def pe(e, blk=blk, xm=xm, ci=ci):
                            r = None
                            for kc in range(8):
                                r = e.matmul(vps[blk % 2][:, :], lhsT=xm[:, kc, ci * 128:(ci + 1) * 128],
                                             rhs=w_in[:, kc, 3072 + blk * 512:3072 + (blk + 1) * 512], start=(kc == 0), stop=(kc == 7))
                            return r
                        P.add("pe", pe, reads=["R1", f"xmT{ti % 2}"], writes=[f"vps{blk % 2}"])
                        P.add("act", lambda e, blk=blk: e.activation(out=v_sb[:, blk * 512:(blk + 1) * 512], in_=vps[blk % 2][:, :], func=AF.Gelu),
                              reads=[f"vps{blk % 2}"], writes=["v_sb"])

                    def stats(e):
                        r = None
                        for q in range(6):
                            r = e.bn_stats(out=st6[:, q, :], in_=v_sb[:, q * 512:(q + 1) * 512])
                        return r
                    P.add("dve", stats, reads=["v_sb"], writes=["st6"])
                    P.add("dve", lambda e: e.bn_aggr(out=mv[:], in_=st6[:].rearrange("p a b -> p (a b)")), reads=["st6"], writes=["mvg"])
                    P.add("act", lambda e: e.activation(out=rs[:, 0:1], in_=mv[:, 1:2], func=AF.Sqrt, bias=epsc[:, 0:1], scale=1.0),
                          reads=["mvg", "epsg"], writes=["rsga"])
                    P.add("dve", lambda e: e.reciprocal(out=rs[:, 0:1], in_=rs[:, 0:1]), reads=["rsga"], writes=["rsga"])
                    P.add("dve", lambda e: e.tensor_scalar(out=rs[:, 1:2], in0=mv[:, 0:1], scalar1=rs[:, 0:1], scalar2=-1.0,
                                                           op0=ALU.mult, op1=ALU.mult), reads=["mvg", "rsga"], writes=["rsgb"])
                    P.add("act", lambda e: e.activation(out=vh[:], in_=v_sb[:], func=AF.Identity, scale=rs[:, 0:1], bias=rs[:, 1:2]),
                          reads=["v_sb", "rsga", "rsgb"], writes=["vh"])
                    for cg in range(6 if G1MODE >= 4 else 0):
                        def pe(e, cg=cg):
                            r = None
                            for i in range(4):
                                cc = cg * 4 + i
                                r = e.matmul(sps[cg % 2][:, i, :], lhsT=vh[:, cc * 128:(cc + 1) * 128], rhs=wsT[:, cc // 3, :],
                                             start=True, stop=True)
                            return r
                        P.add("pe", pe, reads=["vh", "wsT"], writes=[f"sps{cg % 2}"])
                        for i in range(4):
                            cc = cg * 4 + i
                            P.add("dve", lambda e, cc=cc, cg=cg, i=i: e.scalar_tensor_tensor(
                                out=tmp[cc % 2][:], in0=sps[cg % 2][:, i, :], scalar=gcol[:, cc:cc + 1], in1=Bt[:, cc, :],
                                op0=ALU.mult, op1=ALU.add), reads=[f"sps{cg % 2}", "gcol", "Bt"], writes=[f"tmp{cc % 2}"])
                            P.add("pool", lambda e, cc=cc, c=c, ci=ci, u_=u_: e.tensor_tensor(
                                out=hho[c % 2][:, cc, :], in0=tmp[cc % 2][:], in1=u_[:, cc, ci * 128:(ci + 1) * 128], op=ALU.mult),
                                reads=[f"tmp{cc % 2}", f"uT{ti % 2}"], writes=[f"hho{c % 2}"])
                    if G1MODE >= 4:
                        DMA("sp", f"hho{c % 2}", hT_d[c], hho[c % 2][:], reads=[f"hho{c % 2}"], writes=[("hT", c)])
                    if ci == 0 and ti + 1 < len(tiles):
                        prep(ti + 1)
            return done("G1")

    def phase_QKV(src, prefetch=None):
        layer, shift_mod, scale_mod = 1, 3, 4
        wq = R1[:, 0:8 * 1536].rearrange("p (k f) -> p k f", k=8)
        with ExitStack() as es:
            NX = 3
            xc = [sbuf(es, "xc", [128, D], F32) for s_ in range(NX)]
            xmT = [sbuf(es, "xmT", [128, 8, 128], BF16) for s_ in range(2)]
            sq = [sbuf(es, "sq", [128, 512], F32) for s_ in range(3)]
            ss = sbuf(es, "ss", [128, 12], F32)
            rst = sbuf(es, "rst", [128, 12], F32)
            qn = sbuf(es, "qnb", [128, 10, 128], F32)
            ra = sbuf(es, "ra", [128, 10, 64], F32)
            rb = sbuf(es, "rb", [128, 10, 64], F32)
            rc = sbuf(es, "rc", [128, 10, 64], F32)
            rd = sbuf(es, "rd", [128, 10, 64], F32)
            qr = sbuf(es, "qr", [128, 10, 128], BF16)
            cosT = R2[:, 12288:16384].bitcast(F32).rearrange("p (c j) -> p c j", c=NLAT)
            sinT = R2[:, 16384:20480].bitcast(F32).rearrange("p (c j) -> p c j", c=NLAT)
            ang = R2[:, 0:4096].bitcast(F32).rearrange("p (c j) -> p c j", c=NLAT)
            angi = R2[:, 4096:8192].bitcast(I32).rearrange("p (c j) -> p c j", c=NLAT)
            ang2 = R2[:, 8192:12288].bitcast(F32).rearrange("p (c j) -> p c j", c=NLAT)
            prc = sbuf(es, "prc", [128, NLAT, 2], F32)
            frq = sbuf(es, "frq", [128, 64], F32)
            gq_bc = sbuf(es, "gq_bc", [128, 128], F32)
            gk_bc = sbuf(es, "gk_bc", [128, 128], F32)
            qTs = [sbuf(es, "qTs", [128, 8, 128], BF16) for s_ in range(2)]
            kTs = [sbuf(es, "kTs", [128, 2, 128], BF16) for s_ in range(2)]
            vs = [sbuf(es, "vs", [128, 256], BF16) for s_ in range(2)]
            trp = psum(es, "trp", [128, 8, 128], F32)
            qkv = [psum(es, "qkv", [128, 512], F32) for s_ in range(3)]
            tq = psum(es, "tq", [128, 8, 128], BF16)
            tk = psum(es, "tk", [128, 8, 128], BF16)
            if prefetch:
                prefetch()

            def emit_w(e):
                srcw = w_qkv.rearrange("(k p) f -> p k f", p=128)
                return [e.dma_start(out=wq[:, kc, :], in_=srcw[:, kc, :]) for kc in range(8)]
            P.add("pool", emit_w, writes=["R1"], dma="wR1", ndma=8)
            DMA("sp", "prc", prc[:], posrc, writes=["prc"])
            DMA("sp", "frq", frq[:], freq2, writes=["frq"])
            DMA("sp", "gqbc", gq_bc[:], qn_d.partition_broadcast(128), writes=["gq_bc"])
            DMA("sp", "gkbc", gk_bc[:], kn_d.partition_broadcast(128), writes=["gk_bc"])
            for hh_ in range(2):
                P.add("dve", lambda e, hh_=hh_: e.tensor_tensor(
                    out=ang[:, :, hh_ * 32:(hh_ + 1) * 32], in0=prc[:, :, hh_:hh_ + 1].broadcast_to([128, NLAT, 32]),
                    in1=frq[:, hh_ * 32:(hh_ + 1) * 32].unsqueeze(1).broadcast_to([128, NLAT, 32]), op=ALU.mult),
                    reads=["prc", "frq"], writes=["ang"])
            P.add("dve", lambda e: e.tensor_scalar(out=ang[:], in0=ang[:], scalar1=1.0 / (2.0 * math.pi), scalar2=None, op0=ALU.mult),
                  reads=["ang"], writes=["ang"])
            P.add("dve", lambda e: e.tensor_copy(out=angi[:], in_=ang[:]), reads=["ang"], writes=["angi"])
            P.add("dve", lambda e: e.tensor_copy(out=ang2[:], in_=angi[:]), reads=["angi"], writes=["ang2"])
            P.add("dve", lambda e: e.tensor_tensor(out=ang[:], in0=ang[:], in1=ang2[:], op=ALU.subtract), reads=["ang", "ang2"], writes=["ang"])
            P.add("dve", lambda e: e.tensor_scalar(out=ang2[:], in0=ang[:], scalar1=-1.0, scalar2=None, op0=ALU.mult), reads=["ang"], writes=["ang2"])
            P.add("dve", lambda e: e.tensor_tensor(out=ang2[:], in0=ang2[:], in1=ang[:], op=ALU.max), reads=["ang", "ang2"], writes=["ang2"])
            P.add("act", lambda e: e.activation(out=sinT[:], in_=ang[:], func=AF.Sin, scale=math.pi), reads=["ang"], writes=["sinT"])
            P.add("dve", lambda e: e.tensor_scalar(out=ang2[:], in0=ang2[:], scalar1=-math.pi, scalar2=math.pi / 2, op0=ALU.mult, op1=ALU.add),
                  reads=["ang2"], writes=["ang2"])
            P.add("act", lambda e: e.activation(out=cosT[:], in_=ang2[:], func=AF.Sin), reads=["ang2"], writes=["cosT"])
            P.add("dve", lambda e: e.tensor_tensor(out=ang[:], in0=sinT[:], in1=sinT[:], op=ALU.mult), reads=["sinT"], writes=["ang"])
            P.add("dve", lambda e: e.scalar_tensor_tensor(out=sinT[:], in0=sinT[:], scalar=2.0, in1=cosT[:], op0=ALU.mult, op1=ALU.mult),
                  reads=["sinT", "cosT", "ang"], writes=["sinT"])
            P.add("dve", lambda e: e.tensor_scalar(out=cosT[:], in0=ang[:], scalar1=-2.0, scalar2=1.0, op0=ALU.mult, op1=ALU.add),
                  reads=["ang", "sinT"], writes=["cosT"])
            chunks = list(range(NCH))
            load_next = gen_loader("xc", NX, xc, chunks, src, "xc")
            for _ in range(NX):
                load_next()
            for k, c in enumerate(chunks):
                lat = c < NLAT
                cnd = 0 if lat else 1
                b = k % 2
                prep_chunk(xc[k % NX], f"xc{k % NX}", trp, "trp", xmT[b], f"xmT{b}", 0, layer, shift_mod, scale_mod, cnd)
                load_next()
                blks = (0, 1, 2) if lat else (2,)
                for blk in blks:
                    def pe(e, blk=blk, b=b):
                        r = None
                        for kc in range(8):
                            r = e.matmul(qkv[blk][:, :], lhsT=xmT[b][:, kc, :], rhs=wq[:, kc, blk * 512:(blk + 1) * 512],
                                         start=(kc == 0), stop=(kc == 7))
                        return r
                    P.add("pe", pe, reads=["R1", f"xmT{b}"], writes=[f"qkv{blk}"])
                    P.add("act", lambda e, blk=blk: e.activation(out=sq[blk][:], in_=qkv[blk][:, :], func=AF.Square),
                          reads=[f"qkv{blk}"], writes=[f"sq{blk}"])
                    P.add("dve", lambda e, blk=blk: e.tensor_reduce(out=ss[:, blk * 4:(blk + 1) * 4],
                                                                    in_=sq[blk][:].rearrange("p (h d) -> p h d", h=4), axis=AX.X, op=ALU.add),
                          reads=[f"sq{blk}"], writes=["ss"])
                P.add("dve", lambda e: e.tensor_scalar(out=rst[:], in0=ss[:], scalar1=1.0 / 128.0, scalar2=RMS_EPS, op0=ALU.mult, op1=ALU.add),
                      reads=["ss"], writes=["rst"])
                P.add("act", lambda e: e.activation(out=rst[:], in_=rst[:], func=AF.Sqrt), reads=["rst"], writes=["rst"])
                P.add("dve", lambda e: e.reciprocal(out=rst[:], in_=rst[:]), reads=["rst"], writes=["rst"])
                heads = list(range(10)) if lat else [8, 9]
                for h in heads:
                    blk, off = (h // 4, (h % 4) * 128) if h < 8 else (2, (h - 8) * 128)
                    gb = gq_bc if h < 8 else gk_bc
                    P.add("dve", lambda e, h=h, blk=blk, off=off, gb=gb: e.scalar_tensor_tensor(
                        out=qn[:, h, :], in0=qkv[blk][:, off:off + 128], scalar=rst[:, (blk * 4 + (off // 128)):(blk * 4 + (off // 128)) + 1],
                        in1=gb[:], op0=ALU.mult, op1=ALU.mult), reads=[f"qkv{blk}", "rst", "gq_bc", "gk_bc"], writes=["qn"])
                if lat:
                    x1, x2 = qn[:, :, 0:64], qn[:, :, 64:128]
                    cb = cosT[:, c, :].unsqueeze(1).broadcast_to([128, 10, 64])
                    sb_ = sinT[:, c, :].unsqueeze(1).broadcast_to([128, 10, 64])
                    P.add("pool", lambda e, x1=x1, cb=cb: e.tensor_tensor(out=ra[:], in0=x1, in1=cb, op=ALU.mult), reads=["qn", "cosT"], writes=["ra"])
                    P.add("pool", lambda e, x2=x2, sb_=sb_: e.tensor_tensor(out=rb[:], in0=x2, in1=sb_, op=ALU.mult), reads=["qn", "sinT"], writes=["rb"])
                    P.add("dve", lambda e: e.tensor_tensor(out=qr[:, :, 0:64], in0=ra[:], in1=rb[:], op=ALU.subtract), reads=["ra", "rb"], writes=["qr"])
                    P.add("pool", lambda e, x2=x2, cb=cb: e.tensor_tensor(out=rc[:], in0=x2, in1=cb, op=ALU.mult), reads=["qn", "cosT"], writes=["rc"])
                    P.add("pool", lambda e, x1=x1, sb_=sb_: e.tensor_tensor(out=rd[:], in0=x1, in1=sb_, op=ALU.mult), reads=["qn", "sinT"], writes=["rd"])
                    P.add("dve", lambda e: e.tensor_tensor(out=qr[:, :, 64:128], in0=rc[:], in1=rd[:], op=ALU.add), reads=["rc", "rd"], writes=["qr"])
                else:
                    P.add("dve", lambda e: e.tensor_copy(out=qr[:, 8:10, :], in_=qn[:, 8:10, :]), reads=["qn"], writes=["qr"])

                def pe_t(e, lat=lat):
                    r = None
                    if lat:
                        for h in range(8):
                            r = e.transpose(out=tq[:, h, :], in_=qr[:, h, :], identity=ident_b[:])
                    for kv in range(2):
                        r = e.transpose(out=tk[:, kv, :], in_=qr[:, 8 + kv, :], identity=ident_b[:])
                    return r
                P.add("pe", pe_t, reads=["qr", "ident_b"], writes=["tq", "tk"])
                if lat:
                    P.add("act", lambda e, b=b: e.activation(out=qTs[b][:], in_=tq[:], func=AF.Copy), reads=["tq"], writes=[f"qTs{b}"])
                    DMA("sp", f"qTs{b}", qT_d[c].rearrange("p (h t) -> p h t", h=8), qTs[b][:], reads=[f"qTs{b}"], writes=[("qT", c)])
                P.add("dve", lambda e, b=b: e.tensor_copy(out=kTs[b][:], in_=tk[:, 0:2, :]), reads=["tk"], writes=[f"kTs{b}"])
                P.add("act", lambda e, b=b: e.activation(out=vs[b][:], in_=qkv[2][:, 256:512], func=AF.Copy), reads=["qkv2"], writes=[f"vs{b}"])
                if lat:
                    def st_k(e, b=b, c=c):
                        return [e.dma_start(out=kin[kv].ap()[:, c * 128:(c + 1) * 128], in_=kTs[b][:, kv, :]) for kv in range(2)]
                    P.add("sp", st_k, reads=[f"kTs{b}"], writes=["kin"], dma=f"kTs{b}", ndma=2)
                    hv, j16 = c // 16, c % 16
                    DMA("sp", f"vs{b}", vin[hv].ap()[j16 * 8:(j16 + 1) * 8, :].rearrange("a (b c) -> (a b) c", c=256), vs[b][:],
                        reads=[f"vs{b}"], writes=["vin"])
                else:
                    cc_ = c - NLAT

                    def st_k(e, b=b, cc_=cc_):
                        return [e.dma_start(out=kc_d[kv, :, cc_ * 128:(cc_ + 1) * 128], in_=kTs[b][:, kv, :]) for kv in range(2)]
                    P.add("sp", st_k, reads=[f"kTs{b}"], writes=["kc_d"], dma=f"kTs{b}", ndma=2)
                    DMA("sp", f"vs{b}", vc_d[cc_ * 128:(cc_ + 1) * 128, :], vs[b][:], reads=[f"vs{b}"], writes=["vc_d"])
            P.barrier()
            for i in range(2):
                P.add("pool", lambda e, i=i: e.collective_compute("AllGather", ALU.bypass, replica_groups=GROUPS,
                                                                   ins=[kin[i].ap().opt()], outs=[kout[i].ap().opt()]),
                      reads=["kin"], writes=[("kout", i)], dma=f"cck{i}", inc=1)
                P.add("pool", lambda e, i=i: e.collective_compute("AllGather", ALU.bypass, replica_groups=GROUPS,
                                                                   ins=[vin[i].ap().opt()], outs=[vout[i].ap().opt()]),
                      reads=["vin"], writes=[("vout", i)], dma=f"ccv{i}", inc=1)
            return done("QKV")

    def phase_ATT(xsrc, dst):
        layer = 1
        KT = BIG[:, 0:33280].rearrange("p (k t) -> p k t", k=2)
        V = BIG[:, 33280:66560].rearrange("p (c k d) -> p c k d", c=NKC, k=2)
        wo = BIG[:, 66560:74752].rearrange("p (h d) -> p h d", h=8)
        NP = NKC // 2
        with ExitStack() as es:
            tl = Tail(es, layer, 5, 1, nslots=1)
            xc = sbuf(es, "xc", [128, D], F32)
            qTt = [sbuf(es, "qTt", [128, 8, 128], BF16) for s_ in range(2)]
            pt = [sbuf(es, "pt", [128, 1024], BF16) for s_ in range(3)]
            acc = {"dve": sbuf(es, "accD", [128, 1024], F32), "pool": sbuf(es, "accP", [128, 1024], F32)}
            accb = sbuf(es, "accb", [128, 2048], BF16)
            ones_b = sbuf(es, "ones_b", [128, 128], BF16)
            rinv = sbuf(es, "rinv", [128, 512], F32)
            oT = sbuf(es, "oT", [128, 8, 128], BF16)
            spsm = [psum(es, "spsm", [128, 1024], F32) for s_ in range(2)]
            OT = [psum(es, "OT", [128, 512], F32) for s_ in range(2)]
            sump = psum(es, "sump", [128, 512], F32)
            yps = psum(es, "ypsa", [128, 512], F32)
            P.add("pool", lambda e: e.memset(ones_b[:], 1.0), writes=["ones_b"])
            vres = [("Vld", hv, rk) for hv in range(2) for rk in range(4)] + ["Vc"]
            for kv in range(2):
                def ld_k(e, kv=kv):
                    r = [e.dma_start(out=KT[:, kv, rk * 4096:(rk + 1) * 4096], in_=kout[kv].ap()[rk * 128:(rk + 1) * 128, :]) for rk in range(4)]
                    r.append(e.dma_start(out=KT[:, kv, 16384:16640], in_=kc_d[kv]))
                    return r
                P.add("sp", ld_k, reads=[("kout", kv), "kc_d", "R1"], writes=[("Kld", kv)], dma=f"ldk{kv}", ndma=5)
            for hv in range(2):
                for rk in range(4):
                    def ld_v(e, hv=hv, rk=rk):
                        srcv = vout[hv].ap()[rk * 128:(rk + 1) * 128, :].rearrange("(j a) (b k d) -> (a b) j k d", a=8, b=16, k=2)
                        c0 = rk * 32 + hv * 16
                        r = []
                        for kv in range(2):
                            for jh in range(2):
                                r.append(e.dma_start(out=V[:, c0 + jh * 8:c0 + (jh + 1) * 8, kv, :], in_=srcv[:, jh * 8:(jh + 1) * 8, kv, :]))
                        return r
                    P.add("sp", ld_v, reads=[("vout", hv), "R1", "R2"], writes=[("Vld", hv, rk)], dma=f"ldv{hv}{rk}", ndma=4)

            def ld_vc(e):
                srcv = vc_d.rearrange("(j p) (k d) -> p j k d", p=128, k=2)
                return [e.dma_start(out=V[:, 128:130, kv, :], in_=srcv[:, :, kv, :]) for kv in range(2)]
            P.add("sp", ld_vc, reads=["vc_d", "R2"], writes=["Vc"], dma="ldvc", ndma=2)

            def emit_wo(e):
                srcw = w_o_d.rearrange("(h p) d -> p h d", p=128)
                return [e.dma_start(out=wo[:, h, :], in_=srcw[:, h, :]) for h in range(8)]
            P.add("pool", emit_wo, reads=["R2"], writes=["wo"], dma="wo", ndma=8)
            P.add("pe", lambda e: None, reads=vres + [("Kld", 0), ("Kld", 1)], writes=["KVready"])
            qts = list(range(NLAT))
            nl = [0]

            def load_q():
                if nl[0] < len(qts):
                    c = qts[nl[0]]
                    s_ = nl[0] % 2
                    DMA("sp", f"qTt{s_}", qTt[s_][:].rearrange("p h t -> p (h t)"), qT_d[c], reads=[("qT", c)], writes=[f"qTt{s_}"])
                    nl[0] += 1
            load_q()
            load_q()
            first = {"dve": True, "pool": True, "pe": True}
            SUMENG = ["dve", "pool", "dve", "pe", "pool", "dve", "pool", "pe"]
            for qi, c in enumerate(qts):
                s_ = qi % 2
                DMA("sp", "xc0", xc[:], xsrc[c * 128:(c + 1) * 128, :], reads=[("x", id(xsrc), c)], writes=["xc0"])
                for kv in range(2):
                    g = qi * 2 + kv
                    ot = OT[g % 2]
                    otres = f"OT{g % 2}"
                    rhs = qTt[s_][:, kv * 4:(kv + 1) * 4, :].rearrange("p h t -> p (h t)")
                    first["dve"] = True
                    first["pool"] = True
                    first["pe"] = True

                    def S(p, kv=kv, rhs=rhs):
                        sp_ = spsm[p % 2]

                        def pe(e):
                            e.matmul(sp_[:, 0:512], lhsT=KT[:, kv, (2 * p) * 128:(2 * p + 1) * 128], rhs=rhs, start=True, stop=True)
                            return e.matmul(sp_[:, 512:1024], lhsT=KT[:, kv, (2 * p + 1) * 128:(2 * p + 2) * 128], rhs=rhs, start=True, stop=True)
                        P.add("pe", pe, reads=["KVready", f"qTt{s_}"], writes=[f"spsm{p % 2}"])
                        P.add("act", lambda e: e.activation(out=pt[p % 3][:], in_=sp_[:, :], func=AF.Exp, scale=SCALE),
                              reads=[f"spsm{p % 2}"], writes=[f"pt{p % 3}"])

                    def PV(p, kv=kv, ot=ot, otres=otres):
                        def pe(e):
                            e.matmul(ot[:, :], lhsT=V[:, 2 * p, kv, :], rhs=pt[p % 3][:, 0:512], start=(p == 0), stop=False)
                            return e.matmul(ot[:, :], lhsT=V[:, 2 * p + 1, kv, :], rhs=pt[p % 3][:, 512:1024], start=False, stop=(p == NP - 1))
                        P.add("pe", pe, reads=[f"pt{p % 3}", "KVready"], writes=[otres])
                        en = SUMENG[p % 8]
                        if en == "pe":
                            st0 = first["pe"]
                            first["pe"] = False

                            def pes(e):
                                e.matmul(sump[:, :], lhsT=ones_b[:], rhs=pt[p % 3][:, 0:512], start=st0, stop=False)
                                return e.matmul(sump[:, :], lhsT=ones_b[:], rhs=pt[p % 3][:, 512:1024], start=False, stop=False)
                            P.add("pe", pes, reads=[f"pt{p % 3}", "ones_b"], writes=["sump"])
                            return
                        a_ = acc[en]
                        if first[en]:
                            first[en] = False
                            P.add(en, lambda e: e.tensor_copy(out=a_[:], in_=pt[p % 3][:]), reads=[f"pt{p % 3}"], writes=[f"acc_{en}"])
                        else:
                            P.add(en, lambda e: e.tensor_tensor(out=a_[:], in0=a_[:], in1=pt[p % 3][:], op=ALU.add),
                                  reads=[f"pt{p % 3}", f"acc_{en}"], writes=[f"acc_{en}"])
                    S(0)
                    for p in range(NP):
                        if p + 1 < NP:
                            S(p + 1)
                        PV(p)
                    P.add("dve", lambda e: e.tensor_copy(out=accb[:, 0:1024], in_=acc["dve"][:]), reads=["acc_dve"], writes=["accbD"])
                    P.add("pool", lambda e: e.tensor_copy(out=accb[:, 1024:2048], in_=acc["pool"][:]), reads=["acc_pool"], writes=["accbP"])

                    def pe_sum(e):
                        r = None
                        for i in range(4):
                            r = e.matmul(sump[:, :], lhsT=ones_b[:], rhs=accb[:, i * 512:(i + 1) * 512], start=False, stop=(i == 3))
                        return r
                    P.add("pe", pe_sum, reads=["ones_b", "accbD", "accbP"], writes=["sump"])
                    P.add("dve", lambda e: e.reciprocal(out=rinv[:], in_=sump[:, :]), reads=["sump"], writes=["rinv"])
                    P.add("dve", lambda e, kv=kv, ot=ot: e.tensor_tensor(
                        out=oT[:, kv * 4:(kv + 1) * 4, :].rearrange("p h t -> p (h t)"), in0=ot[:, :], in1=rinv[:], op=ALU.mult),
                        reads=[otres, "rinv"], writes=["oT"])
                for half in range(2):
                    def pe(e, half=half):
                        r = None
                        for h in range(8):
                            r = e.matmul(yps[:, :], lhsT=oT[:, h, :], rhs=wo[:, h, half * 512:(half + 1) * 512], start=(h == 0), stop=(h == 7))
                        return r
                    P.add("pe", pe, reads=["oT", "wo"], writes=["ypsa"])
                    tl.piece(0, yps[:, :], "ypsa", half * 512, (half + 1) * 512, 0)
                tl.rest(0, xc[:], "xc0", dst[c * 128:(c + 1) * 128, :], ("x", id(dst), c))
                load_q()
            return done("ATT")

    w2v = R2[:, 0:22 * 1024].rearrange("p (f d) -> p f d", f=22)
    gwo = R2[:, 0:24 * 1024].rearrange("p (f d) -> p f d", f=24)

    def load_gm_in():
        srcw = gm_w_in.rearrange("(k p) f -> p k f", p=128)
        w_in = R1.rearrange("p (k f) -> p k f", k=8)

        def emit(e):
            return [e.dma_start(out=w_in[:, kc, j * 2048:(j + 1) * 2048], in_=srcw[:, kc, j * 2048:(j + 1) * 2048]) for kc in range(8) for j in range(3)]
        P.add("pool", emit, writes=["R1"], dma="wR1", ndma=24, nobar=True)

    def load_gm_out():
        srcw = gm_w_out.rearrange("(f p) d -> p f d", p=128)

        def emit(e):
            return [e.dma_start(out=gwo[:, f, :], in_=srcw[:, f, :]) for f in range(24)]
        P.add("pool", emit, writes=["R2"], dma="wR2", ndma=24, nobar=True)

    allc = list(range(NCH))
    latc = list(range(NLAT))
    rowf = lambda c: c * 128
    stop = fin
    cur = None
    sched = [
        ("FA00", lambda: phase_FA(0, 0, xin, True)),
        ("FB00", lambda: phase_B("FB00", 22, w2v, 0, 2, 0, xin, xa, allc, rowf, prefetch=load_gm_in)),
        ("G1", lambda: phase_G1(xa)),
        ("G2", lambda: phase_B("G2", 24, gwo, 0, 5, 1, xa, xb, allc, rowf, prefetch=lambda: (load_gm_out(), load_w1(0, 1)))),
        ("FA01", lambda: phase_FA(0, 1, xb, True, prefetch=lambda: load_w2(0, 1))),
        ("FB01", lambda: phase_B("FB01", 22, w2v, 0, 8, 2, xb, xa, allc, rowf, prefetch=lambda: load_w1(1, 0))),
        ("FA10", lambda: phase_FA(1, 0, xa, True, prefetch=lambda: load_w2(1, 0))),
        ("FB10", lambda: phase_B("FB10", 22, w2v, 1, 2, 0, xa, xb, allc, rowf)),
        ("QKV", lambda: phase_QKV(xb)),
        ("ATT", lambda: phase_ATT(xb, xa)),
        ("FA11", lambda: phase_FA(1, 1, xa, False, prefetch=lambda: (load_w1(1, 1, nobar=False), load_w2(1, 1, nobar=False)))),
        ("FB11", lambda: phase_B("FB11", 22, w2v, 1, 8, 2, xa, out, latc, rowf)),
    ]
    streams = {"FB00": xa, "G2": xb, "FB01": xa, "FB10": xb, "ATT": xa}
    for name, fn in sched:
        if stop:
            break
        stop = fn()
        if name in streams:
            cur = streams[name]
        elif name != "FB11":
            cur = None if name in ("FA00",) else cur

    if stop_after == "P0":
        DMA("sp", "dbgm", out[0:36, :].rearrange("(a r) c -> a (r c)", a=4), mrow_d.rearrange("l c n -> (l c) n"), reads=["mrow_d"], writes=[("out", 0)])
        DMA("sp", "dbgc", out[128:256, 0:288], mcol[:].rearrange("p l a b -> p (l a b)"), reads=["mcol"], writes=[("out", 1)])
    if stop_after == "FA00":
        DMA("pool", "dbgh", out[0:384, :].rearrange("(p r) c -> p (r c)", r=3), hT_d[0].rearrange("p f t -> p (f t)"), writes=[("out", 0)])
    if stop and cur is not None:
        for i in range(8):
            DMA("sp", f"dbg{i}", out[i * 512:(i + 1) * 512, :], cur[i * 512:(i + 1) * 512, :], writes=[("out", i)])
    P.add("sp", lambda e: None, reads=[("out", i) for i in range(8)] + [("x", id(out), c) for c in range(NLAT)])

    with nc.Block() as block:
        P.emit_all(nc, block, lambda name: root.enter_context(nc.semaphore(name)))
    root.close()
    return nc, P


def _prep_inputs(inp):
    f = lambda a: np.ascontiguousarray(np.asarray(a, dtype=np.float32))
    x, c, ctx, c_ctx = f(inp["x"]), f(inp["c"]), f(inp["ctx"]), f(inp["c_ctx"])
    shared = dict(
        ident=np.eye(128, dtype=np.float32),
        w_mod=f(inp["w_mod"]),
        bmod2=f(np.broadcast_to(f(inp["b_mod"])[None], (2, 2, 9216))),
        ln_g=f(inp["ln_g"]), ln_b=f(inp["ln_b"]),
        ffn_w_in=f(inp["ffn_w_in"]), ffn_w_out=f(inp["ffn_w_out"]),
        gm_w_in=f(inp["gmlp_w_in"][0]),
        glng=f(f(inp["gmlp_ln_g"])[0].reshape(24, 128).T),
        glnb=f(f(inp["gmlp_ln_b"])[0].reshape(24, 128).T),
        wsT=f(f(inp["gmlp_w_s"])[0].transpose(2, 0, 1)),
        gbs=f(f(inp["gmlp_b_s"])[0].reshape(-1)),
        gm_w_out=f(inp["gmlp_w_out"][0]),
        w_qkv=f(inp["attn_w_qkv"][0]),
        qn=f(inp["attn_q_norm"][0]), kn=f(inp["attn_k_norm"][0]),
        w_o=f(inp["attn_w_o"][0]),
    )
    quarter = 32
    fr = (np.float32(10000.0) ** (-np.arange(quarter, dtype=np.float32) / np.float32(quarter))).astype(np.float32)
    shared["freq2"] = f(np.broadcast_to(np.concatenate([fr, fr])[None], (128, 64)))
    in_maps = []
    for i in range(8):
        b, j = i // 4, i % 4
        m = dict(shared)
        m["xin"] = f(np.concatenate([x[b, j * 4096:(j + 1) * 4096], ctx[b]], axis=0))
        cd = np.stack([c[b], c_ctx], axis=-1)
        m["cond"] = f(cd.reshape(8, 128, 2).transpose(1, 0, 2))
        t = j * 4096 + np.arange(4096)
        rc = np.stack([t // 64, t % 64], axis=-1).astype(np.float32)
        m["posrc"] = f(rc.reshape(32, 128, 2).transpose(1, 0, 2))
        in_maps.append(m)
    return in_maps


_CACHE = {}


def run(inp, stop_after=None, trace=False):
    key = stop_after
    if key not in _CACHE:
        _CACHE[key] = build(stop_after)
    nc, P = _CACHE[key]
    in_maps = _prep_inputs(inp)
    res = run_bass_kernel_spmd(nc, in_maps, core_ids=list(range(8)), trace=trace)
    outs = [np.asarray(r["out"], dtype=np.float32) for r in res.results]
    full = np.stack([np.concatenate(outs[0:4], axis=0), np.concatenate(outs[4:8], axis=0)], axis=0)
    return full, res


def kernel(**inputs):
    full, _ = run(inputs)
    return full
```

```python
from contextlib import ExitStack
import math
import os
import numpy as np
G1MODE = int(os.environ.get('G1MODE', '9'))
G1GELU = int(os.environ.get('G1GELU', '1'))
import concourse.bass as bass
import concourse.mybir as mybir
from concourse.bass_utils import run_bass_kernel_spmd

F32 = mybir.dt.float32
BF16 = mybir.dt.bfloat16
I32 = mybir.dt.int32
AF = mybir.ActivationFunctionType
ALU = mybir.AluOpType
AX = mybir.AxisListType

ENGS = ["pe", "act", "dve", "pool", "sp"]
SEM_LIMIT = 60000


class Prog:
    def __init__(self):
        self.ops = []
        self.last_w = {}
        self.readers = {}
        self.bar_from = 0

    def add(self, eng, emit, reads=(), writes=(), dma=None, ndma=1, inc=16, nobar=False):
        i = len(self.ops)
        deps = {}
        for r in reads:
            d = self.last_w.get(r)
            if d is not None:
                deps[d] = "raw"
        for w in writes:
            d = self.last_w.get(w)
            if d is not None:
                deps.setdefault(d, "waw")
            for rd in self.readers.get(w, ()):
                if rd != i:
                    deps.setdefault(rd, "war")
        for r in reads:
            self.readers.setdefault(r, []).append(i)
        for w in writes:
            self.last_w[w] = i
            self.readers[w] = []
        self.ops.append(dict(eng=eng, emit=emit, deps=deps, dma=dma, ndma=ndma, inc=inc, nobar=nobar))
        return i

    def barrier(self):
        lo, hi = self.bar_from, len(self.ops)
        last = {}
        for i in range(lo, hi):
            op = self.ops[i]
            if op["nobar"]:
                continue
            if op["dma"] is not None:
                last[("dma", op["dma"])] = i
            else:
                last[("eng", op["eng"])] = i
        deps = {i: "raw" for i in last.values()}
        for e in ENGS:
            self.ops.append(dict(eng=e, emit=lambda e_: None, deps=dict(deps), dma=None, ndma=1, inc=16,
                                 nobar=True, isbar=True))
        self.bar_from = len(self.ops)

    def _needs_wait(self, op, dop, kind):
        if dop["dma"] is not None:
            return True
        if dop["eng"] != op["eng"]:
            return True
        if op["dma"] is not None or op.get("isbar"):
            return True
        if op["eng"] == "pe":
            return False
        return kind == "raw"

    def emit_all(self, nc, block, sems):
        ops = self.ops
        n = len(ops)
        need_sig = [False] * n
        for op in ops:
            for d, kind in op["deps"].items():
                dop = ops[d]
                if dop["dma"] is None and self._needs_wait(op, dop, kind):
                    need_sig[d] = True
        cnt = {e: 0 for e in ENGS}
        dcnt = {}
        for i, op in enumerate(ops):
            if op["dma"] is not None:
                k = op["dma"]
                dcnt[k] = dcnt.get(k, 0) + op["inc"] * op["ndma"]
                op["dval"] = dcnt[k]
            elif need_sig[i]:
                cnt[op["eng"]] += 1
                op["sig"] = cnt[op["eng"]]
        self.stats = dict(cnt=cnt, dma_keys=len(dcnt), nops=n, maxdma=max(dcnt.values()) if dcnt else 0)
        assert max(cnt.values()) < SEM_LIMIT, cnt
        assert self.stats["maxdma"] < SEM_LIMIT, self.stats
        eng_sem = {e: sems(f"prog_{e}") for e in ENGS if cnt[e] > 0}
        dma_sem = {k: sems(f"dma_{k}") for k in dcnt}

        def run_engine(ename, e):
            seen = {}
            for i, op in enumerate(ops):
                if op["eng"] != ename:
                    continue
                waits = {}
                for d, kind in op["deps"].items():
                    dop = ops[d]
                    if not self._needs_wait(op, dop, kind):
                        continue
                    if dop["dma"] is not None:
                        s, v = dma_sem[dop["dma"]], dop["dval"]
                    else:
                        s, v = eng_sem[dop["eng"]], dop["sig"]
                    key = id(s)
                    if waits.get(key, (None, 0))[1] < v:
                        waits[key] = (s, v)
                for key, (s, v) in waits.items():
                    if seen.get(key, 0) < v:
                        e.wait_ge(s, v)
                        seen[key] = v
                r = op["emit"](e)
                if op["dma"] is not None:
                    insts = r if isinstance(r, (list, tuple)) else [r]
                    assert len(insts) == op["ndma"], (len(insts), op["ndma"], op["dma"])
                    for ins in insts:
                        ins.then_inc(dma_sem[op["dma"]], op["inc"])
                elif need_sig[i]:
                    assert r is not None, f"op {i} on {ename} must return an instruction"
                    r.then_inc(eng_sem[ename], 1)

        @block.tensor
        def _(e):
            run_engine("pe", e)

        @block.scalar
        def _(e):
            run_engine("act", e)

        @block.vector
        def _(e):
            run_engine("dve", e)

        @block.gpsimd
        def _(e):
            run_engine("pool", e)

        @block.sync
        def _(e):
            run_engine("sp", e)


D = 1024
NCH = 34
NLAT = 32
DFF = 2816
ALPHA = 4.0 ** 0.25
LN_EPS = 1e-5
RMS_EPS = 1e-6
SCALE = 128.0 ** -0.5
NKC = 130
GROUPS = [[0, 1, 2, 3], [4, 5, 6, 7]]
PHASES = ["P0", "FA00", "FB00", "G1", "G2", "FA01", "FB01", "FA10", "FB10", "QKV", "ATT", "FA11", "FB11"]


def build(stop_after=None):
    nc = bass.Bass("TRN2", target_bir_lowering=False)

    def din(name, shape, dt=F32):
        return nc.dram_tensor(name, list(shape), dt, kind="ExternalInput").ap()

    def dscr(name, shape, dt):
        return nc.dram_tensor(name, list(shape), dt).ap()

    xin = din("xin", [NCH * 128, D])
    cond = din("cond", [128, 8, 2])
    posrc = din("posrc", [128, NLAT, 2])
    freq2 = din("freq2", [128, 64])
    identd = din("ident", [128, 128])
    w_mod = din("w_mod", [2, D, 9 * D])
    bmod2 = din("bmod2", [2, 2, 9 * D])
    ln_g = din("ln_g", [2, 3, D])
    ln_b = din("ln_b", [2, 3, D])
    ffn_w_in = din("ffn_w_in", [2, 2, D, 2 * DFF])
    ffn_w_out = din("ffn_w_out", [2, 2, DFF, D])
    gm_w_in = din("gm_w_in", [D, 6144])
    glng = din("glng", [128, 24])
    glnb = din("glnb", [128, 24])
    wsT_d = din("wsT", [128, 8, 128])
    gbs = din("gbs", [8 * 128])
    gm_w_out = din("gm_w_out", [3072, D])
    w_qkv = din("w_qkv", [D, 1536])
    qn_d = din("qn", [128])
    kn_d = din("kn", [128])
    w_o_d = din("w_o", [D, D])
    out = nc.dram_tensor("out", [NLAT * 128, D], F32, kind="ExternalOutput").ap()

    xa = dscr("xa", [NCH * 128, D], F32)
    xb = dscr("xb", [NCH * 128, D], F32)
    hT_d = dscr("hT_d", [NCH, 128, 24, 128], BF16)
    qT_d = dscr("qT_d", [NLAT, 128, 1024], BF16)
    kin = [nc.dram_tensor(f"kin{i}", [128, 4096], BF16) for i in range(2)]
    vin = [nc.dram_tensor(f"vin{i}", [128, 4096], BF16) for i in range(2)]
    kout = [nc.dram_tensor(f"kout{i}", [512, 4096], BF16) for i in range(2)]
    vout = [nc.dram_tensor(f"vout{i}", [512, 4096], BF16) for i in range(2)]
    kc_d = dscr("kc_d", [2, 128, 256], BF16)
    vc_d = dscr("vc_d", [256, 256], BF16)
    mrow_d = dscr("mrow_d", [2, 2, 9 * D], F32)

    P = Prog()
    root = ExitStack()

    uid = [0]

    def sbuf(es, name, shape, dt):
        uid[0] += 1
        return es.enter_context(nc.sbuf_tensor(f"{name}_{uid[0]}", list(shape), dt))

    def psum(es, name, shape, dt):
        uid[0] += 1
        return es.enter_context(nc.psum_tensor(f"{name}_{uid[0]}", list(shape), dt))

    ident_f = sbuf(root, "ident_f", [128, 128], F32)
    ident_b = sbuf(root, "ident_b", [128, 128], BF16)
    mcol = sbuf(root, "mcol", [128, 2, 72, 2], F32)
    BIG = sbuf(root, "BIG", [128, 77824], BF16)
    R1 = BIG[:, 0:49152]
    R2 = BIG[:, 49152:77824]

    def DMA(eng, key, out_, in_, reads=(), writes=(), nobar=False):
        return P.add(eng, lambda e: e.dma_start(out=out_, in_=in_), reads=reads, writes=writes, dma=key, nobar=nobar)

    DMA("sp", "c_ident", ident_f[:], identd, writes=["ident_f"])
    P.add("dve", lambda e: e.tensor_copy(out=ident_b[:], in_=ident_f[:]), reads=["ident_f"], writes=["ident_b"])

    def done(ph):
        P.barrier()
        return stop_after == ph

    def load_w1(layer, j, nobar=True):
        w1 = R1[:, 0:8 * 5632].rearrange("p (k f) -> p k f", k=8)
        src = ffn_w_in[layer, j].rearrange("(k p) f -> p k f", p=128)
        pieces = [(0, 2048), (2048, 4096), (4096, 5632)]

        def emit(e):
            r = []
            for kc in range(8):
                for lo, hi in pieces:
                    r.append(e.dma_start(out=w1[:, kc, lo:hi], in_=src[:, kc, lo:hi]))
            return r
        P.add("pool", emit, writes=["R1"], dma="wR1", ndma=24, nobar=nobar)
        return w1

    def load_w2(layer, j, nobar=True):
        w2 = R2[:, 0:22 * 1024].rearrange("p (f d) -> p f d", f=22)
        src = ffn_w_out[layer, j].rearrange("(f p) d -> p f d", p=128)

        def emit(e):
            return [e.dma_start(out=w2[:, f, :], in_=src[:, f, :]) for f in range(22)]
        P.add("pool", emit, writes=["R2"], dma="wR2", ndma=22, nobar=nobar)
        return w2

    def prep_chunk(xc_t, xc_res, trp, trp_res, xmT, xm_res, col0, layer, shift_mod, scale_mod, cnd):
        def pe(e):
            r = None
            for kc in range(8):
                r = e.transpose(out=trp[:, kc, :], in_=xc_t[:, kc * 128:(kc + 1) * 128], identity=ident_f[:])
            return r
        P.add("pe", pe, reads=[xc_res, "ident_f"], writes=[trp_res])
        for kc in range(8):
            P.add("act", lambda e, kc=kc: e.activation(
                out=xmT[:, kc, col0:col0 + 128], in_=trp[:, kc, :], func=AF.Identity,
                scale=mcol[:, layer, scale_mod * 8 + kc, cnd:cnd + 1],
                bias=mcol[:, layer, shift_mod * 8 + kc, cnd:cnd + 1]),
                reads=[trp_res, "mcol"], writes=[xm_res])

    class Tail:
        def __init__(self, es, layer, gate_mod, ln_idx, nslots=2, conds=(0, 1)):
            self.gate = [sbuf(es, f"gate_bc{c}", [128, D], F32) if c in conds else None for c in range(2)]
            self.g_bc = sbuf(es, "g_bc", [128, D], F32)
            self.b_bc = sbuf(es, "b_bc", [128, D], F32)
            self.t1 = [sbuf(es, f"t1_{s}", [128, D], F32) for s in range(nslots)]
            self.st = [sbuf(es, f"st_{s}", [128, 2, 6], F32) for s in range(nslots)]
            self.mv = [sbuf(es, f"mv_{s}", [128, 2], F32) for s in range(nslots)]
            self.rs = [sbuf(es, f"rs_{s}", [128, 2], F32) for s in range(nslots)]
            self.n = nslots
            self.epsc = sbuf(es, "epsc", [128, 1], F32)
            P.add("pool", lambda e: e.memset(self.epsc[:], LN_EPS), writes=["epsc"])
            for c in conds:
                DMA("sp", f"gatebc{c}", self.gate[c][:],
                    mrow_d[layer, c, gate_mod * D:(gate_mod + 1) * D].partition_broadcast(128),
                    reads=["mrow_d"], writes=[f"gate_bc{c}"])
            DMA("sp", "gbc", self.g_bc[:], ln_g[layer, ln_idx].partition_broadcast(128), writes=["g_bc"])
            DMA("sp", "bbc", self.b_bc[:], ln_b[layer, ln_idx].partition_broadcast(128), writes=["b_bc"])

        def piece(self, s, yap, yres, lo, hi, cnd):
            t1 = self.t1[s]
            P.add("dve", lambda e: e.tensor_tensor(out=t1[:, lo:hi], in0=yap, in1=self.gate[cnd][:, lo:hi], op=ALU.mult),
                  reads=[yres, f"gate_bc{cnd}"], writes=[f"t1_{s}"])

        def rest(self, s, xold, xres, dst, dst_res, before_store=None):
            t1, st, mv, rs = self.t1[s], self.st[s], self.mv[s], self.rs[s]
            r = f"t1_{s}"
            P.add("dve", lambda e: e.scalar_tensor_tensor(out=t1[:], in0=xold, scalar=ALPHA, in1=t1[:],
                                                          op0=ALU.mult, op1=ALU.add), reads=[xres, r], writes=[r])

            def stats(e):
                e.bn_stats(out=st[:, 0, :], in_=t1[:, 0:512])
                return e.bn_stats(out=st[:, 1, :], in_=t1[:, 512:1024])
            P.add("dve", stats, reads=[r], writes=[f"st_{s}"])
            P.add("dve", lambda e: e.bn_aggr(out=mv[:], in_=st[:].rearrange("p a b -> p (a b)")),
                  reads=[f"st_{s}"], writes=[f"mv_{s}"])
            P.add("act", lambda e: e.activation(out=rs[:, 0:1], in_=mv[:, 1:2], func=AF.Sqrt, bias=self.epsc[:, 0:1], scale=1.0),
                  reads=[f"mv_{s}", "epsc"], writes=[f"rsa_{s}"])
            P.add("dve", lambda e: e.reciprocal(out=rs[:, 0:1], in_=rs[:, 0:1]), reads=[f"rsa_{s}"], writes=[f"rsa_{s}"])
            P.add("dve", lambda e: e.tensor_scalar(out=rs[:, 1:2], in0=mv[:, 0:1], scalar1=rs[:, 0:1], scalar2=-1.0,
                                                   op0=ALU.mult, op1=ALU.mult), reads=[f"mv_{s}", f"rsa_{s}"], writes=[f"rsb_{s}"])
            P.add("act", lambda e: e.activation(out=t1[:], in_=t1[:], func=AF.Identity, scale=rs[:, 0:1], bias=rs[:, 1:2]),
                  reads=[r, f"rsa_{s}", f"rsb_{s}"], writes=[r])
            P.add("pool", lambda e: e.tensor_tensor(out=t1[:], in0=t1[:], in1=self.g_bc[:], op=ALU.mult),
                  reads=[r, "g_bc"], writes=[r])
            P.add("pool", lambda e: e.tensor_tensor(out=t1[:], in0=t1[:], in1=self.b_bc[:], op=ALU.add),
                  reads=[r, "b_bc"], writes=[r])
            if before_store:
                before_store()
            DMA("sp", f"st_t1_{s}", dst, t1[:], reads=[r], writes=[dst_res])

    with ExitStack() as es:
        scf = sbuf(es, "scf", [128, 8, 2], F32)
        scb = sbuf(es, "scb", [128, 8, 2], BF16)
        mrow_ = R2[0:2, 0:18432].bitcast(F32)
        mrow = [mrow_, mrow_]
        wblk = [sbuf(es, f"wblk{s}", [128, 8, 512], BF16) for s in range(3)]
        brow = [sbuf(es, f"brow{s}", [2, 512], F32) for s in range(3)]
        mps = [psum(es, f"mps{s}", [128, 512], F32) for s in range(2)]
        tps = psum(es, "tps0", [128, 512], F32)
        DMA("sp", "scf", scf[:], cond, writes=["scf"])
        P.add("act", lambda e: e.activation(out=scb[:], in_=scf[:], func=AF.Silu), reads=["scf"], writes=["scb"])
        for layer in range(2):
            for blk in range(18):
                s3, s2 = blk % 3, blk % 2
                DMA("pool", f"wblk{s3}", wblk[s3][:],
                    w_mod[layer, :, blk * 512:(blk + 1) * 512].rearrange("(k p) c -> p k c", p=128),
                    writes=[f"wblk{s3}"])
                DMA("sp", f"brow{s3}", brow[s3][:], bmod2[:, layer, blk * 512:(blk + 1) * 512], writes=[f"brow{s3}"])

                def pe(e, s3=s3, s2=s2):
                    r = None
                    for kc in range(8):
                        r = e.matmul(mps[s2][0:2, :], lhsT=scb[:, kc, :], rhs=wblk[s3][:, kc, :], start=(kc == 0), stop=(kc == 7))
                    return r
                P.add("pe", pe, reads=["scb", f"wblk{s3}"], writes=[f"mps{s2}"])
                P.add("dve", lambda e, s3=s3, s2=s2, blk=blk, layer=layer: e.tensor_tensor(
                    out=mrow[layer][:, blk * 512:(blk + 1) * 512], in0=mps[s2][0:2, :], in1=brow[s3][:], op=ALU.add),
                    reads=[f"mps{s2}", f"brow{s3}"], writes=["R2"])
            for m in (1, 4, 7):
                P.add("dve", lambda e, m=m, layer=layer: e.tensor_scalar(
                    out=mrow[layer][:, m * D:(m + 1) * D], in0=mrow[layer][:, m * D:(m + 1) * D], scalar1=1.0, scalar2=None, op0=ALU.add),
                    reads=["R2"], writes=["R2"])
            for m in (2, 8):
                P.add("dve", lambda e, m=m, layer=layer: e.tensor_scalar(
                    out=mrow[layer][:, m * D:(m + 1) * D], in0=mrow[layer][:, m * D:(m + 1) * D], scalar1=0.5, scalar2=None, op0=ALU.mult),
                    reads=["R2"], writes=["R2"])
            DMA("sp", f"mrowst{layer}", mrow_d[layer], mrow[layer][:], reads=["R2"], writes=["mrow_d"])

            def pe_t(e, layer=layer):
                r = None
                for jj in range(72):
                    r = e.transpose(out=tps[:, 2 * jj:2 * jj + 2], in_=mrow[layer][0:2, jj * 128:(jj + 1) * 128], identity=ident_f[0:2, 0:2])
                return r
            P.add("pe", pe_t, reads=["R2", "ident_f"], writes=["tps0"])
            P.add("dve", lambda e, layer=layer: e.tensor_copy(
                out=mcol[:, layer, :, :].rearrange("p a b -> p (a b)"), in_=tps[:, 0:144]), reads=["tps0"], writes=["mcol"])
        load_w1(0, 0)
        load_w2(0, 0)
        fin = done("P0")

    def phase_FA(layer, j, src, with_ctx, prefetch=None):
        shift_mod, scale_mod = (0, 1) if j == 0 else (6, 7)
        tiles = [(4 * t, 4, 0) for t in range(8)] + ([(32, 2, 1)] if with_ctx else [])
        w1 = R1[:, 0:8 * 5632].rearrange("p (k f) -> p k f", k=8)
        with ExitStack() as es:
            NX = 6
            xc = [sbuf(es, f"xc{s}", [128, D], F32) for s in range(NX)]
            xmT = [sbuf(es, f"xmT{s}", [128, 8, 512], BF16) for s in range(2)]
            sg = [sbuf(es, f"sg{s}", [128, 512], F32) for s in range(2)]
            hto = [sbuf(es, f"hto{s}", [128, 512], BF16) for s in range(4)]
            trp = [psum(es, f"trp{s}", [128, 8, 128], F32) for s in range(2)]
            gu = [psum(es, f"gu{s}", [128, 2, 512], F32) for s in range(2)]
            if prefetch:
                prefetch()
            chunks = [(c0 + ci, cnd) for (c0, n, cnd) in tiles for ci in range(n)]
            nload = [0]

            def load_next():
                if nload[0] < len(chunks):
                    c, _ = chunks[nload[0]]
                    s = nload[0] % NX
                    DMA("sp", f"xc{s}", xc[s][:], src[c * 128:(c + 1) * 128, :], reads=[("x", id(src), c)], writes=[f"xc{s}"])
                    nload[0] += 1
            for _ in range(NX):
                load_next()
            cidx = [0]

            def prep(ti):
                c0, n, cnd = tiles[ti]
                for ci in range(n):
                    k = cidx[0]
                    s = k % NX
                    prep_chunk(xc[s], f"xc{s}", trp[k % 2], f"trp{k % 2}", xmT[ti % 2], f"xmT{ti % 2}", ci * 128,
                               layer, shift_mod, scale_mod, cnd)
                    cidx[0] += 1
                    load_next()
            prep(0)
            for ti, (c0, n, cnd) in enumerate(tiles):
                NT = n * 128
                xm = xmT[ti % 2]
                for f in range(22):
                    b = f % 2

                    def pe(e, f=f, b=b, xm=xm, NT=NT):
                        r = None
                        for half in range(2):
                            col = half * DFF + f * 128
                            for kc in range(8):
                                r = e.matmul(gu[b][:, half, 0:NT], lhsT=w1[:, kc, col:col + 128], rhs=xm[:, kc, 0:NT],
                                             start=(kc == 0), stop=(kc == 7))
                        return r
                    P.add("pe", pe, reads=["R1", f"xmT{ti % 2}"], writes=[f"gu{b}"])
                    P.add("act", lambda e, b=b, NT=NT: e.activation(out=sg[b][:, 0:NT], in_=gu[b][:, 0, 0:NT], func=AF.Silu),
                          reads=[f"gu{b}"], writes=[f"sg{b}"])
                    hs = f % 4
                    P.add("dve", lambda e, b=b, hs=hs, NT=NT: e.tensor_tensor(out=hto[hs][:, 0:NT], in0=sg[b][:, 0:NT],
                                                                              in1=gu[b][:, 1, 0:NT], op=ALU.mult),
                          reads=[f"sg{b}", f"gu{b}"], writes=[f"hto{hs}"])
                    DMA("sp", f"hto{hs}", hT_d[c0:c0 + n, :, f, :].rearrange("c p t -> p c t"),
                        hto[hs][:, 0:NT].rearrange("p (c t) -> p c t", t=128),
                        reads=[f"hto{hs}"], writes=[("hT", c0, f)])
                    if f == 8 and ti + 1 < len(tiles):
                        prep(ti + 1)
            return done(f"FA{layer}{j}")

    def phase_B(name, nf, w2, layer, gate_mod, ln_idx, xsrc, dst, chunk_list, dst_rows, prefetch=None):
        with ExitStack() as es:
            tl = Tail(es, layer, gate_mod, ln_idx)
            xc = [sbuf(es, f"xc{s}", [128, D], F32) for s in range(2)]
            hin = [sbuf(es, f"hin{s}", [128, nf, 128], BF16) for s in range(2)]
            yps = [psum(es, f"yps{s}", [128, D], F32) for s in range(2)]
            if prefetch:
                prefetch()
            nl = [0]

            def load_next():
                if nl[0] < len(chunk_list):
                    c = chunk_list[nl[0]]
                    s = nl[0] % 2
                    DMA("sp", f"xc{s}", xc[s][:], xsrc[c * 128:(c + 1) * 128, :], reads=[("x", id(xsrc), c)], writes=[f"xc{s}"])
                    DMA("sp", f"hin{s}", hin[s][:], hT_d[c, :, 0:nf, :], reads=[("hT", c)], writes=[f"hin{s}"])
                    nl[0] += 1
            for _ in range(2):
                load_next()
            for k, c in enumerate(chunk_list):
                s, b = k % 2, k % 2
                cnd = 0 if c < NLAT else 1
                for half in range(2):
                    def pe(e, s=s, b=b, half=half):
                        r = None
                        for f in range(nf):
                            r = e.matmul(yps[b][:, half * 512:(half + 1) * 512], lhsT=hin[s][:, f, :],
                                         rhs=w2[:, f, half * 512:(half + 1) * 512], start=(f == 0), stop=(f == nf - 1))
                        return r
                    P.add("pe", pe, reads=["R2", f"hin{s}"], writes=[(f"yps{b}", half)])
                    tl.piece(b, yps[b][:, half * 512:(half + 1) * 512], (f"yps{b}", half), half * 512, (half + 1) * 512, cnd)
                r0 = dst_rows(c)
                tl.rest(b, xc[s][:], f"xc{s}", dst[r0:r0 + 128, :], ("x", id(dst), c), before_store=load_next)
            return done(name)


    def gen_loader(name_prefix, nslots, slots, chunk_ids, src, res_prefix):
        st_ = [0]

        def load_next():
            if st_[0] < len(chunk_ids):
                c = chunk_ids[st_[0]]
                s_ = st_[0] % nslots
                DMA("sp", f"{res_prefix}{s_}", slots[s_][:], src[c * 128:(c + 1) * 128, :],
                    reads=[("x", id(src), c)], writes=[f"{res_prefix}{s_}"])
                st_[0] += 1
        return load_next

    def phase_G1(src, prefetch=None):
        layer, shift_mod, scale_mod = 0, 3, 4
        tiles = [(2 * t, 2, 0) for t in range(16)] + [(32, 2, 1)]
        w_in = R1.rearrange("p (k f) -> p k f", k=8)
        with ExitStack() as es:
            NX = 4
            xc = [sbuf(es, "xc", [128, D], F32) for s_ in range(NX)]
            xmT = [sbuf(es, "xmT", [128, 8, 256], BF16) for s_ in range(2)]
            vh = sbuf(es, "vh", [128, 3072], BF16)
            hho = [sbuf(es, "hho", [128, 24, 128], BF16) for s_ in range(2)]
            wsT = sbuf(es, "wsT", [128, 8, 128], BF16)
            bs_bc = sbuf(es, "bs_bc", [128, 8, 128], F32)
            tmp = [sbuf(es, "tmp", [128, 128], F32) for s_ in range(2)]
            gcol = sbuf(es, "gcol", [128, 24], F32)
            bcol = sbuf(es, "bcol", [128, 24], F32)
            ones_b = sbuf(es, "ones_b", [128, 128], BF16)
            st6 = sbuf(es, "st6", [128, 6, 6], F32)
            mv = sbuf(es, "mvg", [128, 2], F32)
            rs = sbuf(es, "rsg", [128, 2], F32)
            epsc = sbuf(es, "epsg", [128, 1], F32)
            uT = [R2[:, i * 6144:(i + 1) * 6144].rearrange("p (f t) -> p f t", f=24) for i in range(2)]
            v_sb = R2[:, 12288:18432].bitcast(F32)
            Bt = R2[:, 18432:24576].bitcast(F32).rearrange("p (f t) -> p f t", f=24)
            trp = psum(es, "trp", [128, 8, 128], F32)
            ups_ = [psum(es, "ups", [128, 512], F32) for s_ in range(2)]
            vps = [psum(es, "vps", [128, 512], F32) for s_ in range(2)]
            sps = [psum(es, "sps", [128, 4, 128], F32) for s_ in range(2)]
            if prefetch:
                prefetch()
            DMA("pool", "wsT", wsT[:], wsT_d, writes=["wsT"])
            DMA("sp", "gcol", gcol[:], glng, writes=["gcol"])
            DMA("sp", "bcol", bcol[:], glnb, writes=["bcol"])
            DMA("sp", "bsbc", bs_bc[:].rearrange("p g t -> p (g t)"), gbs.partition_broadcast(128), writes=["bs_bc"])
            P.add("pool", lambda e: e.memset(ones_b[:], 1.0), writes=["ones_b"])
            P.add("pool", lambda e: e.memset(epsc[:], LN_EPS), writes=["epsg"])
            for g in range(8):
                P.add("pe", lambda e, g=g: e.matmul(sps[g // 4][:, g % 4, :], lhsT=ones_b[:], rhs=wsT[:, g, :], start=True, stop=True),
                      reads=["ones_b", "wsT"], writes=[f"sps{g // 4}"])
            for cc in range(24):
                g = cc // 3
                P.add("dve", lambda e, cc=cc, g=g: e.scalar_tensor_tensor(
                    out=Bt[:, cc, :], in0=sps[g // 4][:, g % 4, :], scalar=bcol[:, cc:cc + 1], in1=bs_bc[:, g, :],
                    op0=ALU.mult, op1=ALU.add), reads=[f"sps{g // 4}", "bcol", "bs_bc"], writes=["Bt"])
            chunks = [c0 + ci for (c0, n, cnd) in tiles for ci in range(n)]
            load_next = gen_loader("xc", NX, xc, chunks, src, "xc")
            for _ in range(NX):
                load_next()
            kidx = [0]

            def prep(ti):
                c0, n, cnd = tiles[ti]
                for ci in range(n):
                    k = kidx[0]
                    prep_chunk(xc[k % NX], f"xc{k % NX}", trp, "trp", xmT[ti % 2], f"xmT{ti % 2}", ci * 128,
                               layer, shift_mod, scale_mod, cnd)
                    kidx[0] += 1
                    load_next()
            prep(0)
            for ti, (c0, n, cnd) in enumerate(tiles):
                xm = xmT[ti % 2]
                u_ = uT[ti % 2]
                for cc in range(24 if G1MODE >= 2 else 0):
                    def pe(e, cc=cc, xm=xm):
                        r = None
                        for kc in range(8):
                            r = e.matmul(ups_[cc % 2][:, 0:256], lhsT=w_in[:, kc, cc * 128:(cc + 1) * 128], rhs=xm[:, kc, :],
                                         start=(kc == 0), stop=(kc == 7))
                        return r
                    P.add("pe", pe, reads=["R1", f"xmT{ti % 2}"], writes=[f"ups{cc % 2}"])
                    P.add("act", lambda e, cc=cc, u_=u_: e.activation(out=u_[:, cc, :], in_=ups_[cc % 2][:, 0:256], func=(AF.Gelu if G1GELU else AF.Identity)),
                          reads=[f"ups{cc % 2}"], writes=[f"uT{ti % 2}"])
                for ci in range(n):
                    c = c0 + ci
                    if G1MODE < 3:
                        if ci == 0 and ti + 1 < len(tiles):
                            prep(ti + 1)
                        continue
                    for blk in range(6):
                        def pe(e, blk=blk, xm=xm, ci=ci):
                            r = None
                            for kc in range(8):
                                r = e.matmul(vps[blk % 2][:, :], lhsT=xm[:, kc, ci * 128:(ci + 1) * 128],
                                             rhs=w_in[:, kc, 3072 + blk * 512:3072 + (blk + 1) * 512], start=(kc == 0), stop=(kc == 7))
                            return r
                        P.add("pe", pe, reads=["R1", f"xmT{ti % 2}"], writes=[f"vps{blk % 2}"])
                        P.add("act", lambda e, blk=blk: e.activation(out=v_sb[:, blk * 512:(blk + 1) * 512], in_=vps[blk % 2][:, :], func=AF.Gelu),
                              reads=[f"vps{blk % 2}"], writes=["v_sb"])

                    def stats(e):
                        r = None
                        for q in range(6):
                            r = e.bn_stats(out=st6[:, q, :], in_=v_sb[:, q * 512:(q + 1) * 512])
                        return r
                    P.add("dve", stats, reads=["v_sb"], writes=["st6"])
                    P.add("dve", lambda e: e.bn_aggr(out=mv[:], in_=st6[:].rearrange("p a b -> p (a b)")), reads=["st6"], writes=["mvg"])
                    P.add("act", lambda e: e.activation(out=rs[:, 0:1], in_=mv[:, 1:2], func=AF.Sqrt, bias=epsc[:, 0:1], scale=1.0),
                          reads=["mvg", "epsg"], writes=["rsga"])
                    P.add("dve", lambda e: e.reciprocal(out=rs[:, 0:1], in_=rs[:, 0:1]), reads=["rsga"], writes=["rsga"])
                    P.add("dve", lambda e: e.tensor_scalar(out=rs[:, 1:2], in0=mv[:, 0:1], scalar1=rs[:, 0:1], scalar2=-1.0,
                                                           op0=ALU.mult, op1=ALU.mult), reads=["mvg", "rsga"], writes=["rsgb"])
                    P.add("act", lambda e: e.activation(out=vh[:], in_=v_sb[:], func=AF.Identity, scale=rs[:, 0:1], bias=rs[:, 1:2]),
                          reads=["v_sb", "rsga", "rsgb"], writes=["vh"])
                    for cg in range(6 if G1MODE >= 4 else 0):
                        def pe(e, cg=cg):
                            r = None
                            for i in range(4):
                                cc = cg * 4 + i
                                r = e.matmul(sps[cg % 2][:, i, :], lhsT=vh[:, cc * 128:(cc + 1) * 128], rhs=wsT[:, cc // 3, :],
                                             start=True, stop=True)
                            return r
                        P.add("pe", pe, reads=["vh", "wsT"], writes=[f"sps{cg % 2}"])
                        for i in range(4):
                            cc = cg * 4 + i
                            P.add("dve", lambda e, cc=cc, cg=cg, i=i: e.scalar_tensor_tensor(
                                out=tmp[cc % 2][:], in0=sps[cg % 2][:, i, :], scalar=gcol[:, cc:cc + 1], in1=Bt[:, cc, :],
                                op0=ALU.mult, op1=ALU.add), reads=[f"sps{cg % 2}", "gcol", "Bt"], writes=[f"tmp{cc % 2}"])
                            P.add("pool", lambda e, cc=cc, c=c, ci=ci, u_=u_: e.tensor_tensor(
                                out=hho[c % 2][:, cc, :], in0=tmp[cc % 2][:], in1=u_[:, cc, ci * 128:(ci + 1) * 128], op=ALU.mult),
                                reads=[f"tmp{cc % 2}", f"uT{ti % 2}"], writes=[f"hho{c % 2}"])
                    if G1MODE >= 4:
                        DMA("sp", f"hho{c % 2}", hT_d[c], hho[c % 2][:], reads=[f"hho{c % 2}"], writes=[("hT", c)])
                    if ci == 0 and ti + 1 < len(tiles):
                        prep(ti + 1)
            return done("G1")

    def phase_QKV(src, prefetch=None):
        layer, shift_mod, scale_mod = 1, 3, 4
        wq = R1[:, 0:8 * 1536].rearrange("p (k f) -> p k f", k=8)
        with ExitStack() as es:
            NX = 3
            xc = [sbuf(es, "xc", [128, D], F32) for s_ in range(NX)]
            xmT = [sbuf(es, "xmT", [128, 8, 128], BF16) for s_ in range(2)]
            sq = [sbuf(es, "sq", [128, 512], F32) for s_ in range(3)]
            ss = sbuf(es, "ss", [128, 12], F32)
            rst = sbuf(es, "rst", [128, 12], F32)
            qn = sbuf(es, "qnb", [128, 10, 128], F32)
            ra = sbuf(es, "ra", [128, 10, 64], F32)
            rb = sbuf(es, "rb", [128, 10, 64], F32)
            rc = sbuf(es, "rc", [128, 10, 64], F32)
            rd = sbuf(es, "rd", [128, 10, 64], F32)
            qr = sbuf(es, "qr", [128, 10, 128], BF16)
            cosT = R2[:, 12288:16384].bitcast(F32).rearrange("p (c j) -> p c j", c=NLAT)
            sinT = R2[:, 16384:20480].bitcast(F32).rearrange("p (c j) -> p c j", c=NLAT)
            ang = R2[:, 0:4096].bitcast(F32).rearrange("p (c j) -> p c j", c=NLAT)
            angi = R2[:, 4096:8192].bitcast(I32).rearrange("p (c j) -> p c j", c=NLAT)
            ang2 = R2[:, 8192:12288].bitcast(F32).rearrange("p (c j) -> p c j", c=NLAT)
            prc = sbuf(es, "prc", [128, NLAT, 2], F32)
            frq = sbuf(es, "frq", [128, 64], F32)
            gq_bc = sbuf(es, "gq_bc", [128, 128], F32)
            gk_bc = sbuf(es, "gk_bc", [128, 128], F32)
            qTs = [sbuf(es, "qTs", [128, 8, 128], BF16) for s_ in range(2)]
            kTs = [sbuf(es, "kTs", [128, 2, 128], BF16) for s_ in range(2)]
            vs = [sbuf(es, "vs", [128, 256], BF16) for s_ in range(2)]
            trp = psum(es, "trp", [128, 8, 128], F32)
            qkv = [psum(es, "qkv", [128, 512], F32) for s_ in range(3)]
            tq = psum(es, "tq", [128, 8, 128], BF16)
            tk = psum(es, "tk", [128, 8, 128], BF16)
            if prefetch:
                prefetch()

            def emit_w(e):
                srcw = w_qkv.rearrange("(k p) f -> p k f", p=128)
                return [e.dma_start(out=wq[:, kc, :], in_=srcw[:, kc, :]) for kc in range(8)]
            P.add("pool", emit_w, writes=["R1"], dma="wR1", ndma=8)
            DMA("sp", "prc", prc[:], posrc, writes=["prc"])
            DMA("sp", "frq", frq[:], freq2, writes=["frq"])
            DMA("sp", "gqbc", gq_bc[:], qn_d.partition_broadcast(128), writes=["gq_bc"])
            DMA("sp", "gkbc", gk_bc[:], kn_d.partition_broadcast(128), writes=["gk_bc"])
            for hh_ in range(2):
                P.add("dve", lambda e, hh_=hh_: e.tensor_tensor(
                    out=ang[:, :, hh_ * 32:(hh_ + 1) * 32], in0=prc[:, :, hh_:hh_ + 1].broadcast_to([128, NLAT, 32]),
                    in1=frq[:, hh_ * 32:(hh_ + 1) * 32].unsqueeze(1).broadcast_to([128, NLAT, 32]), op=ALU.mult),
                    reads=["prc", "frq"], writes=["ang"])
            P.add("dve", lambda e: e.tensor_scalar(out=ang[:], in0=ang[:], scalar1=1.0 / (2.0 * math.pi), scalar2=None, op0=ALU.mult),
                  reads=["ang"], writes=["ang"])
            P.add("dve", lambda e: e.tensor_copy(out=angi[:], in_=ang[:]), reads=["ang"], writes=["angi"])
            P.add("dve", lambda e: e.tensor_copy(out=ang2[:], in_=angi[:]), reads=["angi"], writes=["ang2"])
            P.add("dve", lambda e: e.tensor_tensor(out=ang[:], in0=ang[:], in1=ang2[:], op=ALU.subtract), reads=["ang", "ang2"], writes=["ang"])
            P.add("dve", lambda e: e.tensor_scalar(out=ang2[:], in0=ang[:], scalar1=-1.0, scalar2=None, op0=ALU.mult), reads=["ang"], writes=["ang2"])
            P.add("dve", lambda e: e.tensor_tensor(out=ang2[:], in0=ang2[:], in1=ang[:], op=ALU.max), reads=["ang", "ang2"], writes=["ang2"])
            P.add("act", lambda e: e.activation(out=sinT[:], in_=ang[:], func=AF.Sin, scale=math.pi), reads=["ang"], writes=["sinT"])
            P.add("dve", lambda e: e.tensor_scalar(out=ang2[:], in0=ang2[:], scalar1=-math.pi, scalar2=math.pi / 2, op0=ALU.mult, op1=ALU.add),
                  reads=["ang2"], writes=["ang2"])
            P.add("act", lambda e: e.activation(out=cosT[:], in_=ang2[:], func=AF.Sin), reads=["ang2"], writes=["cosT"])
            P.add("dve", lambda e: e.tensor_tensor(out=ang[:], in0=sinT[:], in1=sinT[:], op=ALU.mult), reads=["sinT"], writes=["ang"])
            P.add("dve", lambda e: e.scalar_tensor_tensor(out=sinT[:], in0=sinT[:], scalar=2.0, in1=cosT[:], op0=ALU.mult, op1=ALU.mult),
                  reads=["sinT", "cosT", "ang"], writes=["sinT"])
            P.add("dve", lambda e: e.tensor_scalar(out=cosT[:], in0=ang[:], scalar1=-2.0, scalar2=1.0, op0=ALU.mult, op1=ALU.add),
                  reads=["ang", "sinT"], writes=["cosT"])
            chunks = list(range(NCH))
            load_next = gen_loader("xc", NX, xc, chunks, src, "xc")
            for _ in range(NX):
                load_next()
            for k, c in enumerate(chunks):
                lat = c < NLAT
                cnd = 0 if lat else 1
                b = k % 2
                prep_chunk(xc[k % NX], f"xc{k % NX}", trp, "trp", xmT[b], f"xmT{b}", 0, layer, shift_mod, scale_mod, cnd)
                load_next()
                blks = (0, 1, 2) if lat else (2,)
                for blk in blks:
                    def pe(e, blk=blk, b=b):
                        r = None
                        for kc in range(8):
                            r = e.matmul(qkv[blk][:, :], lhsT=xmT[b][:, kc, :], rhs=wq[:, kc, blk * 512:(blk + 1) * 512],
                                         start=(kc == 0), stop=(kc == 7))
                        return r
                    P.add("pe", pe, reads=["R1", f"xmT{b}"], writes=[f"qkv{blk}"])
                    P.add("act", lambda e, blk=blk: e.activation(out=sq[blk][:], in_=qkv[blk][:, :], func=AF.Square),
                          reads=[f"qkv{blk}"], writes=[f"sq{blk}"])
                    P.add("dve", lambda e, blk=blk: e.tensor_reduce(out=ss[:, blk * 4:(blk + 1) * 4],
                                                                    in_=sq[blk][:].rearrange("p (h d) -> p h d", h=4), axis=AX.X, op=ALU.add),
                          reads=[f"sq{blk}"], writes=["ss"])
                P.add("dve", lambda e: e.tensor_scalar(out=rst[:], in0=ss[:], scalar1=1.0 / 128.0, scalar2=RMS_EPS, op0=ALU.mult, op1=ALU.add),
                      reads=["ss"], writes=["rst"])
                P.add("act", lambda e: e.activation(out=rst[:], in_=rst[:], func=AF.Sqrt), reads=["rst"], writes=["rst"])
                P.add("dve", lambda e: e.reciprocal(out=rst[:], in_=rst[:]), reads=["rst"], writes=["rst"])
                heads = list(range(10)) if lat else [8, 9]
                for h in heads:
                    blk, off = (h // 4, (h % 4) * 128) if h < 8 else (2, (h - 8) * 128)
                    gb = gq_bc if h < 8 else gk_bc
                    P.add("dve", lambda e, h=h, blk=blk, off=off, gb=gb: e.scalar_tensor_tensor(
                        out=qn[:, h, :], in0=qkv[blk][:, off:off + 128], scalar=rst[:, (blk * 4 + (off // 128)):(blk * 4 + (off // 128)) + 1],
                        in1=gb[:], op0=ALU.mult, op1=ALU.mult), reads=[f"qkv{blk}", "rst", "gq_bc", "gk_bc"], writes=["qn"])
                if lat:
                    x1, x2 = qn[:, :, 0:64], qn[:, :, 64:128]
                    cb = cosT[:, c, :].unsqueeze(1).broadcast_to([128, 10, 64])
                    sb_ = sinT[:, c, :].unsqueeze(1).broadcast_to([128, 10, 64])
                    P.add("pool", lambda e, x1=x1, cb=cb: e.tensor_tensor(out=ra[:], in0=x1, in1=cb, op=ALU.mult), reads=["qn", "cosT"], writes=["ra"])
                    P.add("pool", lambda e, x2=x2, sb_=sb_: e.tensor_tensor(out=rb[:], in0=x2, in1=sb_, op=ALU.mult), reads=["qn", "sinT"], writes=["rb"])
                    P.add("dve", lambda e: e.tensor_tensor(out=qr[:, :, 0:64], in0=ra[:], in1=rb[:], op=ALU.subtract), reads=["ra", "rb"], writes=["qr"])
                    P.add("pool", lambda e, x2=x2, cb=cb: e.tensor_tensor(out=rc[:], in0=x2, in1=cb, op=ALU.mult), reads=["qn", "cosT"], writes=["rc"])
                    P.add("pool", lambda e, x1=x1, sb_=sb_: e.tensor_tensor(out=rd[:], in0=x1, in1=sb_, op=ALU.mult), reads=["qn", "sinT"], writes=["rd"])
                    P.add("dve", lambda e: e.tensor_tensor(out=qr[:, :, 64:128], in0=rc[:], in1=rd[:], op=ALU.add), reads=["rc", "rd"], writes=["qr"])
                else:
                    P.add("dve", lambda e: e.tensor_copy(out=qr[:, 8:10, :], in_=qn[:, 8:10, :]), reads=["qn"], writes=["qr"])

                def pe_t(e, lat=lat):
                    r = None
                    if lat:
                        for h in range(8):
                            r = e.transpose(out=tq[:, h, :], in_=qr[:, h, :], identity=ident_b[:])
                    for kv in range(2):
                        r = e.transpose(out=tk[:, kv, :], in_=qr[:, 8 + kv, :], identity=ident_b[:])
                    return r
                P.add("pe", pe_t, reads=["qr", "ident_b"], writes=["tq", "tk"])
                if lat:
                    P.add("act", lambda e, b=b: e.activation(out=qTs[b][:], in_=tq[:], func=AF.Copy), reads=["tq"], writes=[f"qTs{b}"])
                    DMA("sp", f"qTs{b}", qT_d[c].rearrange("p (h t) -> p h t", h=8), qTs[b][:], reads=[f"qTs{b}"], writes=[("qT", c)])
                P.add("dve", lambda e, b=b: e.tensor_copy(out=kTs[b][:], in_=tk[:, 0:2, :]), reads=["tk"], writes=[f"kTs{b}"])
                P.add("act", lambda e, b=b: e.activation(out=vs[b][:], in_=qkv[2][:, 256:512], func=AF.Copy), reads=["qkv2"], writes=[f"vs{b}"])
                if lat:
                    def st_k(e, b=b, c=c):
                        return [e.dma_start(out=kin[kv].ap()[:, c * 128:(c + 1) * 128], in_=kTs[b][:, kv, :]) for kv in range(2)]
                    P.add("sp", st_k, reads=[f"kTs{b}"], writes=["kin"], dma=f"kTs{b}", ndma=2)
                    hv, j16 = c // 16, c % 16
                    DMA("sp", f"vs{b}", vin[hv].ap()[j16 * 8:(j16 + 1) * 8, :].rearrange("a (b c) -> (a b) c", c=256), vs[b][:],
                        reads=[f"vs{b}"], writes=["vin"])
                else:
                    cc_ = c - NLAT

                    def st_k(e, b=b, cc_=cc_):
                        return [e.dma_start(out=kc_d[kv, :, cc_ * 128:(cc_ + 1) * 128], in_=kTs[b][:, kv, :]) for kv in range(2)]
                    P.add("sp", st_k, reads=[f"kTs{b}"], writes=["kc_d"], dma=f"kTs{b}", ndma=2)
                    DMA("sp", f"vs{b}", vc_d[cc_ * 128:(cc_ + 1) * 128, :], vs[b][:], reads=[f"vs{b}"], writes=["vc_d"])
            P.barrier()
            for i in range(2):
                P.add("pool", lambda e, i=i: e.collective_compute("AllGather", ALU.bypass, replica_groups=GROUPS,
                                                                   ins=[kin[i].ap().opt()], outs=[kout[i].ap().opt()]),
                      reads=["kin"], writes=[("kout", i)], dma=f"cck{i}", inc=1)
                P.add("pool", lambda e, i=i: e.collective_compute("AllGather", ALU.bypass, replica_groups=GROUPS,
                                                                   ins=[vin[i].ap().opt()], outs=[vout[i].ap().opt()]),
                      reads=["vin"], writes=[("vout", i)], dma=f"ccv{i}", inc=1)
            return done("QKV")

    def phase_ATT(xsrc, dst):
        layer = 1
        KT = BIG[:, 0:33280].rearrange("p (k t) -> p k t", k=2)
        V = BIG[:, 33280:66560].rearrange("p (c k d) -> p c k d", c=NKC, k=2)
        wo = BIG[:, 66560:74752].rearrange("p (h d) -> p h d", h=8)
        NP = NKC // 2
        with ExitStack() as es:
            tl = Tail(es, layer, 5, 1, nslots=1, conds=(0,))
            xc = sbuf(es, "xc", [128, D], F32)
            qTt = [sbuf(es, "qTt", [128, 8, 128], BF16) for s_ in range(2)]
            NPT = 5
            pt = [sbuf(es, "pt", [128, 1024], BF16) for s_ in range(NPT)]
            acc = {"dve": sbuf(es, "accD", [128, 1024], F32), "pool": sbuf(es, "accP", [128, 1024], F32)}
            accb = sbuf(es, "accb", [128, 2048], BF16)
            ones_b = sbuf(es, "ones_b", [128, 128], BF16)
            rinv = sbuf(es, "rinv", [128, 512], F32)
            oT = sbuf(es, "oT", [128, 8, 128], BF16)
            spsm = [psum(es, "spsm", [128, 1024], F32) for s_ in range(2)]
            OT = [psum(es, "OT", [128, 512], F32) for s_ in range(2)]
            sump = psum(es, "sump", [128, 512], F32)
            yps = psum(es, "ypsa", [128, 512], F32)
            P.add("pool", lambda e: e.memset(ones_b[:], 1.0), writes=["ones_b"])
            vres = [("Vld", hv, rk) for hv in range(2) for rk in range(4)] + ["Vc"]
            for kv in range(2):
                def ld_k(e, kv=kv):
                    r = [e.dma_start(out=KT[:, kv, rk * 4096:(rk + 1) * 4096], in_=kout[kv].ap()[rk * 128:(rk + 1) * 128, :]) for rk in range(4)]
                    r.append(e.dma_start(out=KT[:, kv, 16384:16640], in_=kc_d[kv]))
                    return r
                P.add("sp", ld_k, reads=[("kout", kv), "kc_d", "R1"], writes=[("Kld", kv)], dma=f"ldk{kv}", ndma=5)
            for hv in range(2):
                for rk in range(4):
                    def ld_v(e, hv=hv, rk=rk):
                        srcv = vout[hv].ap()[rk * 128:(rk + 1) * 128, :].rearrange("(j a) (b k d) -> (a b) j k d", a=8, b=16, k=2)
                        c0 = rk * 32 + hv * 16
                        r = []
                        for kv in range(2):
                            for jh in range(2):
                                r.append(e.dma_start(out=V[:, c0 + jh * 8:c0 + (jh + 1) * 8, kv, :], in_=srcv[:, jh * 8:(jh + 1) * 8, kv, :]))
                        return r
                    P.add("sp", ld_v, reads=[("vout", hv), "R1", "R2"], writes=[("Vld", hv, rk)], dma=f"ldv{hv}{rk}", ndma=4)

            def ld_vc(e):
                srcv = vc_d.rearrange("(j p) (k d) -> p j k d", p=128, k=2)
                return [e.dma_start(out=V[:, 128:130, kv, :], in_=srcv[:, :, kv, :]) for kv in range(2)]
            P.add("sp", ld_vc, reads=["vc_d", "R2"], writes=["Vc"], dma="ldvc", ndma=2)

            def emit_wo(e):
                srcw = w_o_d.rearrange("(h p) d -> p h d", p=128)
                return [e.dma_start(out=wo[:, h, :], in_=srcw[:, h, :]) for h in range(8)]
            P.add("pool", emit_wo, reads=["R2"], writes=["wo"], dma="wo", ndma=8)
            P.add("pe", lambda e: None, reads=vres + [("Kld", 0), ("Kld", 1)], writes=["KVready"])
            qts = list(range(NLAT))
            nl = [0]

            def load_q():
                if nl[0] < len(qts):
                    c = qts[nl[0]]
                    s_ = nl[0] % 2
                    DMA("sp", f"qTt{s_}", qTt[s_][:].rearrange("p h t -> p (h t)"), qT_d[c], reads=[("qT", c)], writes=[f"qTt{s_}"])
                    nl[0] += 1
            load_q()
            load_q()
            first = {"dve": True, "pool": True, "pe": True}
            SUMENG = ["dve", "pool", "pe", "dve", "pe", "pool", "dve", "pe"]
            for qi, c in enumerate(qts):
                s_ = qi % 2
                DMA("sp", "xc0", xc[:], xsrc[c * 128:(c + 1) * 128, :], reads=[("x", id(xsrc), c)], writes=["xc0"])
                for kv in range(2):
                    g = qi * 2 + kv
                    ot = OT[g % 2]
                    otres = f"OT{g % 2}"
                    rhs = qTt[s_][:, kv * 4:(kv + 1) * 4, :].rearrange("p h t -> p (h t)")
                    first["dve"] = True
                    first["pool"] = True
                    first["pe"] = True

                    def S(p, kv=kv, rhs=rhs):
                        sp_ = spsm[p % 2]

                        def pe(e):
                            e.matmul(sp_[:, 0:512], lhsT=KT[:, kv, (2 * p) * 128:(2 * p + 1) * 128], rhs=rhs, start=True, stop=True)
                            return e.matmul(sp_[:, 512:1024], lhsT=KT[:, kv, (2 * p + 1) * 128:(2 * p + 2) * 128], rhs=rhs, start=True, stop=True)
                        P.add("pe", pe, reads=["KVready", f"qTt{s_}"], writes=[f"spsm{p % 2}"])
                        P.add("act", lambda e: e.activation(out=pt[p % NPT][:], in_=sp_[:, :], func=AF.Exp, scale=SCALE),
                              reads=[f"spsm{p % 2}"], writes=[f"pt{p % NPT}"])

                    def PV(p, kv=kv, ot=ot, otres=otres):
                        def pe(e):
                            e.matmul(ot[:, :], lhsT=V[:, 2 * p, kv, :], rhs=pt[p % NPT][:, 0:512], start=(p == 0), stop=False)
                            return e.matmul(ot[:, :], lhsT=V[:, 2 * p + 1, kv, :], rhs=pt[p % NPT][:, 512:1024], start=False, stop=(p == NP - 1))
                        P.add("pe", pe, reads=[f"pt{p % NPT}", "KVready"], writes=[otres])
                        en = SUMENG[p % 8]
                        if en == "pe":
                            st0 = first["pe"]
                            first["pe"] = False

                            def pes(e):
                                e.matmul(sump[:, :], lhsT=ones_b[:], rhs=pt[p % NPT][:, 0:512], start=st0, stop=False)
                                return e.matmul(sump[:, :], lhsT=ones_b[:], rhs=pt[p % NPT][:, 512:1024], start=False, stop=False)
                            P.add("pe", pes, reads=[f"pt{p % NPT}", "ones_b"], writes=["sump"])
                            return
                        a_ = acc[en]
                        if first[en]:
                            first[en] = False
                            P.add(en, lambda e: e.tensor_copy(out=a_[:], in_=pt[p % NPT][:]), reads=[f"pt{p % NPT}"], writes=[f"acc_{en}"])
                        else:
                            P.add(en, lambda e: e.tensor_tensor(out=a_[:], in0=a_[:], in1=pt[p % NPT][:], op=ALU.add),
                                  reads=[f"pt{p % NPT}", f"acc_{en}"], writes=[f"acc_{en}"])
                    S(0)
                    for p in range(NP):
                        if p + 1 < NP:
                            S(p + 1)
                        PV(p)
                    P.add("dve", lambda e: e.tensor_copy(out=accb[:, 0:1024], in_=acc["dve"][:]), reads=["acc_dve"], writes=["accbD"])
                    P.add("pool", lambda e: e.tensor_copy(out=accb[:, 1024:2048], in_=acc["pool"][:]), reads=["acc_pool"], writes=["accbP"])

                    def pe_sum(e):
                        r = None
                        for i in range(4):
                            r = e.matmul(sump[:, :], lhsT=ones_b[:], rhs=accb[:, i * 512:(i + 1) * 512], start=False, stop=(i == 3))
                        return r
                    P.add("pe", pe_sum, reads=["ones_b", "accbD", "accbP"], writes=["sump"])
                    P.add("dve", lambda e: e.reciprocal(out=rinv[:], in_=sump[:, :]), reads=["sump"], writes=["rinv"])
                    P.add("dve", lambda e, kv=kv, ot=ot: e.tensor_tensor(
                        out=oT[:, kv * 4:(kv + 1) * 4, :].rearrange("p h t -> p (h t)"), in0=ot[:, :], in1=rinv[:], op=ALU.mult),
                        reads=[otres, "rinv"], writes=["oT"])
                for half in range(2):
                    def pe(e, half=half):
                        r = None
                        for h in range(8):
                            r = e.matmul(yps[:, :], lhsT=oT[:, h, :], rhs=wo[:, h, half * 512:(half + 1) * 512], start=(h == 0), stop=(h == 7))
                        return r
                    P.add("pe", pe, reads=["oT", "wo"], writes=["ypsa"])
                    tl.piece(0, yps[:, :], "ypsa", half * 512, (half + 1) * 512, 0)
                tl.rest(0, xc[:], "xc0", dst[c * 128:(c + 1) * 128, :], ("x", id(dst), c))
                load_q()
            return done("ATT")

    w2v = R2[:, 0:22 * 1024].rearrange("p (f d) -> p f d", f=22)
    gwo = R2[:, 0:24 * 1024].rearrange("p (f d) -> p f d", f=24)

    def load_gm_in():
        srcw = gm_w_in.rearrange("(k p) f -> p k f", p=128)
        w_in = R1.rearrange("p (k f) -> p k f", k=8)

        def emit(e):
            return [e.dma_start(out=w_in[:, kc, j * 2048:(j + 1) * 2048], in_=srcw[:, kc, j * 2048:(j + 1) * 2048]) for kc in range(8) for j in range(3)]
        P.add("pool", emit, writes=["R1"], dma="wR1", ndma=24, nobar=True)

    def load_gm_out():
        srcw = gm_w_out.rearrange("(f p) d -> p f d", p=128)

        def emit(e):
            return [e.dma_start(out=gwo[:, f, :], in_=srcw[:, f, :]) for f in range(24)]
        P.add("pool", emit, writes=["R2"], dma="wR2", ndma=24, nobar=True)

    allc = list(range(NCH))
    latc = list(range(NLAT))
    rowf = lambda c: c * 128
    stop = fin
    cur = None
    sched = [
        ("FA00", lambda: phase_FA(0, 0, xin, True)),
        ("FB00", lambda: phase_B("FB00", 22, w2v, 0, 2, 0, xin, xa, allc, rowf, prefetch=load_gm_in)),
        ("G1", lambda: phase_G1(xa)),
        ("G2", lambda: phase_B("G2", 24, gwo, 0, 5, 1, xa, xb, allc, rowf, prefetch=lambda: (load_gm_out(), load_w1(0, 1)))),
        ("FA01", lambda: phase_FA(0, 1, xb, True, prefetch=lambda: load_w2(0, 1))),
        ("FB01", lambda: phase_B("FB01", 22, w2v, 0, 8, 2, xb, xa, allc, rowf, prefetch=lambda: load_w1(1, 0))),
        ("FA10", lambda: phase_FA(1, 0, xa, True, prefetch=lambda: load_w2(1, 0))),
        ("FB10", lambda: phase_B("FB10", 22, w2v, 1, 2, 0, xa, xb, allc, rowf)),
        ("QKV", lambda: phase_QKV(xb)),
        ("ATT", lambda: phase_ATT(xb, xa)),
        ("FA11", lambda: phase_FA(1, 1, xa, False, prefetch=lambda: (load_w1(1, 1, nobar=False), load_w2(1, 1, nobar=False)))),
        ("FB11", lambda: phase_B("FB11", 22, w2v, 1, 8, 2, xa, out, latc, rowf)),
    ]
    streams = {"FB00": xa, "G2": xb, "FB01": xa, "FB10": xb, "ATT": xa}
    for name, fn in sched:
        if stop:
            break
        stop = fn()
        if name in streams:
            cur = streams[name]
        elif name != "FB11":
            cur = None if name in ("FA00",) else cur

    if stop_after == "P0":
        DMA("sp", "dbgm", out[0:36, :].rearrange("(a r) c -> a (r c)", a=4), mrow_d.rearrange("l c n -> (l c) n"), reads=["mrow_d"], writes=[("out", 0)])
        DMA("sp", "dbgc", out[128:256, 0:288], mcol[:].rearrange("p l a b -> p (l a b)"), reads=["mcol"], writes=[("out", 1)])
    if stop_after == "FA00":
        DMA("pool", "dbgh", out[0:384, :].rearrange("(p r) c -> p (r c)", r=3), hT_d[0].rearrange("p f t -> p (f t)"), writes=[("out", 0)])
    if stop and cur is not None:
        for i in range(8):
            DMA("sp", f"dbg{i}", out[i * 512:(i + 1) * 512, :], cur[i * 512:(i + 1) * 512, :], writes=[("out", i)])
    P.add("sp", lambda e: None, reads=[("out", i) for i in range(8)] + [("x", id(out), c) for c in range(NLAT)])

    with nc.Block() as block:
        P.emit_all(nc, block, lambda name: root.enter_context(nc.semaphore(name)))
    root.close()
    return nc, P


def _prep_inputs(inp):
    f = lambda a: np.ascontiguousarray(np.asarray(a, dtype=np.float32))
    x, c, ctx, c_ctx = f(inp["x"]), f(inp["c"]), f(inp["ctx"]), f(inp["c_ctx"])
    shared = dict(
        ident=np.eye(128, dtype=np.float32),
        w_mod=f(inp["w_mod"]),
        bmod2=f(np.broadcast_to(f(inp["b_mod"])[None], (2, 2, 9216))),
        ln_g=f(inp["ln_g"]), ln_b=f(inp["ln_b"]),
        ffn_w_in=f(inp["ffn_w_in"]), ffn_w_out=f(inp["ffn_w_out"]),
        gm_w_in=f(inp["gmlp_w_in"][0]),
        glng=f(f(inp["gmlp_ln_g"])[0].reshape(24, 128).T),
        glnb=f(f(inp["gmlp_ln_b"])[0].reshape(24, 128).T),
        wsT=f(f(inp["gmlp_w_s"])[0].transpose(2, 0, 1)),
        gbs=f(f(inp["gmlp_b_s"])[0].reshape(-1)),
        gm_w_out=f(inp["gmlp_w_out"][0]),
        w_qkv=f(inp["attn_w_qkv"][0]),
        qn=f(inp["attn_q_norm"][0]), kn=f(inp["attn_k_norm"][0]),
        w_o=f(inp["attn_w_o"][0]),
    )
    quarter = 32
    fr = (np.float32(10000.0) ** (-np.arange(quarter, dtype=np.float32) / np.float32(quarter))).astype(np.float32)
    shared["freq2"] = f(np.broadcast_to(np.concatenate([fr, fr])[None], (128, 64)))
    in_maps = []
    for i in range(8):
        b, j = i // 4, i % 4
        m = dict(shared)
        m["xin"] = f(np.concatenate([x[b, j * 4096:(j + 1) * 4096], ctx[b]], axis=0))
        cd = np.stack([c[b], c_ctx], axis=-1)
        m["cond"] = f(cd.reshape(8, 128, 2).transpose(1, 0, 2))
        t = j * 4096 + np.arange(4096)
        rc = np.stack([t // 64, t % 64], axis=-1).astype(np.float32)
        m["posrc"] = f(rc.reshape(32, 128, 2).transpose(1, 0, 2))
        in_maps.append(m)
    return in_maps


_CACHE = {}


def run(inp, stop_after=None, trace=False):
    key = stop_after
    if key not in _CACHE:
        _CACHE[key] = build(stop_after)
    nc, P = _CACHE[key]
    in_maps = _prep_inputs(inp)
    res = run_bass_kernel_spmd(nc, in_maps, core_ids=list(range(8)), trace=trace)
    outs = [np.asarray(r["out"], dtype=np.float32) for r in res.results]
    full = np.stack([np.concatenate(outs[0:4], axis=0), np.concatenate(outs[4:8], axis=0)], axis=0)
    return full, res


def kernel(**inputs):
    full, _ = run(inputs)
    return full
```

```python
from contextlib import ExitStack
import math
import os
import numpy as np
G1MODE = int(os.environ.get('G1MODE', '9'))
G1GELU = int(os.environ.get('G1GELU', '1'))
import concourse.bass as bass
import concourse.mybir as mybir
from concourse.bass_utils import run_bass_kernel_spmd

F32 = mybir.dt.float32
BF16 = mybir.dt.bfloat16
I32 = mybir.dt.int32
AF = mybir.ActivationFunctionType
ALU = mybir.AluOpType
AX = mybir.AxisListType

ENGS = ["pe", "act", "dve", "pool", "sp"]
SEM_LIMIT = 60000


class Prog:
    def __init__(self):
        self.ops = []
        self.last_w = {}
        self.readers = {}
        self.bar_from = 0

    def add(self, eng, emit, reads=(), writes=(), dma=None, ndma=1, inc=16, nobar=False):
        i = len(self.ops)
        deps = {}
        for r in reads:
            d = self.last_w.get(r)
            if d is not None:
                deps[d] = "raw"
        for w in writes:
            d = self.last_w.get(w)
            if d is not None:
                deps.setdefault(d, "waw")
            for rd in self.readers.get(w, ()):
                if rd != i:
                    deps.setdefault(rd, "war")
        for r in reads:
            self.readers.setdefault(r, []).append(i)
        for w in writes:
            self.last_w[w] = i
            self.readers[w] = []
        self.ops.append(dict(eng=eng, emit=emit, deps=deps, dma=dma, ndma=ndma, inc=inc, nobar=nobar))
        return i

    def barrier(self):
        lo, hi = self.bar_from, len(self.ops)
        last = {}
        for i in range(lo, hi):
            op = self.ops[i]
            if op["nobar"]:
                continue
            if op["dma"] is not None:
                last[("dma", op["dma"])] = i
            else:
                last[("eng", op["eng"])] = i
        deps = {i: "raw" for i in last.values()}
        for e in ENGS:
            self.ops.append(dict(eng=e, emit=lambda e_: None, deps=dict(deps), dma=None, ndma=1, inc=16,
                                 nobar=True, isbar=True))
        self.bar_from = len(self.ops)

    def _needs_wait(self, op, dop, kind):
        if dop["dma"] is not None:
            return True
        if dop["eng"] != op["eng"]:
            return True
        if op["dma"] is not None or op.get("isbar"):
            return True
        if op["eng"] == "pe":
            return False
        return kind == "raw"

    def emit_all(self, nc, block, sems):
        ops = self.ops
        n = len(ops)
        need_sig = [False] * n
        for op in ops:
            for d, kind in op["deps"].items():
                dop = ops[d]
                if dop["dma"] is None and self._needs_wait(op, dop, kind):
                    need_sig[d] = True
        cnt = {e: 0 for e in ENGS}
        dcnt = {}
        for i, op in enumerate(ops):
            if op["dma"] is not None:
                k = op["dma"]
                dcnt[k] = dcnt.get(k, 0) + op["inc"] * op["ndma"]
                op["dval"] = dcnt[k]
            elif need_sig[i]:
                cnt[op["eng"]] += 1
                op["sig"] = cnt[op["eng"]]
        self.stats = dict(cnt=cnt, dma_keys=len(dcnt), nops=n, maxdma=max(dcnt.values()) if dcnt else 0)
        assert max(cnt.values()) < SEM_LIMIT, cnt
        assert self.stats["maxdma"] < SEM_LIMIT, self.stats
        eng_sem = {e: sems(f"prog_{e}") for e in ENGS if cnt[e] > 0}
        dma_sem = {k: sems(f"dma_{k}") for k in dcnt}

        def run_engine(ename, e):
            seen = {}
            for i, op in enumerate(ops):
                if op["eng"] != ename:
                    continue
                waits = {}
                for d, kind in op["deps"].items():
                    dop = ops[d]
                    if not self._needs_wait(op, dop, kind):
                        continue
                    if dop["dma"] is not None:
                        s, v = dma_sem[dop["dma"]], dop["dval"]
                    else:
                        s, v = eng_sem[dop["eng"]], dop["sig"]
                    key = id(s)
                    if waits.get(key, (None, 0))[1] < v:
                        waits[key] = (s, v)
                for key, (s, v) in waits.items():
                    if seen.get(key, 0) < v:
                        e.wait_ge(s, v)
                        seen[key] = v
                r = op["emit"](e)
                if op["dma"] is not None:
                    insts = r if isinstance(r, (list, tuple)) else [r]
                    assert len(insts) == op["ndma"], (len(insts), op["ndma"], op["dma"])
                    for ins in insts:
                        ins.then_inc(dma_sem[op["dma"]], op["inc"])
                elif need_sig[i]:
                    assert r is not None, f"op {i} on {ename} must return an instruction"
                    r.then_inc(eng_sem[ename], 1)

        @block.tensor
        def _(e):
            run_engine("pe", e)

        @block.scalar
        def _(e):
            run_engine("act", e)

        @block.vector
        def _(e):
            run_engine("dve", e)

        @block.gpsimd
        def _(e):
            run_engine("pool", e)

        @block.sync
        def _(e):
            run_engine("sp", e)


D = 1024
NCH = 34
NLAT = 32
DFF = 2816
ALPHA = 4.0 ** 0.25
LN_EPS = 1e-5
RMS_EPS = 1e-6
SCALE = 128.0 ** -0.5
NKC = 130
GROUPS = [[0, 1, 2, 3], [4, 5, 6, 7]]
PHASES = ["P0", "FA00", "FB00", "G1", "G2", "FA01", "FB01", "FA10", "FB10", "QKV", "ATT", "FA11", "FB11"]


def build(stop_after=None):
    nc = bass.Bass("TRN2", target_bir_lowering=False)

    def din(name, shape, dt=F32):
        return nc.dram_tensor(name, list(shape), dt, kind="ExternalInput").ap()

    def dscr(name, shape, dt):
        return nc.dram_tensor(name, list(shape), dt).ap()

    xin = din("xin", [NCH * 128, D])
    cond = din("cond", [128, 8, 2])
    posrc = din("posrc", [128, NLAT, 2])
    freq2 = din("freq2", [128, 64])
    identd = din("ident", [128, 128])
    w_mod = din("w_mod", [2, D, 9 * D])
    bmod2 = din("bmod2", [2, 2, 9 * D])
    ln_g = din("ln_g", [2, 3, D])
    ln_b = din("ln_b", [2, 3, D])
    ffn_w_in = din("ffn_w_in", [2, 2, D, 2 * DFF])
    ffn_w_out = din("ffn_w_out", [2, 2, DFF, D])
    gm_w_in = din("gm_w_in", [D, 6144])
    glng = din("glng", [128, 24])
    glnb = din("glnb", [128, 24])
    wsT_d = din("wsT", [128, 8, 128])
    gbs = din("gbs", [8 * 128])
    gm_w_out = din("gm_w_out", [3072, D])
    w_qkv = din("w_qkv", [D, 1536])
    qn_d = din("qn", [128])
    kn_d = din("kn", [128])
    w_o_d = din("w_o", [D, D])
    out = nc.dram_tensor("out", [NLAT * 128, D], F32, kind="ExternalOutput").ap()

    xa = dscr("xa", [NCH * 128, D], F32)
    xb = dscr("xb", [NCH * 128, D], F32)
    hT_d = dscr("hT_d", [NCH, 128, 24, 128], BF16)
    qT_d = dscr("qT_d", [NLAT, 128, 1024], BF16)
    kin = [nc.dram_tensor(f"kin{i}", [128, 4096], BF16) for i in range(2)]
    vin = [nc.dram_tensor(f"vin{i}", [128, 4096], BF16) for i in range(2)]
    kout = [nc.dram_tensor(f"kout{i}", [512, 4096], BF16) for i in range(2)]
    vout = [nc.dram_tensor(f"vout{i}", [512, 4096], BF16) for i in range(2)]
    kc_d = dscr("kc_d", [2, 128, 256], BF16)
    vc_d = dscr("vc_d", [256, 256], BF16)
    mrow_d = dscr("mrow_d", [2, 2, 9 * D], F32)

    P = Prog()
    root = ExitStack()

    uid = [0]

    def sbuf(es, name, shape, dt):
        uid[0] += 1
        return es.enter_context(nc.sbuf_tensor(f"{name}_{uid[0]}", list(shape), dt))

    def psum(es, name, shape, dt):
        uid[0] += 1
        return es.enter_context(nc.psum_tensor(f"{name}_{uid[0]}", list(shape), dt))

    ident_f = sbuf(root, "ident_f", [128, 128], F32)
    ident_b = sbuf(root, "ident_b", [128, 128], BF16)
    mcol = sbuf(root, "mcol", [128, 2, 72, 2], F32)
    BIG = sbuf(root, "BIG", [128, 77824], BF16)
    R1 = BIG[:, 0:49152]
    R2 = BIG[:, 49152:77824]

    def DMA(eng, key, out_, in_, reads=(), writes=(), nobar=False):
        return P.add(eng, lambda e: e.dma_start(out=out_, in_=in_), reads=reads, writes=writes, dma=key, nobar=nobar)

    DMA("sp", "c_ident", ident_f[:], identd, writes=["ident_f"])
    P.add("dve", lambda e: e.tensor_copy(out=ident_b[:], in_=ident_f[:]), reads=["ident_f"], writes=["ident_b"])

    def done(ph):
        P.barrier()
        return stop_after == ph

    def load_w1(layer, j, nobar=True):
        w1 = R1[:, 0:8 * 5632].rearrange("p (k f) -> p k f", k=8)
        src = ffn_w_in[layer, j].rearrange("(k p) f -> p k f", p=128)
        pieces = [(0, 2048), (2048, 4096), (4096, 5632)]

        def emit(e):
            r = []
            for kc in range(8):
                for lo, hi in pieces:
                    r.append(e.dma_start(out=w1[:, kc, lo:hi], in_=src[:, kc, lo:hi]))
            return r
        P.add("pool", emit, writes=["R1"], dma="wR1", ndma=24, nobar=nobar)
        return w1

    def load_w2(layer, j, nobar=True):
        w2 = R2[:, 0:22 * 1024].rearrange("p (f d) -> p f d", f=22)
        src = ffn_w_out[layer, j].rearrange("(f p) d -> p f d", p=128)

        def emit(e):
            return [e.dma_start(out=w2[:, f, :], in_=src[:, f, :]) for f in range(22)]
        P.add("pool", emit, writes=["R2"], dma="wR2", ndma=22, nobar=nobar)
        return w2

    def prep_chunk(xc_t, xc_res, trp, trp_res, xmT, xm_res, col0, layer, shift_mod, scale_mod, cnd):
        def pe(e):
            r = None
            for kc in range(8):
                r = e.transpose(out=trp[:, kc, :], in_=xc_t[:, kc * 128:(kc + 1) * 128], identity=ident_f[:])
            return r
        P.add("pe", pe, reads=[xc_res, "ident_f"], writes=[trp_res])
        for kc in range(8):
            P.add("act", lambda e, kc=kc: e.activation(
                out=xmT[:, kc, col0:col0 + 128], in_=trp[:, kc, :], func=AF.Identity,
                scale=mcol[:, layer, scale_mod * 8 + kc, cnd:cnd + 1],
                bias=mcol[:, layer, shift_mod * 8 + kc, cnd:cnd + 1]),
                reads=[trp_res, "mcol"], writes=[xm_res])

    class Tail:
        def __init__(self, es, layer, gate_mod, ln_idx, nslots=2, conds=(0, 1)):
            self.gate = [sbuf(es, f"gate_bc{c}", [128, D], F32) if c in conds else None for c in range(2)]
            self.g_bc = sbuf(es, "g_bc", [128, D], F32)
            self.b_bc = sbuf(es, "b_bc", [128, D], F32)
            self.t1 = [sbuf(es, f"t1_{s}", [128, D], F32) for s in range(nslots)]
            self.st = [sbuf(es, f"st_{s}", [128, 2, 6], F32) for s in range(nslots)]
            self.mv = [sbuf(es, f"mv_{s}", [128, 2], F32) for s in range(nslots)]
            self.rs = [sbuf(es, f"rs_{s}", [128, 2], F32) for s in range(nslots)]
            self.n = nslots
            self.epsc = sbuf(es, "epsc", [128, 1], F32)
            P.add("pool", lambda e: e.memset(self.epsc[:], LN_EPS), writes=["epsc"])
            for c in conds:
                DMA("sp", f"gatebc{c}", self.gate[c][:],
                    mrow_d[layer, c, gate_mod * D:(gate_mod + 1) * D].partition_broadcast(128),
                    reads=["mrow_d"], writes=[f"gate_bc{c}"])
            DMA("sp", "gbc", self.g_bc[:], ln_g[layer, ln_idx].partition_broadcast(128), writes=["g_bc"])
            DMA("sp", "bbc", self.b_bc[:], ln_b[layer, ln_idx].partition_broadcast(128), writes=["b_bc"])

        def piece(self, s, yap, yres, lo, hi, cnd):
            t1 = self.t1[s]
            P.add("dve", lambda e: e.tensor_tensor(out=t1[:, lo:hi], in0=yap, in1=self.gate[cnd][:, lo:hi], op=ALU.mult),
                  reads=[yres, f"gate_bc{cnd}"], writes=[f"t1_{s}"])

        def rest(self, s, xold, xres, dst, dst_res, before_store=None):
            t1, st, mv, rs = self.t1[s], self.st[s], self.mv[s], self.rs[s]
            r = f"t1_{s}"
            P.add("dve", lambda e: e.scalar_tensor_tensor(out=t1[:], in0=xold, scalar=ALPHA, in1=t1[:],
                                                          op0=ALU.mult, op1=ALU.add), reads=[xres, r], writes=[r])

            def stats(e):
                e.bn_stats(out=st[:, 0, :], in_=t1[:, 0:512])
                return e.bn_stats(out=st[:, 1, :], in_=t1[:, 512:1024])
            P.add("dve", stats, reads=[r], writes=[f"st_{s}"])
            P.add("dve", lambda e: e.bn_aggr(out=mv[:], in_=st[:].rearrange("p a b -> p (a b)")),
                  reads=[f"st_{s}"], writes=[f"mv_{s}"])
            P.add("act", lambda e: e.activation(out=rs[:, 0:1], in_=mv[:, 1:2], func=AF.Sqrt, bias=self.epsc[:, 0:1], scale=1.0),
                  reads=[f"mv_{s}", "epsc"], writes=[f"rsa_{s}"])
            P.add("dve", lambda e: e.reciprocal(out=rs[:, 0:1], in_=rs[:, 0:1]), reads=[f"rsa_{s}"], writes=[f"rsa_{s}"])
            P.add("dve", lambda e: e.tensor_scalar(out=rs[:, 1:2], in0=mv[:, 0:1], scalar1=rs[:, 0:1], scalar2=-1.0,
                                                   op0=ALU.mult, op1=ALU.mult), reads=[f"mv_{s}", f"rsa_{s}"], writes=[f"rsb_{s}"])
            P.add("act", lambda e: e.activation(out=t1[:], in_=t1[:], func=AF.Identity, scale=rs[:, 0:1], bias=rs[:, 1:2]),
                  reads=[r, f"rsa_{s}", f"rsb_{s}"], writes=[r])
            P.add("pool", lambda e: e.tensor_tensor(out=t1[:], in0=t1[:], in1=self.g_bc[:], op=ALU.mult),
                  reads=[r, "g_bc"], writes=[r])
            P.add("pool", lambda e: e.tensor_tensor(out=t1[:], in0=t1[:], in1=self.b_bc[:], op=ALU.add),
                  reads=[r, "b_bc"], writes=[r])
            if before_store:
                before_store()
            DMA("sp", f"st_t1_{s}", dst, t1[:], reads=[r], writes=[dst_res])

    with ExitStack() as es:
        scf = sbuf(es, "scf", [128, 8, 2], F32)
        scb = sbuf(es, "scb", [128, 8, 2], BF16)
        mrow_ = R2[0:2, 0:18432].bitcast(F32)
        mrow = [mrow_, mrow_]
        wblk = [sbuf(es, f"wblk{s}", [128, 8, 512], BF16) for s in range(3)]
        brow = [sbuf(es, f"brow{s}", [2, 512], F32) for s in range(3)]
        mps = [psum(es, f"mps{s}", [128, 512], F32) for s in range(2)]
        tps = psum(es, "tps0", [128, 512], F32)
        DMA("sp", "scf", scf[:], cond, writes=["scf"])
        P.add("act", lambda e: e.activation(out=scb[:], in_=scf[:], func=AF.Silu), reads=["scf"], writes=["scb"])
        for layer in range(2):
            for blk in range(18):
                s3, s2 = blk % 3, blk % 2
                DMA("pool", f"wblk{s3}", wblk[s3][:],
                    w_mod[layer, :, blk * 512:(blk + 1) * 512].rearrange("(k p) c -> p k c", p=128),
                    writes=[f"wblk{s3}"])
                DMA("sp", f"brow{s3}", brow[s3][:], bmod2[:, layer, blk * 512:(blk + 1) * 512], writes=[f"brow{s3}"])

                def pe(e, s3=s3, s2=s2):
                    r = None
                    for kc in range(8):
                        r = e.matmul(mps[s2][0:2, :], lhsT=scb[:, kc, :], rhs=wblk[s3][:, kc, :], start=(kc == 0), stop=(kc == 7))
                    return r
                P.add("pe", pe, reads=["scb", f"wblk{s3}"], writes=[f"mps{s2}"])
                P.add("dve", lambda e, s3=s3, s2=s2, blk=blk, layer=layer: e.tensor_tensor(
                    out=mrow[layer][:, blk * 512:(blk + 1) * 512], in0=mps[s2][0:2, :], in1=brow[s3][:], op=ALU.add),
                    reads=[f"mps{s2}", f"brow{s3}"], writes=["R2"])
            for m in (1, 4, 7):
                P.add("dve", lambda e, m=m, layer=layer: e.tensor_scalar(
                    out=mrow[layer][:, m * D:(m + 1) * D], in0=mrow[layer][:, m * D:(m + 1) * D], scalar1=1.0, scalar2=None, op0=ALU.add),
                    reads=["R2"], writes=["R2"])
            for m in (2, 8):
                P.add("dve", lambda e, m=m, layer=layer: e.tensor_scalar(
                    out=mrow[layer][:, m * D:(m + 1) * D], in0=mrow[layer][:, m * D:(m + 1) * D], scalar1=0.5, scalar2=None, op0=ALU.mult),
                    reads=["R2"], writes=["R2"])
            DMA("sp", f"mrowst{layer}", mrow_d[layer], mrow[layer][:], reads=["R2"], writes=["mrow_d"])

            def pe_t(e, layer=layer):
                r = None
                for jj in range(72):
                    r = e.transpose(out=tps[:, 2 * jj:2 * jj + 2], in_=mrow[layer][0:2, jj * 128:(jj + 1) * 128], identity=ident_f[0:2, 0:2])
                return r
            P.add("pe", pe_t, reads=["R2", "ident_f"], writes=["tps0"])
            P.add("dve", lambda e, layer=layer: e.tensor_copy(
                out=mcol[:, layer, :, :].rearrange("p a b -> p (a b)"), in_=tps[:, 0:144]), reads=["tps0"], writes=["mcol"])
        load_w1(0, 0)
        load_w2(0, 0)
        fin = done("P0")

    def phase_FA(layer, j, src, with_ctx, prefetch=None):
        shift_mod, scale_mod = (0, 1) if j == 0 else (6, 7)
        tiles = [(4 * t, 4, 0) for t in range(8)] + ([(32, 2, 1)] if with_ctx else [])
        w1 = R1[:, 0:8 * 5632].rearrange("p (k f) -> p k f", k=8)
        with ExitStack() as es:
            NX = 6
            xc = [sbuf(es, f"xc{s}", [128, D], F32) for s in range(NX)]
            xmT = [sbuf(es, f"xmT{s}", [128, 8, 512], BF16) for s in range(2)]
            sg = [sbuf(es, f"sg{s}", [128, 512], F32) for s in range(2)]
            hto = [sbuf(es, f"hto{s}", [128, 512], BF16) for s in range(4)]
            trp = [psum(es, f"trp{s}", [128, 8, 128], F32) for s in range(2)]
            gu = [psum(es, f"gu{s}", [128, 2, 512], F32) for s in range(2)]
            if prefetch:
                prefetch()
            chunks = [(c0 + ci, cnd) for (c0, n, cnd) in tiles for ci in range(n)]
            nload = [0]

            def load_next():
                if nload[0] < len(chunks):
                    c, _ = chunks[nload[0]]
                    s = nload[0] % NX
                    DMA("sp", f"xc{s}", xc[s][:], src[c * 128:(c + 1) * 128, :], reads=[("x", id(src), c)], writes=[f"xc{s}"])
                    nload[0] += 1
            for _ in range(NX):
                load_next()
            cidx = [0]

            def prep(ti):
                c0, n, cnd = tiles[ti]
                for ci in range(n):
                    k = cidx[0]
                    s = k % NX
                    prep_chunk(xc[s], f"xc{s}", trp[k % 2], f"trp{k % 2}", xmT[ti % 2], f"xmT{ti % 2}", ci * 128,
                               layer, shift_mod, scale_mod, cnd)
                    cidx[0] += 1
                    load_next()
            prep(0)
            for ti, (c0, n, cnd) in enumerate(tiles):
                NT = n * 128
                xm = xmT[ti % 2]
                for f in range(22):
                    b = f % 2

                    def pe(e, f=f, b=b, xm=xm, NT=NT):
                        r = None
                        for half in range(2):
                            col = half * DFF + f * 128
                            for kc in range(8):
                                r = e.matmul(gu[b][:, half, 0:NT], lhsT=w1[:, kc, col:col + 128], rhs=xm[:, kc, 0:NT],
                                             start=(kc == 0), stop=(kc == 7))
                        return r
                    P.add("pe", pe, reads=["R1", f"xmT{ti % 2}"], writes=[f"gu{b}"])
                    P.add("act", lambda e, b=b, NT=NT: e.activation(out=sg[b][:, 0:NT], in_=gu[b][:, 0, 0:NT], func=AF.Silu),
                          reads=[f"gu{b}"], writes=[f"sg{b}"])
                    hs = f % 4
                    P.add("dve", lambda e, b=b, hs=hs, NT=NT: e.tensor_tensor(out=hto[hs][:, 0:NT], in0=sg[b][:, 0:NT],
                                                                              in1=gu[b][:, 1, 0:NT], op=ALU.mult),
                          reads=[f"sg{b}", f"gu{b}"], writes=[f"hto{hs}"])
                    DMA("sp", f"hto{hs}", hT_d[c0:c0 + n, :, f, :].rearrange("c p t -> p c t"),
                        hto[hs][:, 0:NT].rearrange("p (c t) -> p c t", t=128),
                        reads=[f"hto{hs}"], writes=[("hT", c0, f)])
                    if f == 8 and ti + 1 < len(tiles):
                        prep(ti + 1)
            return done(f"FA{layer}{j}")

    def phase_B(name, nf, w2, layer, gate_mod, ln_idx, xsrc, dst, chunk_list, dst_rows, prefetch=None):
        with ExitStack() as es:
            tl = Tail(es, layer, gate_mod, ln_idx)
            xc = [sbuf(es, f"xc{s}", [128, D], F32) for s in range(2)]
            hin = [sbuf(es, f"hin{s}", [128, nf, 128], BF16) for s in range(2)]
            yps = [psum(es, f"yps{s}", [128, D], F32) for s in range(2)]
            if prefetch:
                prefetch()
            nl = [0]

            def load_next():
                if nl[0] < len(chunk_list):
                    c = chunk_list[nl[0]]
                    s = nl[0] % 2
                    DMA("sp", f"xc{s}", xc[s][:], xsrc[c * 128:(c + 1) * 128, :], reads=[("x", id(xsrc), c)], writes=[f"xc{s}"])
                    DMA("sp", f"hin{s}", hin[s][:], hT_d[c, :, 0:nf, :], reads=[("hT", c)], writes=[f"hin{s}"])
                    nl[0] += 1
            for _ in range(2):
                load_next()
            for k, c in enumerate(chunk_list):
                s, b = k % 2, k % 2
                cnd = 0 if c < NLAT else 1
                for half in range(2):
                    def pe(e, s=s, b=b, half=half):
                        r = None
                        for f in range(nf):
                            r = e.matmul(yps[b][:, half * 512:(half + 1) * 512], lhsT=hin[s][:, f, :],
                                         rhs=w2[:, f, half * 512:(half + 1) * 512], start=(f == 0), stop=(f == nf - 1))
                        return r
                    P.add("pe", pe, reads=["R2", f"hin{s}"], writes=[(f"yps{b}", half)])
                    tl.piece(b, yps[b][:, half * 512:(half + 1) * 512], (f"yps{b}", half), half * 512, (half + 1) * 512, cnd)
                r0 = dst_rows(c)
                tl.rest(b, xc[s][:], f"xc{s}", dst[r0:r0 + 128, :], ("x", id(dst), c), before_store=load_next)
            return done(name)


    def gen_loader(name_prefix, nslots, slots, chunk_ids, src, res_prefix):
        st_ = [0]

        def load_next():
            if st_[0] < len(chunk_ids):
                c = chunk_ids[st_[0]]
                s_ = st_[0] % nslots
                DMA("sp", f"{res_prefix}{s_}", slots[s_][:], src[c * 128:(c + 1) * 128, :],
                    reads=[("x", id(src), c)], writes=[f"{res_prefix}{s_}"])
                st_[0] += 1
        return load_next

    def phase_G1(src, prefetch=None):
        layer, shift_mod, scale_mod = 0, 3, 4
        tiles = [(2 * t, 2, 0) for t in range(16)] + [(32, 2, 1)]
        w_in = R1.rearrange("p (k f) -> p k f", k=8)
        with ExitStack() as es:
            NX = 4
            xc = [sbuf(es, "xc", [128, D], F32) for s_ in range(NX)]
            xmT = [sbuf(es, "xmT", [128, 8, 256], BF16) for s_ in range(2)]
            vh = sbuf(es, "vh", [128, 3072], BF16)
            hho = [sbuf(es, "hho", [128, 24, 128], BF16) for s_ in range(2)]
            wsT = sbuf(es, "wsT", [128, 8, 128], BF16)
            bs_bc = sbuf(es, "bs_bc", [128, 8, 128], F32)
            tmp = [sbuf(es, "tmp", [128, 128], F32) for s_ in range(2)]
            gcol = sbuf(es, "gcol", [128, 24], F32)
            bcol = sbuf(es, "bcol", [128, 24], F32)
            ones_b = sbuf(es, "ones_b", [128, 128], BF16)
            st6 = sbuf(es, "st6", [128, 6, 6], F32)
            mv = sbuf(es, "mvg", [128, 2], F32)
            rs = sbuf(es, "rsg", [128, 2], F32)
            epsc = sbuf(es, "epsg", [128, 1], F32)
            uT = [R2[:, i * 6144:(i + 1) * 6144].rearrange("p (f t) -> p f t", f=24) for i in range(2)]
            v_sb = R2[:, 12288:18432].bitcast(F32)
            Bt = R2[:, 18432:24576].bitcast(F32).rearrange("p (f t) -> p f t", f=24)
            trp = psum(es, "trp", [128, 8, 128], F32)
            ups_ = [psum(es, "ups", [128, 512], F32) for s_ in range(2)]
            vps = [psum(es, "vps", [128, 512], F32) for s_ in range(2)]
            sps = [psum(es, "sps", [128, 4, 128], F32) for s_ in range(2)]
            if prefetch:
                prefetch()
            DMA("pool", "wsT", wsT[:], wsT_d, writes=["wsT"])
            DMA("sp", "gcol", gcol[:], glng, writes=["gcol"])
            DMA("sp", "bcol", bcol[:], glnb, writes=["bcol"])
            DMA("sp", "bsbc", bs_bc[:].rearrange("p g t -> p (g t)"), gbs.partition_broadcast(128), writes=["bs_bc"])
            P.add("pool", lambda e: e.memset(ones_b[:], 1.0), writes=["ones_b"])
            P.add("pool", lambda e: e.memset(epsc[:], LN_EPS), writes=["epsg"])
            for g in range(8):
                P.add("pe", lambda e, g=g: e.matmul(sps[g // 4][:, g % 4, :], lhsT=ones_b[:], rhs=wsT[:, g, :], start=True, stop=True),
                      reads=["ones_b", "wsT"], writes=[f"sps{g // 4}"])
            for cc in range(24):
                g = cc // 3
                P.add("dve", lambda e, cc=cc, g=g: e.scalar_tensor_tensor(
                    out=Bt[:, cc, :], in0=sps[g // 4][:, g % 4, :], scalar=bcol[:, cc:cc + 1], in1=bs_bc[:, g, :],
                    op0=ALU.mult, op1=ALU.add), reads=[f"sps{g // 4}", "bcol", "bs_bc"], writes=["Bt"])
            chunks = [c0 + ci for (c0, n, cnd) in tiles for ci in range(n)]
            load_next = gen_loader("xc", NX, xc, chunks, src, "xc")
            for _ in range(NX):
                load_next()
            kidx = [0]

            def prep(ti):
                c0, n, cnd = tiles[ti]
                for ci in range(n):
                    k = kidx[0]
                    prep_chunk(xc[k % NX], f"xc{k % NX}", trp, "trp", xmT[ti % 2], f"xmT{ti % 2}", ci * 128,
                               layer, shift_mod, scale_mod, cnd)
                    kidx[0] += 1
                    load_next()
            prep(0)

            def emit_uT(ti, lo, hi):
                xm_ = xmT[ti % 2]
                uu = uT[ti % 2]
                for cc in range(lo, hi):
                    def pe(e, cc=cc, xm_=xm_):
                        r = None
                        for kc in range(8):
                            r = e.matmul(ups_[cc % 2][:, 0:256], lhsT=w_in[:, kc, cc * 128:(cc + 1) * 128], rhs=xm_[:, kc, :],
                                         start=(kc == 0), stop=(kc == 7))
                        return r
                    P.add("pe", pe, reads=["R1", f"xmT{ti % 2}"], writes=[f"ups{cc % 2}"])
                    P.add("act", lambda e, cc=cc, uu=uu: e.activation(out=uu[:, cc, :], in_=ups_[cc % 2][:, 0:256], func=AF.Gelu),
                          reads=[f"ups{cc % 2}"], writes=[f"uT{ti % 2}"])
            emit_uT(0, 0, 24)
            for ti, (c0, n, cnd) in enumerate(tiles):
                xm = xmT[ti % 2]
                u_ = uT[ti % 2]
                has_next = ti + 1 < len(tiles)
                if has_next:
                    prep(ti + 1)
                for ci in range(n):
                    c = c0 + ci
                    for blk in range(6):
                        def pe(e, blk=blk, xm=xm, ci=ci):
                            r = None
                            for kc in range(8):
                                r = e.matmul(vps[blk % 2][:, :], lhsT=xm[:, kc, ci * 128:(ci + 1) * 128],
                                             rhs=w_in[:, kc, 3072 + blk * 512:3072 + (blk + 1) * 512], start=(kc == 0), stop=(kc == 7))
                            return r
                        P.add("pe", pe, reads=["R1", f"xmT{ti % 2}"], writes=[f"vps{blk % 2}"])
                        P.add("act", lambda e, blk=blk: e.activation(out=v_sb[:, blk * 512:(blk + 1) * 512], in_=vps[blk % 2][:, :], func=AF.Gelu),
                              reads=[f"vps{blk % 2}"], writes=["v_sb"])

                    def stats(e):
                        r = None
                        for q in range(6):
                            r = e.bn_stats(out=st6[:, q, :], in_=v_sb[:, q * 512:(q + 1) * 512])
                        return r
                    P.add("dve", stats, reads=["v_sb"], writes=["st6"])
                    P.add("dve", lambda e: e.bn_aggr(out=mv[:], in_=st6[:].rearrange("p a b -> p (a b)")), reads=["st6"], writes=["mvg"])
                    P.add("act", lambda e: e.activation(out=rs[:, 0:1], in_=mv[:, 1:2], func=AF.Sqrt, bias=epsc[:, 0:1], scale=1.0),
                          reads=["mvg", "epsg"], writes=["rsga"])
                    P.add("dve", lambda e: e.reciprocal(out=rs[:, 0:1], in_=rs[:, 0:1]), reads=["rsga"], writes=["rsga"])
                    P.add("dve", lambda e: e.tensor_scalar(out=rs[:, 1:2], in0=mv[:, 0:1], scalar1=rs[:, 0:1], scalar2=-1.0,
                                                           op0=ALU.mult, op1=ALU.mult), reads=["mvg", "rsga"], writes=["rsgb"])
                    P.add("act", lambda e: e.activation(out=vh[:], in_=v_sb[:], func=AF.Identity, scale=rs[:, 0:1], bias=rs[:, 1:2]),
                          reads=["v_sb", "rsga", "rsgb"], writes=["vh"])
                    if has_next:
                        emit_uT(ti + 1, ci * 12, (ci + 1) * 12)
                    for cg in range(6):
                        def pe(e, cg=cg):
                            r = None
                            for i in range(4):
                                cc = cg * 4 + i
                                r = e.matmul(sps[cg % 2][:, i, :], lhsT=vh[:, cc * 128:(cc + 1) * 128], rhs=wsT[:, cc // 3, :],
                                             start=True, stop=True)
                            return r
                        P.add("pe", pe, reads=["vh", "wsT"], writes=[f"sps{cg % 2}"])
                        for i in range(4):
                            cc = cg * 4 + i
                            P.add("dve", lambda e, cc=cc, cg=cg, i=i: e.scalar_tensor_tensor(
                                out=tmp[cc % 2][:], in0=sps[cg % 2][:, i, :], scalar=gcol[:, cc:cc + 1], in1=Bt[:, cc, :],
                                op0=ALU.mult, op1=ALU.add), reads=[f"sps{cg % 2}", "gcol", "Bt"], writes=[f"tmp{cc % 2}"])
                            P.add("pool", lambda e, cc=cc, c=c, ci=ci, u_=u_: e.tensor_tensor(
                                out=hho[c % 2][:, cc, :], in0=tmp[cc % 2][:], in1=u_[:, cc, ci * 128:(ci + 1) * 128], op=ALU.mult),
                                reads=[f"tmp{cc % 2}", f"uT{ti % 2}"], writes=[f"hho{c % 2}"])
                    DMA("sp", f"hho{c % 2}", hT_d[c], hho[c % 2][:], reads=[f"hho{c % 2}"], writes=[("hT", c)])
            return done("G1")

    def phase_QKV(src, prefetch=None):
        layer, shift_mod, scale_mod = 1, 3, 4
        wq = R1[:, 0:8 * 1536].rearrange("p (k f) -> p k f", k=8)
        with ExitStack() as es:
            NX = 3
            xc = [sbuf(es, "xc", [128, D], F32) for s_ in range(NX)]
            xmT = [sbuf(es, "xmT", [128, 8, 128], BF16) for s_ in range(2)]
            sq = [sbuf(es, "sq", [128, 512], F32) for s_ in range(3)]
            ss = sbuf(es, "ss", [128, 12], F32)
            rst = sbuf(es, "rst", [128, 12], F32)
            qn = sbuf(es, "qnb", [128, 10, 128], F32)
            ra = sbuf(es, "ra", [128, 10, 64], F32)
            rb = sbuf(es, "rb", [128, 10, 64], F32)
            rc = sbuf(es, "rc", [128, 10, 64], F32)
            rd = sbuf(es, "rd", [128, 10, 64], F32)
            qr = sbuf(es, "qr", [128, 10, 128], BF16)
            cosT = R2[:, 12288:16384].bitcast(F32).rearrange("p (c j) -> p c j", c=NLAT)
            sinT = R2[:, 16384:20480].bitcast(F32).rearrange("p (c j) -> p c j", c=NLAT)
            ang = R2[:, 0:4096].bitcast(F32).rearrange("p (c j) -> p c j", c=NLAT)
            angi = R2[:, 4096:8192].bitcast(I32).rearrange("p (c j) -> p c j", c=NLAT)
            ang2 = R2[:, 8192:12288].bitcast(F32).rearrange("p (c j) -> p c j", c=NLAT)
            prc = sbuf(es, "prc", [128, NLAT, 2], F32)
            frq = sbuf(es, "frq", [128, 64], F32)
            gq_bc = sbuf(es, "gq_bc", [128, 128], F32)
            gk_bc = sbuf(es, "gk_bc", [128, 128], F32)
            qTs = [sbuf(es, "qTs", [128, 8, 128], BF16) for s_ in range(2)]
            kTs = [sbuf(es, "kTs", [128, 2, 128], BF16) for s_ in range(2)]
            vs = [sbuf(es, "vs", [128, 256], BF16) for s_ in range(2)]
            trp = psum(es, "trp", [128, 8, 128], F32)
            qkv = [psum(es, "qkv", [128, 512], F32) for s_ in range(3)]
            tq = psum(es, "tq", [128, 8, 128], BF16)
            tk = psum(es, "tk", [128, 8, 128], BF16)
            if prefetch:
                prefetch()

            def emit_w(e):
                srcw = w_qkv.rearrange("(k p) f -> p k f", p=128)
                return [e.dma_start(out=wq[:, kc, :], in_=srcw[:, kc, :]) for kc in range(8)]
            P.add("pool", emit_w, writes=["R1"], dma="wR1", ndma=8)
            DMA("sp", "prc", prc[:], posrc, writes=["prc"])
            DMA("sp", "frq", frq[:], freq2, writes=["frq"])
            DMA("sp", "gqbc", gq_bc[:], qn_d.partition_broadcast(128), writes=["gq_bc"])
            DMA("sp", "gkbc", gk_bc[:], kn_d.partition_broadcast(128), writes=["gk_bc"])
            for hh_ in range(2):
                P.add("dve", lambda e, hh_=hh_: e.tensor_tensor(
                    out=ang[:, :, hh_ * 32:(hh_ + 1) * 32], in0=prc[:, :, hh_:hh_ + 1].broadcast_to([128, NLAT, 32]),
                    in1=frq[:, hh_ * 32:(hh_ + 1) * 32].unsqueeze(1).broadcast_to([128, NLAT, 32]), op=ALU.mult),
                    reads=["prc", "frq"], writes=["ang"])
            P.add("dve", lambda e: e.tensor_scalar(out=ang[:], in0=ang[:], scalar1=1.0 / (2.0 * math.pi), scalar2=None, op0=ALU.mult),
                  reads=["ang"], writes=["ang"])
            P.add("dve", lambda e: e.tensor_copy(out=angi[:], in_=ang[:]), reads=["ang"], writes=["angi"])
            P.add("dve", lambda e: e.tensor_copy(out=ang2[:], in_=angi[:]), reads=["angi"], writes=["ang2"])
            P.add("dve", lambda e: e.tensor_tensor(out=ang[:], in0=ang[:], in1=ang2[:], op=ALU.subtract), reads=["ang", "ang2"], writes=["ang"])
            P.add("dve", lambda e: e.tensor_scalar(out=ang2[:], in0=ang[:], scalar1=-1.0, scalar2=None, op0=ALU.mult), reads=["ang"], writes=["ang2"])
            P.add("dve", lambda e: e.tensor_tensor(out=ang2[:], in0=ang2[:], in1=ang[:], op=ALU.max), reads=["ang", "ang2"], writes=["ang2"])
            P.add("act", lambda e: e.activation(out=sinT[:], in_=ang[:], func=AF.Sin, scale=math.pi), reads=["ang"], writes=["sinT"])
            P.add("dve", lambda e: e.tensor_scalar(out=ang2[:], in0=ang2[:], scalar1=-math.pi, scalar2=math.pi / 2, op0=ALU.mult, op1=ALU.add),
                  reads=["ang2"], writes=["ang2"])
            P.add("act", lambda e: e.activation(out=cosT[:], in_=ang2[:], func=AF.Sin), reads=["ang2"], writes=["cosT"])
            P.add("dve", lambda e: e.tensor_tensor(out=ang[:], in0=sinT[:], in1=sinT[:], op=ALU.mult), reads=["sinT"], writes=["ang"])
            P.add("dve", lambda e: e.scalar_tensor_tensor(out=sinT[:], in0=sinT[:], scalar=2.0, in1=cosT[:], op0=ALU.mult, op1=ALU.mult),
                  reads=["sinT", "cosT", "ang"], writes=["sinT"])
            P.add("dve", lambda e: e.tensor_scalar(out=cosT[:], in0=ang[:], scalar1=-2.0, scalar2=1.0, op0=ALU.mult, op1=ALU.add),
                  reads=["ang", "sinT"], writes=["cosT"])
            chunks = list(range(NCH))
            load_next = gen_loader("xc", NX, xc, chunks, src, "xc")
            for _ in range(NX):
                load_next()
            pending = [None]
            for k, c in enumerate(chunks):
                lat = c < NLAT
                cnd = 0 if lat else 1
                b = k % 2
                prep_chunk(xc[k % NX], f"xc{k % NX}", trp, "trp", xmT[b], f"xmT{b}", 0, layer, shift_mod, scale_mod, cnd)
                load_next()
                blks = (0, 1, 2) if lat else (2,)
                for blk in blks:
                    def pe(e, blk=blk, b=b):
                        r = None
                        for kc in range(8):
                            r = e.matmul(qkv[blk][:, :], lhsT=xmT[b][:, kc, :], rhs=wq[:, kc, blk * 512:(blk + 1) * 512],
                                         start=(kc == 0), stop=(kc == 7))
                        return r
                    P.add("pe", pe, reads=["R1", f"xmT{b}"], writes=[f"qkv{blk}"])
                    P.add("act", lambda e, blk=blk: e.activation(out=sq[blk][:], in_=qkv[blk][:, :], func=AF.Square),
                          reads=[f"qkv{blk}"], writes=[f"sq{blk}"])
                    P.add("dve", lambda e, blk=blk: e.tensor_reduce(out=ss[:, blk * 4:(blk + 1) * 4],
                                                                    in_=sq[blk][:].rearrange("p (h d) -> p h d", h=4), axis=AX.X, op=ALU.add),
                          reads=[f"sq{blk}"], writes=["ss"])
                P.add("dve", lambda e: e.tensor_scalar(out=rst[:], in0=ss[:], scalar1=1.0 / 128.0, scalar2=RMS_EPS, op0=ALU.mult, op1=ALU.add),
                      reads=["ss"], writes=["rst"])
                P.add("act", lambda e: e.activation(out=rst[:], in_=rst[:], func=AF.Sqrt), reads=["rst"], writes=["rst"])
                P.add("dve", lambda e: e.reciprocal(out=rst[:], in_=rst[:]), reads=["rst"], writes=["rst"])
                if pending[0] is not None:
                    pending[0]()
                    pending[0] = None
                heads = list(range(10)) if lat else [8, 9]
                for h in heads:
                    blk, off = (h // 4, (h % 4) * 128) if h < 8 else (2, (h - 8) * 128)
                    gb = gq_bc if h < 8 else gk_bc
                    P.add("dve", lambda e, h=h, blk=blk, off=off, gb=gb: e.scalar_tensor_tensor(
                        out=qn[:, h, :], in0=qkv[blk][:, off:off + 128], scalar=rst[:, (blk * 4 + (off // 128)):(blk * 4 + (off // 128)) + 1],
                        in1=gb[:], op0=ALU.mult, op1=ALU.mult), reads=[f"qkv{blk}", "rst", "gq_bc", "gk_bc"], writes=["qn"])
                if lat:
                    x1, x2 = qn[:, :, 0:64], qn[:, :, 64:128]
                    cb = cosT[:, c, :].unsqueeze(1).broadcast_to([128, 10, 64])
                    sb_ = sinT[:, c, :].unsqueeze(1).broadcast_to([128, 10, 64])
                    P.add("pool", lambda e, x1=x1, cb=cb: e.tensor_tensor(out=ra[:], in0=x1, in1=cb, op=ALU.mult), reads=["qn", "cosT"], writes=["ra"])
                    P.add("pool", lambda e, x2=x2, sb_=sb_: e.tensor_tensor(out=rb[:], in0=x2, in1=sb_, op=ALU.mult), reads=["qn", "sinT"], writes=["rb"])
                    P.add("dve", lambda e: e.tensor_tensor(out=qr[:, :, 0:64], in0=ra[:], in1=rb[:], op=ALU.subtract), reads=["ra", "rb"], writes=["qr"])
                    P.add("pool", lambda e, x2=x2, cb=cb: e.tensor_tensor(out=rc[:], in0=x2, in1=cb, op=ALU.mult), reads=["qn", "cosT"], writes=["rc"])
                    P.add("pool", lambda e, x1=x1, sb_=sb_: e.tensor_tensor(out=rd[:], in0=x1, in1=sb_, op=ALU.mult), reads=["qn", "sinT"], writes=["rd"])
                    P.add("dve", lambda e: e.tensor_tensor(out=qr[:, :, 64:128], in0=rc[:], in1=rd[:], op=ALU.add), reads=["rc", "rd"], writes=["qr"])
                else:
                    P.add("dve", lambda e: e.tensor_copy(out=qr[:, 8:10, :], in_=qn[:, 8:10, :]), reads=["qn"], writes=["qr"])

                P.add("act", lambda e, b=b: e.activation(out=vs[b][:], in_=qkv[2][:, 256:512], func=AF.Copy), reads=["qkv2"], writes=[f"vs{b}"])

                def stage_b(c=c, lat=lat, b=b):
                  def pe_t(e, lat=lat):
                      r = None
                      if lat:
                          for h in range(8):
                              r = e.transpose(out=tq[:, h, :], in_=qr[:, h, :], identity=ident_b[:])
                      for kv in range(2):
                          r = e.transpose(out=tk[:, kv, :], in_=qr[:, 8 + kv, :], identity=ident_b[:])
                      return r
                  P.add("pe", pe_t, reads=["qr", "ident_b"], writes=["tq", "tk"])
                  if lat:
                      P.add("act", lambda e, b=b: e.activation(out=qTs[b][:], in_=tq[:], func=AF.Copy), reads=["tq"], writes=[f"qTs{b}"])
                      DMA("sp", f"qTs{b}", qT_d[c].rearrange("p (h t) -> p h t", h=8), qTs[b][:], reads=[f"qTs{b}"], writes=[("qT", c)])
                  P.add("dve", lambda e, b=b: e.tensor_copy(out=kTs[b][:], in_=tk[:, 0:2, :]), reads=["tk"], writes=[f"kTs{b}"])
                  if lat:
                      def st_k(e, b=b, c=c):
                          return [e.dma_start(out=kin[kv].ap()[:, c * 128:(c + 1) * 128], in_=kTs[b][:, kv, :]) for kv in range(2)]
                      P.add("sp", st_k, reads=[f"kTs{b}"], writes=["kin"], dma=f"kTs{b}", ndma=2)
                      hv, j16 = c // 16, c % 16
                      DMA("sp", f"vs{b}", vin[hv].ap()[j16 * 8:(j16 + 1) * 8, :].rearrange("a (b c) -> (a b) c", c=256), vs[b][:],
                          reads=[f"vs{b}"], writes=["vin"])
                  else:
                      cc_ = c - NLAT

                      def st_k(e, b=b, cc_=cc_):
                          return [e.dma_start(out=kc_d[kv, :, cc_ * 128:(cc_ + 1) * 128], in_=kTs[b][:, kv, :]) for kv in range(2)]
                      P.add("sp", st_k, reads=[f"kTs{b}"], writes=["kc_d"], dma=f"kTs{b}", ndma=2)
                      DMA("sp", f"vs{b}", vc_d[cc_ * 128:(cc_ + 1) * 128, :], vs[b][:], reads=[f"vs{b}"], writes=["vc_d"])
                pending[0] = stage_b
            pending[0]()
            P.barrier()
            for i in range(2):
                P.add("pool", lambda e, i=i: e.collective_compute("AllGather", ALU.bypass, replica_groups=GROUPS,
                                                                   ins=[kin[i].ap().opt()], outs=[kout[i].ap().opt()]),
                      reads=["kin"], writes=[("kout", i)], dma=f"cck{i}", inc=1)
                P.add("pool", lambda e, i=i: e.collective_compute("AllGather", ALU.bypass, replica_groups=GROUPS,
                                                                   ins=[vin[i].ap().opt()], outs=[vout[i].ap().opt()]),
                      reads=["vin"], writes=[("vout", i)], dma=f"ccv{i}", inc=1)
            return done("QKV")

    def phase_ATT(xsrc, dst):
        layer = 1
        KT = BIG[:, 0:33280].rearrange("p (k t) -> p k t", k=2)
        V = BIG[:, 33280:66560].rearrange("p (c k d) -> p c k d", c=NKC, k=2)
        wo = BIG[:, 66560:74752].rearrange("p (h d) -> p h d", h=8)
        NP = NKC // 2
        with ExitStack() as es:
            tl = Tail(es, layer, 5, 1, nslots=1, conds=(0,))
            xc = sbuf(es, "xc", [128, D], F32)
            qTt = [sbuf(es, "qTt", [128, 8, 128], BF16) for s_ in range(2)]
            NPT = 5
            pt = [sbuf(es, "pt", [128, 1024], BF16) for s_ in range(NPT)]
            acc = {"dve": sbuf(es, "accD", [128, 1024], F32), "pool": sbuf(es, "accP", [128, 1024], F32)}
            accb = sbuf(es, "accb", [128, 2048], BF16)
            ones_b = sbuf(es, "ones_b", [128, 128], BF16)
            rinv = sbuf(es, "rinv", [128, 512], F32)
            oT = sbuf(es, "oT", [128, 8, 128], BF16)
            spsm = [psum(es, "spsm", [128, 1024], F32) for s_ in range(2)]
            OT = [psum(es, "OT", [128, 512], F32) for s_ in range(2)]
            sump = psum(es, "sump", [128, 512], F32)
            yps = psum(es, "ypsa", [128, 512], F32)
            P.add("pool", lambda e: e.memset(ones_b[:], 1.0), writes=["ones_b"])
            vres = [("Vld", hv, rk) for hv in range(2) for rk in range(4)] + ["Vc"]
            for kv in range(2):
                def ld_k(e, kv=kv):
                    r = [e.dma_start(out=KT[:, kv, rk * 4096:(rk + 1) * 4096], in_=kout[kv].ap()[rk * 128:(rk + 1) * 128, :]) for rk in range(4)]
                    r.append(e.dma_start(out=KT[:, kv, 16384:16640], in_=kc_d[kv]))
                    return r
                P.add("sp", ld_k, reads=[("kout", kv), "kc_d", "R1"], writes=[("Kld", kv)], dma=f"ldk{kv}", ndma=5)
            for hv in range(2):
                for rk in range(4):
                    def ld_v(e, hv=hv, rk=rk):
                        srcv = vout[hv].ap()[rk * 128:(rk + 1) * 128, :].rearrange("(j a) (b k d) -> (a b) j k d", a=8, b=16, k=2)
                        c0 = rk * 32 + hv * 16
                        r = []
                        for kv in range(2):
                            for jh in range(2):
                                r.append(e.dma_start(out=V[:, c0 + jh * 8:c0 + (jh + 1) * 8, kv, :], in_=srcv[:, jh * 8:(jh + 1) * 8, kv, :]))
                        return r
                    P.add("sp", ld_v, reads=[("vout", hv), "R1", "R2"], writes=[("Vld", hv, rk)], dma=f"ldv{hv}{rk}", ndma=4)

            def ld_vc(e):
                srcv = vc_d.rearrange("(j p) (k d) -> p j k d", p=128, k=2)
                return [e.dma_start(out=V[:, 128:130, kv, :], in_=srcv[:, :, kv, :]) for kv in range(2)]
            P.add("sp", ld_vc, reads=["vc_d", "R2"], writes=["Vc"], dma="ldvc", ndma=2)

            def emit_wo(e):
                srcw = w_o_d.rearrange("(h p) d -> p h d", p=128)
                return [e.dma_start(out=wo[:, h, :], in_=srcw[:, h, :]) for h in range(8)]
            P.add("pool", emit_wo, reads=["R2"], writes=["wo"], dma="wo", ndma=8)
            P.add("pe", lambda e: None, reads=vres + [("Kld", 0), ("Kld", 1)], writes=["KVready"])
            qts = list(range(NLAT))
            nl = [0]

            def load_q():
                if nl[0] < len(qts):
                    c = qts[nl[0]]
                    s_ = nl[0] % 2
                    DMA("sp", f"qTt{s_}", qTt[s_][:].rearrange("p h t -> p (h t)"), qT_d[c], reads=[("qT", c)], writes=[f"qTt{s_}"])
                    nl[0] += 1
            load_q()
            load_q()
            first = {"dve": True, "pool": True, "pe": True}
            SUMENG = ["dve", "pool", "pe", "dve", "pe", "pool", "dve", "pe"]
            for qi, c in enumerate(qts):
                s_ = qi % 2
                DMA("sp", "xc0", xc[:], xsrc[c * 128:(c + 1) * 128, :], reads=[("x", id(xsrc), c)], writes=["xc0"])
                for kv in range(2):
                    g = qi * 2 + kv
                    ot = OT[g % 2]
                    otres = f"OT{g % 2}"
                    rhs = qTt[s_][:, kv * 4:(kv + 1) * 4, :].rearrange("p h t -> p (h t)")
                    first["dve"] = True
                    first["pool"] = True
                    first["pe"] = True

                    def S(p, kv=kv, rhs=rhs):
                        sp_ = spsm[p % 2]

                        def pe(e):
                            e.matmul(sp_[:, 0:512], lhsT=KT[:, kv, (2 * p) * 128:(2 * p + 1) * 128], rhs=rhs, start=True, stop=True)
                            return e.matmul(sp_[:, 512:1024], lhsT=KT[:, kv, (2 * p + 1) * 128:(2 * p + 2) * 128], rhs=rhs, start=True, stop=True)
                        P.add("pe", pe, reads=["KVready", f"qTt{s_}"], writes=[f"spsm{p % 2}"])
                        P.add("act", lambda e: e.activation(out=pt[p % NPT][:], in_=sp_[:, :], func=AF.Exp, scale=SCALE),
                              reads=[f"spsm{p % 2}"], writes=[f"pt{p % NPT}"])

                    def PV(p, kv=kv, ot=ot, otres=otres):
                        def pe(e):
                            e.matmul(ot[:, :], lhsT=V[:, 2 * p, kv, :], rhs=pt[p % NPT][:, 0:512], start=(p == 0), stop=False)
                            return e.matmul(ot[:, :], lhsT=V[:, 2 * p + 1, kv, :], rhs=pt[p % NPT][:, 512:1024], start=False, stop=(p == NP - 1))
                        P.add("pe", pe, reads=[f"pt{p % NPT}", "KVready"], writes=[otres])
                        en = SUMENG[p % 8]
                        if en == "pe":
                            st0 = first["pe"]
                            first["pe"] = False

                            def pes(e):
                                e.matmul(sump[:, :], lhsT=ones_b[:], rhs=pt[p % NPT][:, 0:512], start=st0, stop=False)
                                return e.matmul(sump[:, :], lhsT=ones_b[:], rhs=pt[p % NPT][:, 512:1024], start=False, stop=False)
                            P.add("pe", pes, reads=[f"pt{p % NPT}", "ones_b"], writes=["sump"])
                            return
                        a_ = acc[en]
                        if first[en]:
                            first[en] = False
                            P.add(en, lambda e: e.tensor_copy(out=a_[:], in_=pt[p % NPT][:]), reads=[f"pt{p % NPT}"], writes=[f"acc_{en}"])
                        else:
                            P.add(en, lambda e: e.tensor_tensor(out=a_[:], in0=a_[:], in1=pt[p % NPT][:], op=ALU.add),
                                  reads=[f"pt{p % NPT}", f"acc_{en}"], writes=[f"acc_{en}"])
                    S(0)
                    for p in range(NP):
                        if p + 1 < NP:
                            S(p + 1)
                        PV(p)
                    P.add("dve", lambda e: e.tensor_copy(out=accb[:, 0:1024], in_=acc["dve"][:]), reads=["acc_dve"], writes=["accbD"])
                    P.add("pool", lambda e: e.tensor_copy(out=accb[:, 1024:2048], in_=acc["pool"][:]), reads=["acc_pool"], writes=["accbP"])

                    def pe_sum(e):
                        r = None
                        for i in range(4):
                            r = e.matmul(sump[:, :], lhsT=ones_b[:], rhs=accb[:, i * 512:(i + 1) * 512], start=False, stop=(i == 3))
                        return r
                    P.add("pe", pe_sum, reads=["ones_b", "accbD", "accbP"], writes=["sump"])
                    P.add("dve", lambda e: e.reciprocal(out=rinv[:], in_=sump[:, :]), reads=["sump"], writes=["rinv"])
                    P.add("dve", lambda e, kv=kv, ot=ot: e.tensor_tensor(
                        out=oT[:, kv * 4:(kv + 1) * 4, :].rearrange("p h t -> p (h t)"), in0=ot[:, :], in1=rinv[:], op=ALU.mult),
                        reads=[otres, "rinv"], writes=["oT"])
                for half in range(2):
                    def pe(e, half=half):
                        r = None
                        for h in range(8):
                            r = e.matmul(yps[:, :], lhsT=oT[:, h, :], rhs=wo[:, h, half * 512:(half + 1) * 512], start=(h == 0), stop=(h == 7))
                        return r
                    P.add("pe", pe, reads=["oT", "wo"], writes=["ypsa"])
                    tl.piece(0, yps[:, :], "ypsa", half * 512, (half + 1) * 512, 0)
                tl.rest(0, xc[:], "xc0", dst[c * 128:(c + 1) * 128, :], ("x", id(dst), c))
                load_q()
            return done("ATT")

    w2v = R2[:, 0:22 * 1024].rearrange("p (f d) -> p f d", f=22)
    gwo = R2[:, 0:24 * 1024].rearrange("p (f d) -> p f d", f=24)

    def load_gm_in():
        srcw = gm_w_in.rearrange("(k p) f -> p k f", p=128)
        w_in = R1.rearrange("p (k f) -> p k f", k=8)

        def emit(e):
            return [e.dma_start(out=w_in[:, kc, j * 2048:(j + 1) * 2048], in_=srcw[:, kc, j * 2048:(j + 1) * 2048]) for kc in range(8) for j in range(3)]
        P.add("pool", emit, writes=["R1"], dma="wR1", ndma=24, nobar=True)

    def load_gm_out():
        srcw = gm_w_out.rearrange("(f p) d -> p f d", p=128)

        def emit(e):
            return [e.dma_start(out=gwo[:, f, :], in_=srcw[:, f, :]) for f in range(24)]
        P.add("pool", emit, writes=["R2"], dma="wR2", ndma=24, nobar=True)

    allc = list(range(NCH))
    latc = list(range(NLAT))
    rowf = lambda c: c * 128
    stop = fin
    cur = None
    sched = [
        ("FA00", lambda: phase_FA(0, 0, xin, True)),
        ("FB00", lambda: phase_B("FB00", 22, w2v, 0, 2, 0, xin, xa, allc, rowf, prefetch=load_gm_in)),
        ("G1", lambda: phase_G1(xa)),
        ("G2", lambda: phase_B("G2", 24, gwo, 0, 5, 1, xa, xb, allc, rowf, prefetch=lambda: (load_gm_out(), load_w1(0, 1)))),
        ("FA01", lambda: phase_FA(0, 1, xb, True, prefetch=lambda: load_w2(0, 1))),
        ("FB01", lambda: phase_B("FB01", 22, w2v, 0, 8, 2, xb, xa, allc, rowf, prefetch=lambda: load_w1(1, 0))),
        ("FA10", lambda: phase_FA(1, 0, xa, True, prefetch=lambda: load_w2(1, 0))),
        ("FB10", lambda: phase_B("FB10", 22, w2v, 1, 2, 0, xa, xb, allc, rowf)),
        ("QKV", lambda: phase_QKV(xb)),
        ("ATT", lambda: phase_ATT(xb, xa)),
        ("FA11", lambda: phase_FA(1, 1, xa, False, prefetch=lambda: (load_w1(1, 1, nobar=False), load_w2(1, 1, nobar=False)))),
        ("FB11", lambda: phase_B("FB11", 22, w2v, 1, 8, 2, xa, out, latc, rowf)),
    ]
    streams = {"FB00": xa, "G2": xb, "FB01": xa, "FB10": xb, "ATT": xa}
    for name, fn in sched:
        if stop:
            break
        stop = fn()
        if name in streams:
            cur = streams[name]
        elif name != "FB11":
            cur = None if name in ("FA00",) else cur

    if stop_after == "P0":
        DMA("sp", "dbgm", out[0:36, :].rearrange("(a r) c -> a (r c)", a=4), mrow_d.rearrange("l c n -> (l c) n"), reads=["mrow_d"], writes=[("out", 0)])
        DMA("sp", "dbgc", out[128:256, 0:288], mcol[:].rearrange("p l a b -> p (l a b)"), reads=["mcol"], writes=[("out", 1)])
    if stop_after == "FA00":
        DMA("pool", "dbgh", out[0:384, :].rearrange("(p r) c -> p (r c)", r=3), hT_d[0].rearrange("p f t -> p (f t)"), writes=[("out", 0)])
    if stop and cur is not None:
        for i in range(8):
            DMA("sp", f"dbg{i}", out[i * 512:(i + 1) * 512, :], cur[i * 512:(i + 1) * 512, :], writes=[("out", i)])
    P.add("sp", lambda e: None, reads=[("out", i) for i in range(8)] + [("x", id(out), c) for c in range(NLAT)])

    with nc.Block() as block:
        P.emit_all(nc, block, lambda name: root.enter_context(nc.semaphore(name)))
    root.close()
    return nc, P


def _prep_inputs(inp):
    f = lambda a: np.ascontiguousarray(np.asarray(a, dtype=np.float32))
    x, c, ctx, c_ctx = f(inp["x"]), f(inp["c"]), f(inp["ctx"]), f(inp["c_ctx"])
    shared = dict(
        ident=np.eye(128, dtype=np.float32),
        w_mod=f(inp["w_mod"]),
        bmod2=f(np.broadcast_to(f(inp["b_mod"])[None], (2, 2, 9216))),
        ln_g=f(inp["ln_g"]), ln_b=f(inp["ln_b"]),
        ffn_w_in=f(inp["ffn_w_in"]), ffn_w_out=f(inp["ffn_w_out"]),
        gm_w_in=f(inp["gmlp_w_in"][0]),
        glng=f(f(inp["gmlp_ln_g"])[0].reshape(24, 128).T),
        glnb=f(f(inp["gmlp_ln_b"])[0].reshape(24, 128).T),
        wsT=f(f(inp["gmlp_w_s"])[0].transpose(2, 0, 1)),
        gbs=f(f(inp["gmlp_b_s"])[0].reshape(-1)),
        gm_w_out=f(inp["gmlp_w_out"][0]),
        w_qkv=f(inp["attn_w_qkv"][0]),
        qn=f(inp["attn_q_norm"][0]), kn=f(inp["attn_k_norm"][0]),
        w_o=f(inp["attn_w_o"][0]),
    )
    quarter = 32
    fr = (np.float32(10000.0) ** (-np.arange(quarter, dtype=np.float32) / np.float32(quarter))).astype(np.float32)
    shared["freq2"] = f(np.broadcast_to(np.concatenate([fr, fr])[None], (128, 64)))
    in_maps = []
    for i in range(8):
        b, j = i // 4, i % 4
        m = dict(shared)
        m["xin"] = f(np.concatenate([x[b, j * 4096:(j + 1) * 4096], ctx[b]], axis=0))
        cd = np.stack([c[b], c_ctx], axis=-1)
        m["cond"] = f(cd.reshape(8, 128, 2).transpose(1, 0, 2))
        t = j * 4096 + np.arange(4096)
        rc = np.stack([t // 64, t % 64], axis=-1).astype(np.float32)
        m["posrc"] = f(rc.reshape(32, 128, 2).transpose(1, 0, 2))
        in_maps.append(m)
    return in_maps


_CACHE = {}


def run(inp, stop_after=None, trace=False):
    key = stop_after
    if key not in _CACHE:
        _CACHE[key] = build(stop_after)
    nc, P = _CACHE[key]
    in_maps = _prep_inputs(inp)
    res = run_bass_kernel_spmd(nc, in_maps, core_ids=list(range(8)), trace=trace)
    outs = [np.asarray(r["out"], dtype=np.float32) for r in res.results]
    full = np.stack([np.concatenate(outs[0:4], axis=0), np.concatenate(outs[4:8], axis=0)], axis=0)
    return full, res


def kernel(**inputs):
    full, _ = run(inputs)
    return full
```

```python
from contextlib import ExitStack
import math
import os
import numpy as np
G1MODE = int(os.environ.get('G1MODE', '9'))
G1GELU = int(os.environ.get('G1GELU', '1'))
import concourse.bass as bass
import concourse.mybir as mybir
from concourse.bass_utils import run_bass_kernel_spmd

F32 = mybir.dt.float32
BF16 = mybir.dt.bfloat16
I32 = mybir.dt.int32
AF = mybir.ActivationFunctionType
ALU = mybir.AluOpType
AX = mybir.AxisListType

ENGS = ["pe", "act", "dve", "pool", "sp"]
SEM_LIMIT = 60000


class Prog:
    def __init__(self):
        self.ops = []
        self.last_w = {}
        self.readers = {}
        self.bar_from = 0

    def add(self, eng, emit, reads=(), writes=(), dma=None, ndma=1, inc=16, nobar=False):
        i = len(self.ops)
        deps = {}
        for r in reads:
            d = self.last_w.get(r)
            if d is not None:
                deps[d] = "raw"
        for w in writes:
            d = self.last_w.get(w)
            if d is not None:
                deps.setdefault(d, "waw")
            for rd in self.readers.get(w, ()):
                if rd != i:
                    deps.setdefault(rd, "war")
        for r in reads:
            self.readers.setdefault(r, []).append(i)
        for w in writes:
            self.last_w[w] = i
            self.readers[w] = []
        self.ops.append(dict(eng=eng, emit=emit, deps=deps, dma=dma, ndma=ndma, inc=inc, nobar=nobar))
        return i

    def barrier(self):
        lo, hi = self.bar_from, len(self.ops)
        last = {}
        for i in range(lo, hi):
            op = self.ops[i]
            if op["nobar"]:
                continue
            if op["dma"] is not None:
                last[("dma", op["dma"])] = i
            else:
                last[("eng", op["eng"])] = i
        deps = {i: "raw" for i in last.values()}
        for e in ENGS:
            self.ops.append(dict(eng=e, emit=lambda e_: None, deps=dict(deps), dma=None, ndma=1, inc=16,
                                 nobar=True, isbar=True))
        self.bar_from = len(self.ops)

    def _needs_wait(self, op, dop, kind):
        if dop["dma"] is not None:
            return True
        if dop["eng"] != op["eng"]:
            return True
        if op["dma"] is not None or op.get("isbar"):
            return True
        if op["eng"] == "pe":
            return False
        return kind == "raw"

    def emit_all(self, nc, block, sems):
        ops = self.ops
        n = len(ops)
        need_sig = [False] * n
        for op in ops:
            for d, kind in op["deps"].items():
                dop = ops[d]
                if dop["dma"] is None and self._needs_wait(op, dop, kind):
                    need_sig[d] = True
        cnt = {e: 0 for e in ENGS}
        dcnt = {}
        for i, op in enumerate(ops):
            if op["dma"] is not None:
                k = op["dma"]
                dcnt[k] = dcnt.get(k, 0) + op["inc"] * op["ndma"]
                op["dval"] = dcnt[k]
            elif need_sig[i]:
                cnt[op["eng"]] += 1
                op["sig"] = cnt[op["eng"]]
        self.stats = dict(cnt=cnt, dma_keys=len(dcnt), nops=n, maxdma=max(dcnt.values()) if dcnt else 0)
        assert max(cnt.values()) < SEM_LIMIT, cnt
        assert self.stats["maxdma"] < SEM_LIMIT, self.stats
        eng_sem = {e: sems(f"prog_{e}") for e in ENGS if cnt[e] > 0}
        dma_sem = {k: sems(f"dma_{k}") for k in dcnt}

        def run_engine(ename, e):
            seen = {}
            for i, op in enumerate(ops):
                if op["eng"] != ename:
                    continue
                waits = {}
                for d, kind in op["deps"].items():
                    dop = ops[d]
                    if not self._needs_wait(op, dop, kind):
                        continue
                    if dop["dma"] is not None:
                        s, v = dma_sem[dop["dma"]], dop["dval"]
                    else:
                        s, v = eng_sem[dop["eng"]], dop["sig"]
                    key = id(s)
                    if waits.get(key, (None, 0))[1] < v:
                        waits[key] = (s, v)
                for key, (s, v) in waits.items():
                    if seen.get(key, 0) < v:
                        e.wait_ge(s, v)
                        seen[key] = v
                r = op["emit"](e)
                if op["dma"] is not None:
                    insts = r if isinstance(r, (list, tuple)) else [r]
                    assert len(insts) == op["ndma"], (len(insts), op["ndma"], op["dma"])
                    for ins in insts:
                        ins.then_inc(dma_sem[op["dma"]], op["inc"])
                elif need_sig[i]:
                    assert r is not None, f"op {i} on {ename} must return an instruction"
                    r.then_inc(eng_sem[ename], 1)

        @block.tensor
        def _(e):
            run_engine("pe", e)

        @block.scalar
        def _(e):
            run_engine("act", e)

        @block.vector
        def _(e):
            run_engine("dve", e)

        @block.gpsimd
        def _(e):
            run_engine("pool", e)

        @block.sync
        def _(e):
            run_engine("sp", e)


D = 1024
NCH = 34
NLAT = 32
DFF = 2816
ALPHA = 4.0 ** 0.25
LN_EPS = 1e-5
RMS_EPS = 1e-6
SCALE = 128.0 ** -0.5
NKC = 130
GROUPS = [[0, 1, 2, 3], [4, 5, 6, 7]]
PHASES = ["P0", "FA00", "FB00", "G1", "G2", "FA01", "FB01", "FA10", "FB10", "QKV", "ATT", "FA11", "FB11"]


def build(stop_after=None):
    nc = bass.Bass("TRN2", target_bir_lowering=False)

    def din(name, shape, dt=F32):
        return nc.dram_tensor(name, list(shape), dt, kind="ExternalInput").ap()

    def dscr(name, shape, dt):
        return nc.dram_tensor(name, list(shape), dt).ap()

    xin = din("xin", [NCH * 128, D])
    cond = din("cond", [128, 8, 2])
    posrc = din("posrc", [128, NLAT, 2])
    freq2 = din("freq2", [128, 64])
    identd = din("ident", [128, 128])
    w_mod = din("w_mod", [2, D, 9 * D])
    bmod2 = din("bmod2", [2, 2, 9 * D])
    ln_g = din("ln_g", [2, 3, D])
    ln_b = din("ln_b", [2, 3, D])
    ffn_w_in = din("ffn_w_in", [2, 2, D, 2 * DFF])
    ffn_w_out = din("ffn_w_out", [2, 2, DFF, D])
    gm_w_in = din("gm_w_in", [D, 6144])
    glng = din("glng", [128, 24])
    glnb = din("glnb", [128, 24])
    wsT_d = din("wsT", [128, 8, 128])
    gbs = din("gbs", [8 * 128])
    gm_w_out = din("gm_w_out", [3072, D])
    w_qkv = din("w_qkv", [D, 1536])
    qn_d = din("qn", [128])
    kn_d = din("kn", [128])
    w_o_d = din("w_o", [D, D])
    out = nc.dram_tensor("out", [NLAT * 128, D], F32, kind="ExternalOutput").ap()

    xa = dscr("xa", [NCH * 128, D], F32)
    xb = dscr("xb", [NCH * 128, D], F32)
    hT_d = dscr("hT_d", [NCH, 128, 24, 128], BF16)
    qT_d = dscr("qT_d", [NLAT, 128, 1024], BF16)
    kin = [nc.dram_tensor(f"kin{i}", [128, 4096], BF16) for i in range(2)]
    vin = [nc.dram_tensor(f"vin{i}", [128, 4096], BF16) for i in range(2)]
    kout = [nc.dram_tensor(f"kout{i}", [512, 4096], BF16) for i in range(2)]
    vout = [nc.dram_tensor(f"vout{i}", [512, 4096], BF16) for i in range(2)]
    kc_d = dscr("kc_d", [2, 128, 256], BF16)
    vc_d = dscr("vc_d", [256, 256], BF16)
    mrow_d = dscr("mrow_d", [2, 2, 9 * D], F32)

    P = Prog()
    root = ExitStack()

    uid = [0]

    def sbuf(es, name, shape, dt):
        uid[0] += 1
        return es.enter_context(nc.sbuf_tensor(f"{name}_{uid[0]}", list(shape), dt))

    def psum(es, name, shape, dt):
        uid[0] += 1
        return es.enter_context(nc.psum_tensor(f"{name}_{uid[0]}", list(shape), dt))

    ident_f = sbuf(root, "ident_f", [128, 128], F32)
    ident_b = sbuf(root, "ident_b", [128, 128], BF16)
    mcol = sbuf(root, "mcol", [128, 2, 72, 2], F32)
    BIG = sbuf(root, "BIG", [128, 77824], BF16)
    R1 = BIG[:, 0:49152]
    R2 = BIG[:, 49152:77824]

    def DMA(eng, key, out_, in_, reads=(), writes=(), nobar=False):
        return P.add(eng, lambda e: e.dma_start(out=out_, in_=in_), reads=reads, writes=writes, dma=key, nobar=nobar)

    DMA("sp", "c_ident", ident_f[:], identd, writes=["ident_f"])
    P.add("dve", lambda e: e.tensor_copy(out=ident_b[:], in_=ident_f[:]), reads=["ident_f"], writes=["ident_b"])

    def done(ph):
        P.barrier()
        return stop_after == ph

    def load_w1(layer, j, nobar=True):
        w1 = R1[:, 0:8 * 5632].rearrange("p (k f) -> p k f", k=8)
        src = ffn_w_in[layer, j].rearrange("(k p) f -> p k f", p=128)
        pieces = [(0, 2048), (2048, 4096), (4096, 5632)]

        def emit(e):
            r = []
            for kc in range(8):
                for lo, hi in pieces:
                    r.append(e.dma_start(out=w1[:, kc, lo:hi], in_=src[:, kc, lo:hi]))
            return r
        P.add("pool", emit, writes=["R1"], dma="wR1", ndma=24, nobar=nobar)
        return w1

    def load_w2(layer, j, nobar=True):
        w2 = R2[:, 0:22 * 1024].rearrange("p (f d) -> p f d", f=22)
        src = ffn_w_out[layer, j].rearrange("(f p) d -> p f d", p=128)

        def emit(e):
            return [e.dma_start(out=w2[:, f, :], in_=src[:, f, :]) for f in range(22)]
        P.add("pool", emit, writes=["R2"], dma="wR2", ndma=22, nobar=nobar)
        return w2

    def prep_chunk(xc_t, xc_res, trp, trp_res, xmT, xm_res, col0, layer, shift_mod, scale_mod, cnd):
        def pe(e):
            r = None
            for kc in range(8):
                r = e.transpose(out=trp[:, kc, :], in_=xc_t[:, kc * 128:(kc + 1) * 128], identity=ident_f[:])
            return r
        P.add("pe", pe, reads=[xc_res, "ident_f"], writes=[trp_res])
        for kc in range(8):
            P.add("act", lambda e, kc=kc: e.activation(
                out=xmT[:, kc, col0:col0 + 128], in_=trp[:, kc, :], func=AF.Identity,
                scale=mcol[:, layer, scale_mod * 8 + kc, cnd:cnd + 1],
                bias=mcol[:, layer, shift_mod * 8 + kc, cnd:cnd + 1]),
                reads=[trp_res, "mcol"], writes=[xm_res])

    class Tail:
        def __init__(self, es, layer, gate_mod, ln_idx, nslots=2, conds=(0, 1)):
            self.gate = [sbuf(es, f"gate_bc{c}", [128, D], F32) if c in conds else None for c in range(2)]
            self.g_bc = sbuf(es, "g_bc", [128, D], F32)
            self.b_bc = sbuf(es, "b_bc", [128, D], F32)
            self.t1 = [sbuf(es, f"t1_{s}", [128, D], F32) for s in range(nslots)]
            self.st = [sbuf(es, f"st_{s}", [128, 2, 6], F32) for s in range(nslots)]
            self.mv = [sbuf(es, f"mv_{s}", [128, 2], F32) for s in range(nslots)]
            self.rs = [sbuf(es, f"rs_{s}", [128, 2], F32) for s in range(nslots)]
            self.n = nslots
            self.epsc = sbuf(es, "epsc", [128, 1], F32)
            P.add("pool", lambda e: e.memset(self.epsc[:], LN_EPS), writes=["epsc"])
            for c in conds:
                DMA("sp", f"gatebc{c}", self.gate[c][:],
                    mrow_d[layer, c, gate_mod * D:(gate_mod + 1) * D].partition_broadcast(128),
                    reads=["mrow_d"], writes=[f"gate_bc{c}"])
            DMA("sp", "gbc", self.g_bc[:], ln_g[layer, ln_idx].partition_broadcast(128), writes=["g_bc"])
            DMA("sp", "bbc", self.b_bc[:], ln_b[layer, ln_idx].partition_broadcast(128), writes=["b_bc"])

        def piece(self, s, yap, yres, lo, hi, cnd):
            t1 = self.t1[s]
            P.add("dve", lambda e: e.tensor_tensor(out=t1[:, lo:hi], in0=yap, in1=self.gate[cnd][:, lo:hi], op=ALU.mult),
                  reads=[yres, f"gate_bc{cnd}"], writes=[f"t1_{s}"])

        def rest(self, s, xold, xres, dst, dst_res, before_store=None):
            t1, st, mv, rs = self.t1[s], self.st[s], self.mv[s], self.rs[s]
            r = f"t1_{s}"
            P.add("dve", lambda e: e.scalar_tensor_tensor(out=t1[:], in0=xold, scalar=ALPHA, in1=t1[:],
                                                          op0=ALU.mult, op1=ALU.add), reads=[xres, r], writes=[r])

            def stats(e):
                e.bn_stats(out=st[:, 0, :], in_=t1[:, 0:512])
                return e.bn_stats(out=st[:, 1, :], in_=t1[:, 512:1024])
            P.add("dve", stats, reads=[r], writes=[f"st_{s}"])
            P.add("dve", lambda e: e.bn_aggr(out=mv[:], in_=st[:].rearrange("p a b -> p (a b)")),
                  reads=[f"st_{s}"], writes=[f"mv_{s}"])
            P.add("act", lambda e: e.activation(out=rs[:, 0:1], in_=mv[:, 1:2], func=AF.Sqrt, bias=self.epsc[:, 0:1], scale=1.0),
                  reads=[f"mv_{s}", "epsc"], writes=[f"rsa_{s}"])
            P.add("dve", lambda e: e.reciprocal(out=rs[:, 0:1], in_=rs[:, 0:1]), reads=[f"rsa_{s}"], writes=[f"rsa_{s}"])
            P.add("dve", lambda e: e.tensor_scalar(out=rs[:, 1:2], in0=mv[:, 0:1], scalar1=rs[:, 0:1], scalar2=-1.0,
                                                   op0=ALU.mult, op1=ALU.mult), reads=[f"mv_{s}", f"rsa_{s}"], writes=[f"rsb_{s}"])
            P.add("act", lambda e: e.activation(out=t1[:], in_=t1[:], func=AF.Identity, scale=rs[:, 0:1], bias=rs[:, 1:2]),
                  reads=[r, f"rsa_{s}", f"rsb_{s}"], writes=[r])
            P.add("pool", lambda e: e.tensor_tensor(out=t1[:], in0=t1[:], in1=self.g_bc[:], op=ALU.mult),
                  reads=[r, "g_bc"], writes=[r])
            P.add("pool", lambda e: e.tensor_tensor(out=t1[:], in0=t1[:], in1=self.b_bc[:], op=ALU.add),
                  reads=[r, "b_bc"], writes=[r])
            if before_store:
                before_store()
            DMA("sp", f"st_t1_{s}", dst, t1[:], reads=[r], writes=[dst_res])

    with ExitStack() as es:
        scf = sbuf(es, "scf", [128, 8, 2], F32)
        scb = sbuf(es, "scb", [128, 8, 2], BF16)
        mrow_ = R2[0:2, 0:18432].bitcast(F32)
        mrow = [mrow_, mrow_]
        wblk = [sbuf(es, f"wblk{s}", [128, 8, 512], BF16) for s in range(3)]
        brow = [sbuf(es, f"brow{s}", [2, 512], F32) for s in range(3)]
        mps = [psum(es, f"mps{s}", [128, 512], F32) for s in range(2)]
        tps = psum(es, "tps0", [128, 512], F32)
        DMA("sp", "scf", scf[:], cond, writes=["scf"])
        P.add("act", lambda e: e.activation(out=scb[:], in_=scf[:], func=AF.Silu), reads=["scf"], writes=["scb"])
        for layer in range(2):
            for blk in range(18):
                s3, s2 = blk % 3, blk % 2
                DMA("pool", f"wblk{s3}", wblk[s3][:],
                    w_mod[layer, :, blk * 512:(blk + 1) * 512].rearrange("(k p) c -> p k c", p=128),
                    writes=[f"wblk{s3}"])
                DMA("sp", f"brow{s3}", brow[s3][:], bmod2[:, layer, blk * 512:(blk + 1) * 512], writes=[f"brow{s3}"])

                def pe(e, s3=s3, s2=s2):
                    r = None
                    for kc in range(8):
                        r = e.matmul(mps[s2][0:2, :], lhsT=scb[:, kc, :], rhs=wblk[s3][:, kc, :], start=(kc == 0), stop=(kc == 7))
                    return r
                P.add("pe", pe, reads=["scb", f"wblk{s3}"], writes=[f"mps{s2}"])
                P.add("dve", lambda e, s3=s3, s2=s2, blk=blk, layer=layer: e.tensor_tensor(
                    out=mrow[layer][:, blk * 512:(blk + 1) * 512], in0=mps[s2][0:2, :], in1=brow[s3][:], op=ALU.add),
                    reads=[f"mps{s2}", f"brow{s3}"], writes=["R2"])
            for m in (1, 4, 7):
                P.add("dve", lambda e, m=m, layer=layer: e.tensor_scalar(
                    out=mrow[layer][:, m * D:(m + 1) * D], in0=mrow[layer][:, m * D:(m + 1) * D], scalar1=1.0, scalar2=None, op0=ALU.add),
                    reads=["R2"], writes=["R2"])
            for m in (2, 8):
                P.add("dve", lambda e, m=m, layer=layer: e.tensor_scalar(
                    out=mrow[layer][:, m * D:(m + 1) * D], in0=mrow[layer][:, m * D:(m + 1) * D], scalar1=0.5, scalar2=None, op0=ALU.mult),
                    reads=["R2"], writes=["R2"])
            DMA("sp", f"mrowst{layer}", mrow_d[layer], mrow[layer][:], reads=["R2"], writes=["mrow_d"])

            def pe_t(e, layer=layer):
                r = None
                for jj in range(72):
                    r = e.transpose(out=tps[:, 2 * jj:2 * jj + 2], in_=mrow[layer][0:2, jj * 128:(jj + 1) * 128], identity=ident_f[0:2, 0:2])
                return r
            P.add("pe", pe_t, reads=["R2", "ident_f"], writes=["tps0"])
            P.add("dve", lambda e, layer=layer: e.tensor_copy(
                out=mcol[:, layer, :, :].rearrange("p a b -> p (a b)"), in_=tps[:, 0:144]), reads=["tps0"], writes=["mcol"])
        load_w1(0, 0)
        load_w2(0, 0)
        fin = done("P0")

    def phase_FA(layer, j, src, with_ctx, prefetch=None):
        shift_mod, scale_mod = (0, 1) if j == 0 else (6, 7)
        tiles = [(4 * t, 4, 0) for t in range(8)] + ([(32, 2, 1)] if with_ctx else [])
        w1 = R1[:, 0:8 * 5632].rearrange("p (k f) -> p k f", k=8)
        with ExitStack() as es:
            NX = 6
            xc = [sbuf(es, f"xc{s}", [128, D], F32) for s in range(NX)]
            xmT = [sbuf(es, f"xmT{s}", [128, 8, 512], BF16) for s in range(2)]
            sg = [sbuf(es, f"sg{s}", [128, 512], F32) for s in range(2)]
            hto = [sbuf(es, f"hto{s}", [128, 512], BF16) for s in range(4)]
            trp = [psum(es, f"trp{s}", [128, 8, 128], F32) for s in range(2)]
            gu = [psum(es, f"gu{s}", [128, 2, 512], F32) for s in range(2)]
            if prefetch:
                prefetch()
            chunks = [(c0 + ci, cnd) for (c0, n, cnd) in tiles for ci in range(n)]
            nload = [0]

            def load_next():
                if nload[0] < len(chunks):
                    c, _ = chunks[nload[0]]
                    s = nload[0] % NX
                    DMA("sp", f"xc{s}", xc[s][:], src[c * 128:(c + 1) * 128, :], reads=[("x", id(src), c)], writes=[f"xc{s}"])
                    nload[0] += 1
            for _ in range(NX):
                load_next()
            cidx = [0]

            def prep(ti):
                c0, n, cnd = tiles[ti]
                for ci in range(n):
                    k = cidx[0]
                    s = k % NX
                    prep_chunk(xc[s], f"xc{s}", trp[k % 2], f"trp{k % 2}", xmT[ti % 2], f"xmT{ti % 2}", ci * 128,
                               layer, shift_mod, scale_mod, cnd)
                    cidx[0] += 1
                    load_next()
            prep(0)
            for ti, (c0, n, cnd) in enumerate(tiles):
                NT = n * 128
                xm = xmT[ti % 2]
                for f in range(22):
                    b = f % 2

                    def pe(e, f=f, b=b, xm=xm, NT=NT):
                        r = None
                        for half in range(2):
                            col = half * DFF + f * 128
                            for kc in range(8):
                                r = e.matmul(gu[b][:, half, 0:NT], lhsT=w1[:, kc, col:col + 128], rhs=xm[:, kc, 0:NT],
                                             start=(kc == 0), stop=(kc == 7))
                        return r
                    P.add("pe", pe, reads=["R1", f"xmT{ti % 2}"], writes=[f"gu{b}"])
                    P.add("act", lambda e, b=b, NT=NT: e.activation(out=sg[b][:, 0:NT], in_=gu[b][:, 0, 0:NT], func=AF.Silu),
                          reads=[f"gu{b}"], writes=[f"sg{b}"])
                    hs = f % 4
                    P.add("dve", lambda e, b=b, hs=hs, NT=NT: e.tensor_tensor(out=hto[hs][:, 0:NT], in0=sg[b][:, 0:NT],
                                                                              in1=gu[b][:, 1, 0:NT], op=ALU.mult),
                          reads=[f"sg{b}", f"gu{b}"], writes=[f"hto{hs}"])
                    DMA("sp", f"hto{hs}", hT_d[c0:c0 + n, :, f, :].rearrange("c p t -> p c t"),
                        hto[hs][:, 0:NT].rearrange("p (c t) -> p c t", t=128),
                        reads=[f"hto{hs}"], writes=[("hT", c0, f)])
                    if f == 8 and ti + 1 < len(tiles):
                        prep(ti + 1)
            return done(f"FA{layer}{j}")

    def phase_B(name, nf, w2, layer, gate_mod, ln_idx, xsrc, dst, chunk_list, dst_rows, prefetch=None):
        with ExitStack() as es:
            tl = Tail(es, layer, gate_mod, ln_idx)
            xc = [sbuf(es, f"xc{s}", [128, D], F32) for s in range(2)]
            hin = [sbuf(es, f"hin{s}", [128, nf, 128], BF16) for s in range(2)]
            yps = [psum(es, f"yps{s}", [128, D], F32) for s in range(2)]
            if prefetch:
                prefetch()
            nl = [0]

            def load_next():
                if nl[0] < len(chunk_list):
                    c = chunk_list[nl[0]]
                    s = nl[0] % 2
                    DMA("sp", f"xc{s}", xc[s][:], xsrc[c * 128:(c + 1) * 128, :], reads=[("x", id(xsrc), c)], writes=[f"xc{s}"])
                    DMA("sp", f"hin{s}", hin[s][:], hT_d[c, :, 0:nf, :], reads=[("hT", c)], writes=[f"hin{s}"])
                    nl[0] += 1
            for _ in range(2):
                load_next()
            for k, c in enumerate(chunk_list):
                s, b = k % 2, k % 2
                cnd = 0 if c < NLAT else 1
                for half in range(2):
                    def pe(e, s=s, b=b, half=half):
                        r = None
                        for f in range(nf):
                            r = e.matmul(yps[b][:, half * 512:(half + 1) * 512], lhsT=hin[s][:, f, :],
                                         rhs=w2[:, f, half * 512:(half + 1) * 512], start=(f == 0), stop=(f == nf - 1))
                        return r
                    P.add("pe", pe, reads=["R2", f"hin{s}"], writes=[(f"yps{b}", half)])
                    tl.piece(b, yps[b][:, half * 512:(half + 1) * 512], (f"yps{b}", half), half * 512, (half + 1) * 512, cnd)
                r0 = dst_rows(c)
                tl.rest(b, xc[s][:], f"xc{s}", dst[r0:r0 + 128, :], ("x", id(dst), c), before_store=load_next)
            return done(name)


    def gen_loader(name_prefix, nslots, slots, chunk_ids, src, res_prefix):
        st_ = [0]

        def load_next():
            if st_[0] < len(chunk_ids):
                c = chunk_ids[st_[0]]
                s_ = st_[0] % nslots
                DMA("sp", f"{res_prefix}{s_}", slots[s_][:], src[c * 128:(c + 1) * 128, :],
                    reads=[("x", id(src), c)], writes=[f"{res_prefix}{s_}"])
                st_[0] += 1
        return load_next

    def phase_G1(src, prefetch=None):
        layer, shift_mod, scale_mod = 0, 3, 4
        tiles = [(2 * t, 2, 0) for t in range(16)] + [(32, 2, 1)]
        w_in = R1.rearrange("p (k f) -> p k f", k=8)
        with ExitStack() as es:
            NX = 4
            xc = [sbuf(es, "xc", [128, D], F32) for s_ in range(NX)]
            xmT = [sbuf(es, "xmT", [128, 8, 256], BF16) for s_ in range(2)]
            vh = sbuf(es, "vh", [128, 3072], BF16)
            hho = [sbuf(es, "hho", [128, 24, 128], BF16) for s_ in range(2)]
            wsT = sbuf(es, "wsT", [128, 8, 128], BF16)
            bs_bc = sbuf(es, "bs_bc", [128, 8, 128], F32)
            tmp = [sbuf(es, "tmp", [128, 128], F32) for s_ in range(2)]
            gcol = sbuf(es, "gcol", [128, 24], F32)
            bcol = sbuf(es, "bcol", [128, 24], F32)
            ones_b = sbuf(es, "ones_b", [128, 128], BF16)
            st6 = sbuf(es, "st6", [128, 6, 6], F32)
            mv = sbuf(es, "mvg", [128, 2], F32)
            rs = sbuf(es, "rsg", [128, 2], F32)
            epsc = sbuf(es, "epsg", [128, 1], F32)
            uT = [R2[:, i * 6144:(i + 1) * 6144].rearrange("p (f t) -> p f t", f=24) for i in range(2)]
            v_sb = R2[:, 12288:18432].bitcast(F32)
            Bt = R2[:, 18432:24576].bitcast(F32).rearrange("p (f t) -> p f t", f=24)
            trp = psum(es, "trp", [128, 8, 128], F32)
            ups_ = [psum(es, "ups", [128, 512], F32) for s_ in range(2)]
            vps = [psum(es, "vps", [128, 512], F32) for s_ in range(2)]
            sps = [psum(es, "sps", [128, 4, 128], F32) for s_ in range(2)]
            if prefetch:
                prefetch()
            DMA("pool", "wsT", wsT[:], wsT_d, writes=["wsT"])
            DMA("sp", "gcol", gcol[:], glng, writes=["gcol"])
            DMA("sp", "bcol", bcol[:], glnb, writes=["bcol"])
            DMA("sp", "bsbc", bs_bc[:].rearrange("p g t -> p (g t)"), gbs.partition_broadcast(128), writes=["bs_bc"])
            P.add("pool", lambda e: e.memset(ones_b[:], 1.0), writes=["ones_b"])
            P.add("pool", lambda e: e.memset(epsc[:], LN_EPS), writes=["epsg"])
            for g in range(8):
                P.add("pe", lambda e, g=g: e.matmul(sps[g // 4][:, g % 4, :], lhsT=ones_b[:], rhs=wsT[:, g, :], start=True, stop=True),
                      reads=["ones_b", "wsT"], writes=[f"sps{g // 4}"])
            for cc in range(24):
                g = cc // 3
                P.add("dve", lambda e, cc=cc, g=g: e.scalar_tensor_tensor(
                    out=Bt[:, cc, :], in0=sps[g // 4][:, g % 4, :], scalar=bcol[:, cc:cc + 1], in1=bs_bc[:, g, :],
                    op0=ALU.mult, op1=ALU.add), reads=[f"sps{g // 4}", "bcol", "bs_bc"], writes=["Bt"])
            chunks = [c0 + ci for (c0, n, cnd) in tiles for ci in range(n)]
            load_next = gen_loader("xc", NX, xc, chunks, src, "xc")
            for _ in range(NX):
                load_next()
            kidx = [0]

            def prep(ti):
                c0, n, cnd = tiles[ti]
                for ci in range(n):
                    k = kidx[0]
                    prep_chunk(xc[k % NX], f"xc{k % NX}", trp, "trp", xmT[ti % 2], f"xmT{ti % 2}", ci * 128,
                               layer, shift_mod, scale_mod, cnd)
                    kidx[0] += 1
                    load_next()
            prep(0)

            def emit_uT(ti, lo, hi):
                xm_ = xmT[ti % 2]
                uu = uT[ti % 2]
                for cc in range(lo, hi):
                    def pe(e, cc=cc, xm_=xm_):
                        r = None
                        for kc in range(8):
                            r = e.matmul(ups_[cc % 2][:, 0:256], lhsT=w_in[:, kc, cc * 128:(cc + 1) * 128], rhs=xm_[:, kc, :],
                                         start=(kc == 0), stop=(kc == 7))
                        return r
                    P.add("pe", pe, reads=["R1", f"xmT{ti % 2}"], writes=[f"ups{cc % 2}"])
                    P.add("act", lambda e, cc=cc, uu=uu: e.activation(out=uu[:, cc, :], in_=ups_[cc % 2][:, 0:256], func=AF.Gelu),
                          reads=[f"ups{cc % 2}"], writes=[f"uT{ti % 2}"])
            emit_uT(0, 0, 24)
            for ti, (c0, n, cnd) in enumerate(tiles):
                xm = xmT[ti % 2]
                u_ = uT[ti % 2]
                has_next = ti + 1 < len(tiles)
                if has_next:
                    prep(ti + 1)
                for ci in range(n):
                    c = c0 + ci
                    for blk in range(6):
                        def pe(e, blk=blk, xm=xm, ci=ci):
                            r = None
                            for kc in range(8):
                                r = e.matmul(vps[blk % 2][:, :], lhsT=xm[:, kc, ci * 128:(ci + 1) * 128],
                                             rhs=w_in[:, kc, 3072 + blk * 512:3072 + (blk + 1) * 512], start=(kc == 0), stop=(kc == 7))
                            return r
                        P.add("pe", pe, reads=["R1", f"xmT{ti % 2}"], writes=[f"vps{blk % 2}"])
                        P.add("act", lambda e, blk=blk: e.activation(out=v_sb[:, blk * 512:(blk + 1) * 512], in_=vps[blk % 2][:, :], func=AF.Gelu),
                              reads=[f"vps{blk % 2}"], writes=["v_sb"])

                    def stats(e):
                        r = None
                        for q in range(6):
                            r = e.bn_stats(out=st6[:, q, :], in_=v_sb[:, q * 512:(q + 1) * 512])
                        return r
                    P.add("dve", stats, reads=["v_sb"], writes=["st6"])
                    P.add("dve", lambda e: e.bn_aggr(out=mv[:], in_=st6[:].rearrange("p a b -> p (a b)")), reads=["st6"], writes=["mvg"])
                    P.add("act", lambda e: e.activation(out=rs[:, 0:1], in_=mv[:, 1:2], func=AF.Sqrt, bias=epsc[:, 0:1], scale=1.0),
                          reads=["mvg", "epsg"], writes=["rsga"])
                    P.add("dve", lambda e: e.reciprocal(out=rs[:, 0:1], in_=rs[:, 0:1]), reads=["rsga"], writes=["rsga"])
                    P.add("dve", lambda e: e.tensor_scalar(out=rs[:, 1:2], in0=mv[:, 0:1], scalar1=rs[:, 0:1], scalar2=-1.0,
                                                           op0=ALU.mult, op1=ALU.mult), reads=["mvg", "rsga"], writes=["rsgb"])
                    P.add("act", lambda e: e.activation(out=vh[:], in_=v_sb[:], func=AF.Identity, scale=rs[:, 0:1], bias=rs[:, 1:2]),
                          reads=["v_sb", "rsga", "rsgb"], writes=["vh"])
                    if has_next:
                        emit_uT(ti + 1, ci * 12, (ci + 1) * 12)
                    for cg in range(6):
                        def pe(e, cg=cg):
                            r = None
                            for i in range(4):
                                cc = cg * 4 + i
                                r = e.matmul(sps[cg % 2][:, i, :], lhsT=vh[:, cc * 128:(cc + 1) * 128], rhs=wsT[:, cc // 3, :],
                                             start=True, stop=True)
                            return r
                        P.add("pe", pe, reads=["vh", "wsT"], writes=[f"sps{cg % 2}"])
                        for i in range(4):
                            cc = cg * 4 + i
                            P.add("dve", lambda e, cc=cc, cg=cg, i=i: e.scalar_tensor_tensor(
                                out=tmp[cc % 2][:], in0=sps[cg % 2][:, i, :], scalar=gcol[:, cc:cc + 1], in1=Bt[:, cc, :],
                                op0=ALU.mult, op1=ALU.add), reads=[f"sps{cg % 2}", "gcol", "Bt"], writes=[f"tmp{cc % 2}"])
                            P.add("pool", lambda e, cc=cc, c=c, ci=ci, u_=u_: e.tensor_tensor(
                                out=hho[c % 2][:, cc, :], in0=tmp[cc % 2][:], in1=u_[:, cc, ci * 128:(ci + 1) * 128], op=ALU.mult),
                                reads=[f"tmp{cc % 2}", f"uT{ti % 2}"], writes=[f"hho{c % 2}"])
                    DMA("sp", f"hho{c % 2}", hT_d[c], hho[c % 2][:], reads=[f"hho{c % 2}"], writes=[("hT", c)])
            return done("G1")

    def phase_QKV(src, prefetch=None):
        layer, shift_mod, scale_mod = 1, 3, 4
        wq = R1[:, 0:8 * 1536].rearrange("p (k f) -> p k f", k=8)
        with ExitStack() as es:
            NX = 3
            xc = [sbuf(es, "xc", [128, D], F32) for s_ in range(NX)]
            xmT = [sbuf(es, "xmT", [128, 8, 128], BF16) for s_ in range(2)]
            sq = [sbuf(es, "sq", [128, 512], F32) for s_ in range(3)]
            ss = sbuf(es, "ss", [128, 12], F32)
            rst = sbuf(es, "rst", [128, 12], F32)
            qn = sbuf(es, "qnb", [128, 10, 128], F32)
            ra = sbuf(es, "ra", [128, 10, 64], F32)
            rb = sbuf(es, "rb", [128, 10, 64], F32)
            rc = sbuf(es, "rc", [128, 10, 64], F32)
            rd = sbuf(es, "rd", [128, 10, 64], F32)
            qr = sbuf(es, "qr", [128, 10, 128], BF16)
            cosT = R2[:, 12288:16384].bitcast(F32).rearrange("p (c j) -> p c j", c=NLAT)
            sinT = R2[:, 16384:20480].bitcast(F32).rearrange("p (c j) -> p c j", c=NLAT)
            ang = R2[:, 0:4096].bitcast(F32).rearrange("p (c j) -> p c j", c=NLAT)
            angi = R2[:, 4096:8192].bitcast(I32).rearrange("p (c j) -> p c j", c=NLAT)
            ang2 = R2[:, 8192:12288].bitcast(F32).rearrange("p (c j) -> p c j", c=NLAT)
            prc = sbuf(es, "prc", [128, NLAT, 2], F32)
            frq = sbuf(es, "frq", [128, 64], F32)
            gq_bc = sbuf(es, "gq_bc", [128, 128], F32)
            gk_bc = sbuf(es, "gk_bc", [128, 128], F32)
            qTs = [sbuf(es, "qTs", [128, 8, 128], BF16) for s_ in range(2)]
            kTs = [sbuf(es, "kTs", [128, 2, 128], BF16) for s_ in range(2)]
            vs = [sbuf(es, "vs", [128, 256], BF16) for s_ in range(2)]
            trp = psum(es, "trp", [128, 8, 128], F32)
            qkv = [psum(es, "qkv", [128, 512], F32) for s_ in range(3)]
            tq = psum(es, "tq", [128, 8, 128], BF16)
            tk = psum(es, "tk", [128, 8, 128], BF16)
            if prefetch:
                prefetch()

            def emit_w(e):
                srcw = w_qkv.rearrange("(k p) f -> p k f", p=128)
                return [e.dma_start(out=wq[:, kc, :], in_=srcw[:, kc, :]) for kc in range(8)]
            P.add("pool", emit_w, writes=["R1"], dma="wR1", ndma=8)
            DMA("sp", "prc", prc[:], posrc, writes=["prc"])
            DMA("sp", "frq", frq[:], freq2, writes=["frq"])
            DMA("sp", "gqbc", gq_bc[:], qn_d.partition_broadcast(128), writes=["gq_bc"])
            DMA("sp", "gkbc", gk_bc[:], kn_d.partition_broadcast(128), writes=["gk_bc"])
            for hh_ in range(2):
                P.add("dve", lambda e, hh_=hh_: e.tensor_tensor(
                    out=ang[:, :, hh_ * 32:(hh_ + 1) * 32], in0=prc[:, :, hh_:hh_ + 1].broadcast_to([128, NLAT, 32]),
                    in1=frq[:, hh_ * 32:(hh_ + 1) * 32].unsqueeze(1).broadcast_to([128, NLAT, 32]), op=ALU.mult),
                    reads=["prc", "frq"], writes=["ang"])
            P.add("dve", lambda e: e.tensor_scalar(out=ang[:], in0=ang[:], scalar1=1.0 / (2.0 * math.pi), scalar2=None, op0=ALU.mult),
                  reads=["ang"], writes=["ang"])
            P.add("dve", lambda e: e.tensor_copy(out=angi[:], in_=ang[:]), reads=["ang"], writes=["angi"])
            P.add("dve", lambda e: e.tensor_copy(out=ang2[:], in_=angi[:]), reads=["angi"], writes=["ang2"])
            P.add("dve", lambda e: e.tensor_tensor(out=ang[:], in0=ang[:], in1=ang2[:], op=ALU.subtract), reads=["ang", "ang2"], writes=["ang"])
            P.add("dve", lambda e: e.tensor_scalar(out=ang2[:], in0=ang[:], scalar1=-1.0, scalar2=None, op0=ALU.mult), reads=["ang"], writes=["ang2"])
            P.add("dve", lambda e: e.tensor_tensor(out=ang2[:], in0=ang2[:], in1=ang[:], op=ALU.max), reads=["ang", "ang2"], writes=["ang2"])
            P.add("act", lambda e: e.activation(out=sinT[:], in_=ang[:], func=AF.Sin, scale=math.pi), reads=["ang"], writes=["sinT"])
            P.add("dve", lambda e: e.tensor_scalar(out=ang2[:], in0=ang2[:], scalar1=-math.pi, scalar2=math.pi / 2, op0=ALU.mult, op1=ALU.add),
                  reads=["ang2"], writes=["ang2"])
            P.add("act", lambda e: e.activation(out=cosT[:], in_=ang2[:], func=AF.Sin), reads=["ang2"], writes=["cosT"])
            P.add("dve", lambda e: e.tensor_tensor(out=ang[:], in0=sinT[:], in1=sinT[:], op=ALU.mult), reads=["sinT"], writes=["ang"])
            P.add("dve", lambda e: e.scalar_tensor_tensor(out=sinT[:], in0=sinT[:], scalar=2.0, in1=cosT[:], op0=ALU.mult, op1=ALU.mult),
                  reads=["sinT", "cosT", "ang"], writes=["sinT"])
            P.add("dve", lambda e: e.tensor_scalar(out=cosT[:], in0=ang[:], scalar1=-2.0, scalar2=1.0, op0=ALU.mult, op1=ALU.add),
                  reads=["ang", "sinT"], writes=["cosT"])
            chunks = list(range(NCH))
            load_next = gen_loader("xc", NX, xc, chunks, src, "xc")
            for _ in range(NX):
                load_next()
            pending = [None]
            for k, c in enumerate(chunks):
                lat = c < NLAT
                cnd = 0 if lat else 1
                b = k % 2
                prep_chunk(xc[k % NX], f"xc{k % NX}", trp, "trp", xmT[b], f"xmT{b}", 0, layer, shift_mod, scale_mod, cnd)
                load_next()
                blks = (0, 1, 2) if lat else (2,)
                for blk in blks:
                    def pe(e, blk=blk, b=b):
                        r = None
                        for kc in range(8):
                            r = e.matmul(qkv[blk][:, :], lhsT=xmT[b][:, kc, :], rhs=wq[:, kc, blk * 512:(blk + 1) * 512],
                                         start=(kc == 0), stop=(kc == 7))
                        return r
                    P.add("pe", pe, reads=["R1", f"xmT{b}"], writes=[f"qkv{blk}"])
                    P.add("act", lambda e, blk=blk: e.activation(out=sq[blk][:], in_=qkv[blk][:, :], func=AF.Square),
                          reads=[f"qkv{blk}"], writes=[f"sq{blk}"])
                    P.add("dve", lambda e, blk=blk: e.tensor_reduce(out=ss[:, blk * 4:(blk + 1) * 4],
                                                                    in_=sq[blk][:].rearrange("p (h d) -> p h d", h=4), axis=AX.X, op=ALU.add),
                          reads=[f"sq{blk}"], writes=["ss"])
                P.add("dve", lambda e: e.tensor_scalar(out=rst[:], in0=ss[:], scalar1=1.0 / 128.0, scalar2=RMS_EPS, op0=ALU.mult, op1=ALU.add),
                      reads=["ss"], writes=["rst"])
                P.add("act", lambda e: e.activation(out=rst[:], in_=rst[:], func=AF.Sqrt), reads=["rst"], writes=["rst"])
                P.add("dve", lambda e: e.reciprocal(out=rst[:], in_=rst[:]), reads=["rst"], writes=["rst"])
                if pending[0] is not None:
                    pending[0]()
                    pending[0] = None
                heads = list(range(10)) if lat else [8, 9]
                for h in heads:
                    blk, off = (h // 4, (h % 4) * 128) if h < 8 else (2, (h - 8) * 128)
                    gb = gq_bc if h < 8 else gk_bc
                    P.add("dve", lambda e, h=h, blk=blk, off=off, gb=gb: e.scalar_tensor_tensor(
                        out=qn[:, h, :], in0=qkv[blk][:, off:off + 128], scalar=rst[:, (blk * 4 + (off // 128)):(blk * 4 + (off // 128)) + 1],
                        in1=gb[:], op0=ALU.mult, op1=ALU.mult), reads=[f"qkv{blk}", "rst", "gq_bc", "gk_bc"], writes=["qn"])
                if lat:
                    x1, x2 = qn[:, :, 0:64], qn[:, :, 64:128]
                    cb = cosT[:, c, :].unsqueeze(1).broadcast_to([128, 10, 64])
                    sb_ = sinT[:, c, :].unsqueeze(1).broadcast_to([128, 10, 64])
                    P.add("pool", lambda e, x1=x1, cb=cb: e.tensor_tensor(out=ra[:], in0=x1, in1=cb, op=ALU.mult), reads=["qn", "cosT"], writes=["ra"])
                    P.add("pool", lambda e, x2=x2, sb_=sb_: e.tensor_tensor(out=rb[:], in0=x2, in1=sb_, op=ALU.mult), reads=["qn", "sinT"], writes=["rb"])
                    P.add("dve", lambda e: e.tensor_tensor(out=qr[:, :, 0:64], in0=ra[:], in1=rb[:], op=ALU.subtract), reads=["ra", "rb"], writes=["qr"])
                    P.add("pool", lambda e, x2=x2, cb=cb: e.tensor_tensor(out=rc[:], in0=x2, in1=cb, op=ALU.mult), reads=["qn", "cosT"], writes=["rc"])
                    P.add("pool", lambda e, x1=x1, sb_=sb_: e.tensor_tensor(out=rd[:], in0=x1, in1=sb_, op=ALU.mult), reads=["qn", "sinT"], writes=["rd"])
                    P.add("dve", lambda e: e.tensor_tensor(out=qr[:, :, 64:128], in0=rc[:], in1=rd[:], op=ALU.add), reads=["rc", "rd"], writes=["qr"])
                else:
                    P.add("dve", lambda e: e.tensor_copy(out=qr[:, 8:10, :], in_=qn[:, 8:10, :]), reads=["qn"], writes=["qr"])

                P.add("act", lambda e, b=b: e.activation(out=vs[b][:], in_=qkv[2][:, 256:512], func=AF.Copy), reads=["qkv2"], writes=[f"vs{b}"])

                def stage_b(c=c, lat=lat, b=b):
                  def pe_t(e, lat=lat):
                      r = None
                      if lat:
                          for h in range(8):
                              r = e.transpose(out=tq[:, h, :], in_=qr[:, h, :], identity=ident_b[:])
                      for kv in range(2):
                          r = e.transpose(out=tk[:, kv, :], in_=qr[:, 8 + kv, :], identity=ident_b[:])
                      return r
                  P.add("pe", pe_t, reads=["qr", "ident_b"], writes=["tq", "tk"])
                  if lat:
                      P.add("act", lambda e, b=b: e.activation(out=qTs[b][:], in_=tq[:], func=AF.Copy), reads=["tq"], writes=[f"qTs{b}"])
                      DMA("sp", f"qTs{b}", qT_d[c].rearrange("p (h t) -> p h t", h=8), qTs[b][:], reads=[f"qTs{b}"], writes=[("qT", c)])
                  P.add("dve", lambda e, b=b: e.tensor_copy(out=kTs[b][:], in_=tk[:, 0:2, :]), reads=["tk"], writes=[f"kTs{b}"])
                  if lat:
                      def st_k(e, b=b, c=c):
                          return [e.dma_start(out=kin[kv].ap()[:, c * 128:(c + 1) * 128], in_=kTs[b][:, kv, :]) for kv in range(2)]
                      P.add("sp", st_k, reads=[f"kTs{b}"], writes=["kin"], dma=f"kTs{b}", ndma=2)
                      hv, j16 = c // 16, c % 16
                      DMA("sp", f"vs{b}", vin[hv].ap()[j16 * 8:(j16 + 1) * 8, :].rearrange("a (b c) -> (a b) c", c=256), vs[b][:],
                          reads=[f"vs{b}"], writes=["vin"])
                  else:
                      cc_ = c - NLAT

                      def st_k(e, b=b, cc_=cc_):
                          return [e.dma_start(out=kc_d[kv, :, cc_ * 128:(cc_ + 1) * 128], in_=kTs[b][:, kv, :]) for kv in range(2)]
                      P.add("sp", st_k, reads=[f"kTs{b}"], writes=["kc_d"], dma=f"kTs{b}", ndma=2)
                      DMA("sp", f"vs{b}", vc_d[cc_ * 128:(cc_ + 1) * 128, :], vs[b][:], reads=[f"vs{b}"], writes=["vc_d"])
                pending[0] = stage_b
            pending[0]()
            P.barrier()
            for i in range(2):
                P.add("pool", lambda e, i=i: e.collective_compute("AllGather", ALU.bypass, replica_groups=GROUPS,
                                                                   ins=[kin[i].ap().opt()], outs=[kout[i].ap().opt()]),
                      reads=["kin"], writes=[("kout", i)], dma=f"cck{i}", inc=1)
                P.add("pool", lambda e, i=i: e.collective_compute("AllGather", ALU.bypass, replica_groups=GROUPS,
                                                                   ins=[vin[i].ap().opt()], outs=[vout[i].ap().opt()]),
                      reads=["vin"], writes=[("vout", i)], dma=f"ccv{i}", inc=1)
            return done("QKV")

    def phase_ATT(xsrc, dst):
        layer = 1
        KT = BIG[:, 0:33280].rearrange("p (k t) -> p k t", k=2)
        V = BIG[:, 33280:66560].rearrange("p (c k d) -> p c k d", c=NKC, k=2)
        wo = BIG[:, 66560:74752].rearrange("p (h d) -> p h d", h=8)
        NP = NKC // 2
        with ExitStack() as es:
            tl = Tail(es, layer, 5, 1, nslots=1, conds=(0,))
            xc = sbuf(es, "xc", [128, D], F32)
            qTt = [sbuf(es, "qTt", [128, 8, 128], BF16) for s_ in range(2)]
            NPT = 5
            pt = [sbuf(es, "pt", [128, 1024], BF16) for s_ in range(NPT)]
            acc = {"dve": sbuf(es, "accD", [128, 1024], F32), "pool": sbuf(es, "accP", [128, 1024], F32)}
            accb = sbuf(es, "accb", [128, 2048], BF16)
            ones_b = sbuf(es, "ones_b", [128, 128], BF16)
            rinv = sbuf(es, "rinv", [128, 512], F32)
            oT = sbuf(es, "oT", [128, 8, 128], BF16)
            spsm = [psum(es, "spsm", [128, 1024], F32) for s_ in range(3)]
            OT = [psum(es, "OT", [128, 512], F32)]
            sump = psum(es, "sump", [128, 512], F32)
            yps = sump
            P.add("pool", lambda e: e.memset(ones_b[:], 1.0), writes=["ones_b"])
            vres = [("Vld", hv, rk) for hv in range(2) for rk in range(4)] + ["Vc"]
            for kv in range(2):
                def ld_k(e, kv=kv):
                    r = [e.dma_start(out=KT[:, kv, rk * 4096:(rk + 1) * 4096], in_=kout[kv].ap()[rk * 128:(rk + 1) * 128, :]) for rk in range(4)]
                    r.append(e.dma_start(out=KT[:, kv, 16384:16640], in_=kc_d[kv]))
                    return r
                P.add("sp", ld_k, reads=[("kout", kv), "kc_d", "R1"], writes=[("Kld", kv)], dma=f"ldk{kv}", ndma=5)
            for hv in range(2):
                for rk in range(4):
                    def ld_v(e, hv=hv, rk=rk):
                        srcv = vout[hv].ap()[rk * 128:(rk + 1) * 128, :].rearrange("(j a) (b k d) -> (a b) j k d", a=8, b=16, k=2)
                        c0 = rk * 32 + hv * 16
                        r = []
                        for kv in range(2):
                            for jh in range(2):
                                r.append(e.dma_start(out=V[:, c0 + jh * 8:c0 + (jh + 1) * 8, kv, :], in_=srcv[:, jh * 8:(jh + 1) * 8, kv, :]))
                        return r
                    P.add("sp", ld_v, reads=[("vout", hv), "R1", "R2"], writes=[("Vld", hv, rk)], dma=f"ldv{hv}{rk}", ndma=4)

            def ld_vc(e):
                srcv = vc_d.rearrange("(j p) (k d) -> p j k d", p=128, k=2)
                return [e.dma_start(out=V[:, 128:130, kv, :], in_=srcv[:, :, kv, :]) for kv in range(2)]
            P.add("sp", ld_vc, reads=["vc_d", "R2"], writes=["Vc"], dma="ldvc", ndma=2)

            def emit_wo(e):
                srcw = w_o_d.rearrange("(h p) d -> p h d", p=128)
                return [e.dma_start(out=wo[:, h, :], in_=srcw[:, h, :]) for h in range(8)]
            P.add("pool", emit_wo, reads=["R2"], writes=["wo"], dma="wo", ndma=8)
            P.add("pe", lambda e: None, reads=vres + [("Kld", 0), ("Kld", 1)], writes=["KVready"])
            qts = list(range(NLAT))
            nl = [0]

            def load_q():
                if nl[0] < len(qts):
                    c = qts[nl[0]]
                    s_ = nl[0] % 2
                    DMA("sp", f"qTt{s_}", qTt[s_][:].rearrange("p h t -> p (h t)"), qT_d[c], reads=[("qT", c)], writes=[f"qTt{s_}"])
                    nl[0] += 1
            load_q()
            load_q()
            first = {"dve": True, "pool": True, "pe": True}
            SUMENG = ["dve", "pool", "pe", "dve", "pe", "pool", "dve", "pe"]
            for qi, c in enumerate(qts):
                s_ = qi % 2
                DMA("sp", "xc0", xc[:], xsrc[c * 128:(c + 1) * 128, :], reads=[("x", id(xsrc), c)], writes=["xc0"])
                for kv in range(2):
                    g = qi * 2 + kv
                    ot = OT[0]
                    otres = "OT0"
                    rhs = qTt[s_][:, kv * 4:(kv + 1) * 4, :].rearrange("p h t -> p (h t)")
                    first["dve"] = True
                    first["pool"] = True
                    first["pe"] = True

                    def S(p, kv=kv, rhs=rhs):
                        sp_ = spsm[p % 3]

                        def pe(e):
                            e.matmul(sp_[:, 0:512], lhsT=KT[:, kv, (2 * p) * 128:(2 * p + 1) * 128], rhs=rhs, start=True, stop=True)
                            return e.matmul(sp_[:, 512:1024], lhsT=KT[:, kv, (2 * p + 1) * 128:(2 * p + 2) * 128], rhs=rhs, start=True, stop=True)
                        P.add("pe", pe, reads=["KVready", f"qTt{s_}"], writes=[f"spsm{p % 3}"])
                        P.add("act", lambda e: e.activation(out=pt[p % NPT][:], in_=sp_[:, :], func=AF.Exp, scale=SCALE),
                              reads=[f"spsm{p % 3}"], writes=[f"pt{p % NPT}"])

                    def PV(p, kv=kv, ot=ot, otres=otres):
                        def pe(e):
                            e.matmul(ot[:, :], lhsT=V[:, 2 * p, kv, :], rhs=pt[p % NPT][:, 0:512], start=(p == 0), stop=False)
                            return e.matmul(ot[:, :], lhsT=V[:, 2 * p + 1, kv, :], rhs=pt[p % NPT][:, 512:1024], start=False, stop=(p == NP - 1))
                        P.add("pe", pe, reads=[f"pt{p % NPT}", "KVready"], writes=[otres])
                        en = SUMENG[p % 8]
                        if en == "pe":
                            st0 = first["pe"]
                            first["pe"] = False

                            def pes(e):
                                e.matmul(sump[:, :], lhsT=ones_b[:], rhs=pt[p % NPT][:, 0:512], start=st0, stop=False)
                                return e.matmul(sump[:, :], lhsT=ones_b[:], rhs=pt[p % NPT][:, 512:1024], start=False, stop=False)
                            P.add("pe", pes, reads=[f"pt{p % NPT}", "ones_b"], writes=["sump"])
                            return
                        a_ = acc[en]
                        if first[en]:
                            first[en] = False
                            P.add(en, lambda e: e.tensor_copy(out=a_[:], in_=pt[p % NPT][:]), reads=[f"pt{p % NPT}"], writes=[f"acc_{en}"])
                        else:
                            P.add(en, lambda e: e.tensor_tensor(out=a_[:], in0=a_[:], in1=pt[p % NPT][:], op=ALU.add),
                                  reads=[f"pt{p % NPT}", f"acc_{en}"], writes=[f"acc_{en}"])
                    S(0)
                    S(1)
                    for p in range(NP):
                        if p + 2 < NP:
                            S(p + 2)
                        PV(p)
                    P.add("dve", lambda e: e.tensor_copy(out=accb[:, 0:1024], in_=acc["dve"][:]), reads=["acc_dve"], writes=["accbD"])
                    P.add("pool", lambda e: e.tensor_copy(out=accb[:, 1024:2048], in_=acc["pool"][:]), reads=["acc_pool"], writes=["accbP"])

                    def pe_sum(e):
                        r = None
                        for i in range(4):
                            r = e.matmul(sump[:, :], lhsT=ones_b[:], rhs=accb[:, i * 512:(i + 1) * 512], start=False, stop=(i == 3))
                        return r
                    P.add("pe", pe_sum, reads=["ones_b", "accbD", "accbP"], writes=["sump"])
                    P.add("dve", lambda e: e.reciprocal(out=rinv[:], in_=sump[:, :]), reads=["sump"], writes=["rinv"])
                    P.add("dve", lambda e, kv=kv, ot=ot: e.tensor_tensor(
                        out=oT[:, kv * 4:(kv + 1) * 4, :].rearrange("p h t -> p (h t)"), in0=ot[:, :], in1=rinv[:], op=ALU.mult),
                        reads=[otres, "rinv"], writes=["oT"])
                for half in range(2):
                    def pe(e, half=half):
                        r = None
                        for h in range(8):
                            r = e.matmul(yps[:, :], lhsT=oT[:, h, :], rhs=wo[:, h, half * 512:(half + 1) * 512], start=(h == 0), stop=(h == 7))
                        return r
                    P.add("pe", pe, reads=["oT", "wo"], writes=["sump"])
                    tl.piece(0, yps[:, :], "sump", half * 512, (half + 1) * 512, 0)
                tl.rest(0, xc[:], "xc0", dst[c * 128:(c + 1) * 128, :], ("x", id(dst), c))
                load_q()
            return done("ATT")

    w2v = R2[:, 0:22 * 1024].rearrange("p (f d) -> p f d", f=22)
    gwo = R2[:, 0:24 * 1024].rearrange("p (f d) -> p f d", f=24)

    def load_gm_in():
        srcw = gm_w_in.rearrange("(k p) f -> p k f", p=128)
        w_in = R1.rearrange("p (k f) -> p k f", k=8)

        def emit(e):
            return [e.dma_start(out=w_in[:, kc, j * 2048:(j + 1) * 2048], in_=srcw[:, kc, j * 2048:(j + 1) * 2048]) for kc in range(8) for j in range(3)]
        P.add("pool", emit, writes=["R1"], dma="wR1", ndma=24, nobar=True)

    def load_gm_out():
        srcw = gm_w_out.rearrange("(f p) d -> p f d", p=128)

        def emit(e):
            return [e.dma_start(out=gwo[:, f, :], in_=srcw[:, f, :]) for f in range(24)]
        P.add("pool", emit, writes=["R2"], dma="wR2", ndma=24, nobar=True)

    allc = list(range(NCH))
    latc = list(range(NLAT))
    rowf = lambda c: c * 128
    stop = fin
    cur = None
    sched = [
        ("FA00", lambda: phase_FA(0, 0, xin, True)),
        ("FB00", lambda: phase_B("FB00", 22, w2v, 0, 2, 0, xin, xa, allc, rowf, prefetch=load_gm_in)),
        ("G1", lambda: phase_G1(xa)),
        ("G2", lambda: phase_B("G2", 24, gwo, 0, 5, 1, xa, xb, allc, rowf, prefetch=lambda: (load_gm_out(), load_w1(0, 1)))),
        ("FA01", lambda: phase_FA(0, 1, xb, True, prefetch=lambda: load_w2(0, 1))),
        ("FB01", lambda: phase_B("FB01", 22, w2v, 0, 8, 2, xb, xa, allc, rowf, prefetch=lambda: load_w1(1, 0))),
        ("FA10", lambda: phase_FA(1, 0, xa, True, prefetch=lambda: load_w2(1, 0))),
        ("FB10", lambda: phase_B("FB10", 22, w2v, 1, 2, 0, xa, xb, allc, rowf)),
        ("QKV", lambda: phase_QKV(xb)),
        ("ATT", lambda: phase_ATT(xb, xa)),
        ("FA11", lambda: phase_FA(1, 1, xa, False, prefetch=lambda: (load_w1(1, 1, nobar=False), load_w2(1, 1, nobar=False)))),
        ("FB11", lambda: phase_B("FB11", 22, w2v, 1, 8, 2, xa, out, latc, rowf)),
    ]
    streams = {"FB00": xa, "G2": xb, "FB01": xa, "FB10": xb, "ATT": xa}
    for name, fn in sched:
        if stop:
            break
        stop = fn()
        if name in streams:
            cur = streams[name]
        elif name != "FB11":
            cur = None if name in ("FA00",) else cur

    if stop_after == "P0":
        DMA("sp", "dbgm", out[0:36, :].rearrange("(a r) c -> a (r c)", a=4), mrow_d.rearrange("l c n -> (l c) n"), reads=["mrow_d"], writes=[("out", 0)])
        DMA("sp", "dbgc", out[128:256, 0:288], mcol[:].rearrange("p l a b -> p (l a b)"), reads=["mcol"], writes=[("out", 1)])
    if stop_after == "FA00":
        DMA("pool", "dbgh", out[0:384, :].rearrange("(p r) c -> p (r c)", r=3), hT_d[0].rearrange("p f t -> p (f t)"), writes=[("out", 0)])
    if stop and cur is not None:
        for i in range(8):
            DMA("sp", f"dbg{i}", out[i * 512:(i + 1) * 512, :], cur[i * 512:(i + 1) * 512, :], writes=[("out", i)])
    P.add("sp", lambda e: None, reads=[("out", i) for i in range(8)] + [("x", id(out), c) for c in range(NLAT)])

    with nc.Block() as block:
        P.emit_all(nc, block, lambda name: root.enter_context(nc.semaphore(name)))
    root.close()
    return nc, P


def _prep_inputs(inp):
    f = lambda a: np.ascontiguousarray(np.asarray(a, dtype=np.float32))
    x, c, ctx, c_ctx = f(inp["x"]), f(inp["c"]), f(inp["ctx"]), f(inp["c_ctx"])
    shared = dict(
        ident=np.eye(128, dtype=np.float32),
        w_mod=f(inp["w_mod"]),
        bmod2=f(np.broadcast_to(f(inp["b_mod"])[None], (2, 2, 9216))),
        ln_g=f(inp["ln_g"]), ln_b=f(inp["ln_b"]),
        ffn_w_in=f(inp["ffn_w_in"]), ffn_w_out=f(inp["ffn_w_out"]),
        gm_w_in=f(inp["gmlp_w_in"][0]),
        glng=f(f(inp["gmlp_ln_g"])[0].reshape(24, 128).T),
        glnb=f(f(inp["gmlp_ln_b"])[0].reshape(24, 128).T),
        wsT=f(f(inp["gmlp_w_s"])[0].transpose(2, 0, 1)),
        gbs=f(f(inp["gmlp_b_s"])[0].reshape(-1)),
        gm_w_out=f(inp["gmlp_w_out"][0]),
        w_qkv=f(inp["attn_w_qkv"][0]),
        qn=f(inp["attn_q_norm"][0]), kn=f(inp["attn_k_norm"][0]),
        w_o=f(inp["attn_w_o"][0]),
    )
    quarter = 32
    fr = (np.float32(10000.0) ** (-np.arange(quarter, dtype=np.float32) / np.float32(quarter))).astype(np.float32)
    shared["freq2"] = f(np.broadcast_to(np.concatenate([fr, fr])[None], (128, 64)))
    in_maps = []
    for i in range(8):
        b, j = i // 4, i % 4
        m = dict(shared)
        m["xin"] = f(np.concatenate([x[b, j * 4096:(j + 1) * 4096], ctx[b]], axis=0))
        cd = np.stack([c[b], c_ctx], axis=-1)
        m["cond"] = f(cd.reshape(8, 128, 2).transpose(1, 0, 2))
        t = j * 4096 + np.arange(4096)
        rc = np.stack([t // 64, t % 64], axis=-1).astype(np.float32)
        m["posrc"] = f(rc.reshape(32, 128, 2).transpose(1, 0, 2))
        in_maps.append(m)
    return in_maps


_CACHE = {}


def run(inp, stop_after=None, trace=False):
    key = stop_after
    if key not in _CACHE:
        _CACHE[key] = build(stop_after)
    nc, P = _CACHE[key]
    in_maps = _prep_inputs(inp)
    res = run_bass_kernel_spmd(nc, in_maps, core_ids=list(range(8)), trace=trace)
    outs = [np.asarray(r["out"], dtype=np.float32) for r in res.results]
    full = np.stack([np.concatenate(outs[0:4], axis=0), np.concatenate(outs[4:8], axis=0)], axis=0)
    return full, res


def kernel(**inputs):
    full, _ = run(inputs)
    return full
```

```python
from contextlib import ExitStack
import math
import os
import numpy as np
G1MODE = int(os.environ.get('G1MODE', '9'))
G1GELU = int(os.environ.get('G1GELU', '1'))
import concourse.bass as bass
import concourse.mybir as mybir
from concourse.bass_utils import run_bass_kernel_spmd

F32 = mybir.dt.float32
BF16 = mybir.dt.bfloat16
I32 = mybir.dt.int32
AF = mybir.ActivationFunctionType
ALU = mybir.AluOpType
AX = mybir.AxisListType

ENGS = ["pe", "act", "dve", "pool", "sp"]
SEM_LIMIT = 60000


class Prog:
    def __init__(self):
        self.ops = []
        self.last_w = {}
        self.readers = {}
        self.bar_from = 0

    def add(self, eng, emit, reads=(), writes=(), dma=None, ndma=1, inc=16, nobar=False):
        i = len(self.ops)
        deps = {}
        for r in reads:
            d = self.last_w.get(r)
            if d is not None:
                deps[d] = "raw"
        for w in writes:
            d = self.last_w.get(w)
            if d is not None:
                deps.setdefault(d, "waw")
            for rd in self.readers.get(w, ()):
                if rd != i:
                    deps.setdefault(rd, "war")
        for r in reads:
            self.readers.setdefault(r, []).append(i)
        for w in writes:
            self.last_w[w] = i
            self.readers[w] = []
        self.ops.append(dict(eng=eng, emit=emit, deps=deps, dma=dma, ndma=ndma, inc=inc, nobar=nobar))
        return i

    def barrier(self):
        lo, hi = self.bar_from, len(self.ops)
        last = {}
        for i in range(lo, hi):
            op = self.ops[i]
            if op["nobar"]:
                continue
            if op["dma"] is not None:
                last[("dma", op["dma"])] = i
            else:
                last[("eng", op["eng"])] = i
        deps = {i: "raw" for i in last.values()}
        for e in ENGS:
            self.ops.append(dict(eng=e, emit=lambda e_: None, deps=dict(deps), dma=None, ndma=1, inc=16,
                                 nobar=True, isbar=True))
        self.bar_from = len(self.ops)

    def _needs_wait(self, op, dop, kind):
        if dop["dma"] is not None:
            return True
        if dop["eng"] != op["eng"]:
            return True
        if op["dma"] is not None or op.get("isbar"):
            return True
        if op["eng"] == "pe":
            return False
        return kind == "raw"

    def emit_all(self, nc, block, sems):
        ops = self.ops
        n = len(ops)
        need_sig = [False] * n
        for op in ops:
            for d, kind in op["deps"].items():
                dop = ops[d]
                if dop["dma"] is None and self._needs_wait(op, dop, kind):
                    need_sig[d] = True
        cnt = {e: 0 for e in ENGS}
        dcnt = {}
        for i, op in enumerate(ops):
            if op["dma"] is not None:
                k = op["dma"]
                dcnt[k] = dcnt.get(k, 0) + op["inc"] * op["ndma"]
                op["dval"] = dcnt[k]
            elif need_sig[i]:
                cnt[op["eng"]] += 1
                op["sig"] = cnt[op["eng"]]
        self.stats = dict(cnt=cnt, dma_keys=len(dcnt), nops=n, maxdma=max(dcnt.values()) if dcnt else 0)
        assert max(cnt.values()) < SEM_LIMIT, cnt
        assert self.stats["maxdma"] < SEM_LIMIT, self.stats
        eng_sem = {e: sems(f"prog_{e}") for e in ENGS if cnt[e] > 0}
        dma_sem = {k: sems(f"dma_{k}") for k in dcnt}

        def run_engine(ename, e):
            seen = {}
            for i, op in enumerate(ops):
                if op["eng"] != ename:
                    continue
                waits = {}
                for d, kind in op["deps"].items():
                    dop = ops[d]
                    if not self._needs_wait(op, dop, kind):
                        continue
                    if dop["dma"] is not None:
                        s, v = dma_sem[dop["dma"]], dop["dval"]
                    else:
                        s, v = eng_sem[dop["eng"]], dop["sig"]
                    key = id(s)
                    if waits.get(key, (None, 0))[1] < v:
                        waits[key] = (s, v)
                for key, (s, v) in waits.items():
                    if seen.get(key, 0) < v:
                        e.wait_ge(s, v)
                        seen[key] = v
                r = op["emit"](e)
                if op["dma"] is not None:
                    insts = r if isinstance(r, (list, tuple)) else [r]
                    assert len(insts) == op["ndma"], (len(insts), op["ndma"], op["dma"])
                    for ins in insts:
                        ins.then_inc(dma_sem[op["dma"]], op["inc"])
                elif need_sig[i]:
                    assert r is not None, f"op {i} on {ename} must return an instruction"
                    r.then_inc(eng_sem[ename], 1)

        @block.tensor
        def _(e):
            run_engine("pe", e)

        @block.scalar
        def _(e):
            run_engine("act", e)

        @block.vector
        def _(e):
            run_engine("dve", e)

        @block.gpsimd
        def _(e):
            run_engine("pool", e)

        @block.sync
        def _(e):
            run_engine("sp", e)


D = 1024
NCH = 34
NLAT = 32
DFF = 2816
ALPHA = 4.0 ** 0.25
LN_EPS = 1e-5
RMS_EPS = 1e-6
SCALE = 128.0 ** -0.5
NKC = 130
GROUPS = [[0, 1, 2, 3], [4, 5, 6, 7]]
PHASES = ["P0", "FA00", "FB00", "G1", "G2", "FA01", "FB01", "FA10", "FB10", "QKV", "ATT", "FA11", "FB11"]


def build(stop_after=None):
    nc = bass.Bass("TRN2", target_bir_lowering=False)

    def din(name, shape, dt=F32):
        return nc.dram_tensor(name, list(shape), dt, kind="ExternalInput").ap()

    def dscr(name, shape, dt):
        return nc.dram_tensor(name, list(shape), dt).ap()

    xin = din("xin", [NCH * 128, D])
    cond = din("cond", [128, 8, 2])
    posrc = din("posrc", [128, NLAT, 2])
    freq2 = din("freq2", [128, 64])
    identd = din("ident", [128, 128])
    w_mod = din("w_mod", [2, D, 9 * D])
    bmod2 = din("bmod2", [2, 2, 9 * D])
    ln_g = din("ln_g", [2, 3, D])
    ln_b = din("ln_b", [2, 3, D])
    ffn_w_in = din("ffn_w_in", [2, 2, D, 2 * DFF])
    ffn_w_out = din("ffn_w_out", [2, 2, DFF, D])
    gm_w_in = din("gm_w_in", [D, 6144])
    glng = din("glng", [128, 24])
    glnb = din("glnb", [128, 24])
    wsT_d = din("wsT", [128, 8, 128])
    gbs = din("gbs", [8 * 128])
    gm_w_out = din("gm_w_out", [3072, D])
    w_qkv = din("w_qkv", [D, 1536])
    qn_d = din("qn", [128])
    kn_d = din("kn", [128])
    w_o_d = din("w_o", [D, D])
    out = nc.dram_tensor("out", [NLAT * 128, D], F32, kind="ExternalOutput").ap()

    xa = dscr("xa", [NCH * 128, D], F32)
    xb = dscr("xb", [NCH * 128, D], F32)
    hT_d = dscr("hT_d", [NCH, 128, 24, 128], BF16)
    qT_d = dscr("qT_d", [NLAT, 128, 1024], BF16)
    kin = [nc.dram_tensor(f"kin{i}", [128, 4096], BF16) for i in range(2)]
    vin = [nc.dram_tensor(f"vin{i}", [128, 4096], BF16) for i in range(2)]
    kout = [nc.dram_tensor(f"kout{i}", [512, 4096], BF16) for i in range(2)]
    vout = [nc.dram_tensor(f"vout{i}", [512, 4096], BF16) for i in range(2)]
    kc_d = dscr("kc_d", [2, 128, 256], BF16)
    vc_d = dscr("vc_d", [256, 256], BF16)
    mrow_d = dscr("mrow_d", [2, 2, 9 * D], F32)

    P = Prog()
    root = ExitStack()

    uid = [0]

    def sbuf(es, name, shape, dt):
        uid[0] += 1
        return es.enter_context(nc.sbuf_tensor(f"{name}_{uid[0]}", list(shape), dt))

    def psum(es, name, shape, dt):
        uid[0] += 1
        return es.enter_context(nc.psum_tensor(f"{name}_{uid[0]}", list(shape), dt))

    ident_f = sbuf(root, "ident_f", [128, 128], F32)
    ident_b = sbuf(root, "ident_b", [128, 128], BF16)
    mcol = sbuf(root, "mcol", [128, 2, 72, 2], F32)
    BIG = sbuf(root, "BIG", [128, 77824], BF16)
    R1 = BIG[:, 0:49152]
    R2 = BIG[:, 49152:77824]

    def DMA(eng, key, out_, in_, reads=(), writes=(), nobar=False):
        return P.add(eng, lambda e: e.dma_start(out=out_, in_=in_), reads=reads, writes=writes, dma=key, nobar=nobar)

    DMA("sp", "c_ident", ident_f[:], identd, writes=["ident_f"])
    P.add("dve", lambda e: e.tensor_copy(out=ident_b[:], in_=ident_f[:]), reads=["ident_f"], writes=["ident_b"])

    def done(ph):
        P.barrier()
        return stop_after == ph

    def load_w1(layer, j, nobar=True):
        w1 = R1[:, 0:8 * 5632].rearrange("p (k f) -> p k f", k=8)
        src = ffn_w_in[layer, j].rearrange("(k p) f -> p k f", p=128)
        pieces = [(0, 2048), (2048, 4096), (4096, 5632)]

        def emit(e):
            r = []
            for kc in range(8):
                for lo, hi in pieces:
                    r.append(e.dma_start(out=w1[:, kc, lo:hi], in_=src[:, kc, lo:hi]))
            return r
        P.add("pool", emit, writes=["R1"], dma="wR1", ndma=24, nobar=nobar)
        return w1

    def load_w2(layer, j, nobar=True):
        w2 = R2[:, 0:22 * 1024].rearrange("p (f d) -> p f d", f=22)
        src = ffn_w_out[layer, j].rearrange("(f p) d -> p f d", p=128)

        def emit(e):
            return [e.dma_start(out=w2[:, f, :], in_=src[:, f, :]) for f in range(22)]
        P.add("pool", emit, writes=["R2"], dma="wR2", ndma=22, nobar=nobar)
        return w2

    def prep_chunk(xc_t, xc_res, trp, trp_res, xmT, xm_res, col0, layer, shift_mod, scale_mod, cnd):
        def pe(e):
            r = None
            for kc in range(8):
                r = e.transpose(out=trp[:, kc, :], in_=xc_t[:, kc * 128:(kc + 1) * 128], identity=ident_f[:])
            return r
        P.add("pe", pe, reads=[xc_res, "ident_f"], writes=[trp_res])
        for kc in range(8):
            P.add("act", lambda e, kc=kc: e.activation(
                out=xmT[:, kc, col0:col0 + 128], in_=trp[:, kc, :], func=AF.Identity,
                scale=mcol[:, layer, scale_mod * 8 + kc, cnd:cnd + 1],
                bias=mcol[:, layer, shift_mod * 8 + kc, cnd:cnd + 1]),
                reads=[trp_res, "mcol"], writes=[xm_res])

    class Tail:
        def __init__(self, es, layer, gate_mod, ln_idx, nslots=2, conds=(0, 1)):
            self.gate = [sbuf(es, f"gate_bc{c}", [128, D], F32) if c in conds else None for c in range(2)]
            self.g_bc = sbuf(es, "g_bc", [128, D], F32)
            self.b_bc = sbuf(es, "b_bc", [128, D], F32)
            self.t1 = [sbuf(es, f"t1_{s}", [128, D], F32) for s in range(nslots)]
            self.st = [sbuf(es, f"st_{s}", [128, 2, 6], F32) for s in range(nslots)]
            self.mv = [sbuf(es, f"mv_{s}", [128, 2], F32) for s in range(nslots)]
            self.rs = [sbuf(es, f"rs_{s}", [128, 2], F32) for s in range(nslots)]
            self.n = nslots
            self.epsc = sbuf(es, "epsc", [128, 1], F32)
            P.add("pool", lambda e: e.memset(self.epsc[:], LN_EPS), writes=["epsc"])
            for c in conds:
                DMA("sp", f"gatebc{c}", self.gate[c][:],
                    mrow_d[layer, c, gate_mod * D:(gate_mod + 1) * D].partition_broadcast(128),
                    reads=["mrow_d"], writes=[f"gate_bc{c}"])
            DMA("sp", "gbc", self.g_bc[:], ln_g[layer, ln_idx].partition_broadcast(128), writes=["g_bc"])
            DMA("sp", "bbc", self.b_bc[:], ln_b[layer, ln_idx].partition_broadcast(128), writes=["b_bc"])

        def piece(self, s, yap, yres, lo, hi, cnd):
            t1 = self.t1[s]
            P.add("dve", lambda e: e.tensor_tensor(out=t1[:, lo:hi], in0=yap, in1=self.gate[cnd][:, lo:hi], op=ALU.mult),
                  reads=[yres, f"gate_bc{cnd}"], writes=[f"t1_{s}"])

        def rest(self, s, xold, xres, dst, dst_res, before_store=None):
            t1, st, mv, rs = self.t1[s], self.st[s], self.mv[s], self.rs[s]
            r = f"t1_{s}"
            P.add("dve", lambda e: e.scalar_tensor_tensor(out=t1[:], in0=xold, scalar=ALPHA, in1=t1[:],
                                                          op0=ALU.mult, op1=ALU.add), reads=[xres, r], writes=[r])

            def stats(e):
                e.bn_stats(out=st[:, 0, :], in_=t1[:, 0:512])
                return e.bn_stats(out=st[:, 1, :], in_=t1[:, 512:1024])
            P.add("dve", stats, reads=[r], writes=[f"st_{s}"])
            P.add("dve", lambda e: e.bn_aggr(out=mv[:], in_=st[:].rearrange("p a b -> p (a b)")),
                  reads=[f"st_{s}"], writes=[f"mv_{s}"])
            P.add("act", lambda e: e.activation(out=rs[:, 0:1], in_=mv[:, 1:2], func=AF.Sqrt, bias=self.epsc[:, 0:1], scale=1.0),
                  reads=[f"mv_{s}", "epsc"], writes=[f"rsa_{s}"])
            P.add("dve", lambda e: e.reciprocal(out=rs[:, 0:1], in_=rs[:, 0:1]), reads=[f"rsa_{s}"], writes=[f"rsa_{s}"])
            P.add("dve", lambda e: e.tensor_scalar(out=rs[:, 1:2], in0=mv[:, 0:1], scalar1=rs[:, 0:1], scalar2=-1.0,
                                                   op0=ALU.mult, op1=ALU.mult), reads=[f"mv_{s}", f"rsa_{s}"], writes=[f"rsb_{s}"])
            P.add("act", lambda e: e.activation(out=t1[:], in_=t1[:], func=AF.Identity, scale=rs[:, 0:1], bias=rs[:, 1:2]),
                  reads=[r, f"rsa_{s}", f"rsb_{s}"], writes=[r])
            P.add("pool", lambda e: e.tensor_tensor(out=t1[:], in0=t1[:], in1=self.g_bc[:], op=ALU.mult),
                  reads=[r, "g_bc"], writes=[r])
            P.add("pool", lambda e: e.tensor_tensor(out=t1[:], in0=t1[:], in1=self.b_bc[:], op=ALU.add),
                  reads=[r, "b_bc"], writes=[r])
            if before_store:
                before_store()
            DMA("sp", f"st_t1_{s}", dst, t1[:], reads=[r], writes=[dst_res])

    with ExitStack() as es:
        scf = sbuf(es, "scf", [128, 8, 2], F32)
        scb = sbuf(es, "scb", [128, 8, 2], BF16)
        mrow_ = R2[0:2, 0:18432].bitcast(F32)
        mrow = [mrow_, mrow_]
        wblk = [sbuf(es, f"wblk{s}", [128, 8, 512], BF16) for s in range(3)]
        brow = [sbuf(es, f"brow{s}", [2, 512], F32) for s in range(3)]
        mps = [psum(es, f"mps{s}", [128, 512], F32) for s in range(2)]
        tps = psum(es, "tps0", [128, 512], F32)
        DMA("sp", "scf", scf[:], cond, writes=["scf"])
        P.add("act", lambda e: e.activation(out=scb[:], in_=scf[:], func=AF.Silu), reads=["scf"], writes=["scb"])
        for layer in range(2):
            for blk in range(18):
                s3, s2 = blk % 3, blk % 2
                DMA("pool", f"wblk{s3}", wblk[s3][:],
                    w_mod[layer, :, blk * 512:(blk + 1) * 512].rearrange("(k p) c -> p k c", p=128),
                    writes=[f"wblk{s3}"])
                DMA("sp", f"brow{s3}", brow[s3][:], bmod2[:, layer, blk * 512:(blk + 1) * 512], writes=[f"brow{s3}"])

                def pe(e, s3=s3, s2=s2):
                    r = None
                    for kc in range(8):
                        r = e.matmul(mps[s2][0:2, :], lhsT=scb[:, kc, :], rhs=wblk[s3][:, kc, :], start=(kc == 0), stop=(kc == 7))
                    return r
                P.add("pe", pe, reads=["scb", f"wblk{s3}"], writes=[f"mps{s2}"])
                P.add("dve", lambda e, s3=s3, s2=s2, blk=blk, layer=layer: e.tensor_tensor(
                    out=mrow[layer][:, blk * 512:(blk + 1) * 512], in0=mps[s2][0:2, :], in1=brow[s3][:], op=ALU.add),
                    reads=[f"mps{s2}", f"brow{s3}"], writes=["R2"])
            for m in (1, 4, 7):
                P.add("dve", lambda e, m=m, layer=layer: e.tensor_scalar(
                    out=mrow[layer][:, m * D:(m + 1) * D], in0=mrow[layer][:, m * D:(m + 1) * D], scalar1=1.0, scalar2=None, op0=ALU.add),
                    reads=["R2"], writes=["R2"])
            for m in (2, 8):
                P.add("dve", lambda e, m=m, layer=layer: e.tensor_scalar(
                    out=mrow[layer][:, m * D:(m + 1) * D], in0=mrow[layer][:, m * D:(m + 1) * D], scalar1=0.5, scalar2=None, op0=ALU.mult),
                    reads=["R2"], writes=["R2"])
            DMA("sp", f"mrowst{layer}", mrow_d[layer], mrow[layer][:], reads=["R2"], writes=["mrow_d"])

            def pe_t(e, layer=layer):
                r = None
                for jj in range(72):
                    r = e.transpose(out=tps[:, 2 * jj:2 * jj + 2], in_=mrow[layer][0:2, jj * 128:(jj + 1) * 128], identity=ident_f[0:2, 0:2])
                return r
            P.add("pe", pe_t, reads=["R2", "ident_f"], writes=["tps0"])
            P.add("dve", lambda e, layer=layer: e.tensor_copy(
                out=mcol[:, layer, :, :].rearrange("p a b -> p (a b)"), in_=tps[:, 0:144]), reads=["tps0"], writes=["mcol"])
        load_w1(0, 0)
        load_w2(0, 0)
        fin = done("P0")

    def phase_FA(layer, j, src, with_ctx, prefetch=None):
        shift_mod, scale_mod = (0, 1) if j == 0 else (6, 7)
        tiles = [(4 * t, 4, 0) for t in range(8)] + ([(32, 2, 1)] if with_ctx else [])
        w1 = R1[:, 0:8 * 5632].rearrange("p (k f) -> p k f", k=8)
        with ExitStack() as es:
            NX = 6
            xc = [sbuf(es, f"xc{s}", [128, D], F32) for s in range(NX)]
            xmT = [sbuf(es, f"xmT{s}", [128, 8, 512], BF16) for s in range(2)]
            sg = [sbuf(es, f"sg{s}", [128, 512], F32) for s in range(2)]
            hto = [sbuf(es, f"hto{s}", [128, 512], BF16) for s in range(4)]
            trp = [psum(es, f"trp{s}", [128, 8, 128], F32) for s in range(2)]
            gu = [psum(es, f"gu{s}", [128, 2, 512], F32) for s in range(2)]
            if prefetch:
                prefetch()
            chunks = [(c0 + ci, cnd) for (c0, n, cnd) in tiles for ci in range(n)]
            nload = [0]

            def load_next():
                if nload[0] < len(chunks):
                    c, _ = chunks[nload[0]]
                    s = nload[0] % NX
                    DMA("sp", f"xc{s}", xc[s][:], src[c * 128:(c + 1) * 128, :], reads=[("x", id(src), c)], writes=[f"xc{s}"])
                    nload[0] += 1
            for _ in range(NX):
                load_next()
            cidx = [0]

            def prep(ti):
                c0, n, cnd = tiles[ti]
                for ci in range(n):
                    k = cidx[0]
                    s = k % NX
                    prep_chunk(xc[s], f"xc{s}", trp[k % 2], f"trp{k % 2}", xmT[ti % 2], f"xmT{ti % 2}", ci * 128,
                               layer, shift_mod, scale_mod, cnd)
                    cidx[0] += 1
                    load_next()
            prep(0)
            for ti, (c0, n, cnd) in enumerate(tiles):
                NT = n * 128
                xm = xmT[ti % 2]
                for f in range(22):
                    b = f % 2

                    def pe(e, f=f, b=b, xm=xm, NT=NT):
                        r = None
                        for half in range(2):
                            col = half * DFF + f * 128
                            for kc in range(8):
                                r = e.matmul(gu[b][:, half, 0:NT], lhsT=w1[:, kc, col:col + 128], rhs=xm[:, kc, 0:NT],
                                             start=(kc == 0), stop=(kc == 7))
                        return r
                    P.add("pe", pe, reads=["R1", f"xmT{ti % 2}"], writes=[f"gu{b}"])
                    P.add("act", lambda e, b=b, NT=NT: e.activation(out=sg[b][:, 0:NT], in_=gu[b][:, 0, 0:NT], func=AF.Silu),
                          reads=[f"gu{b}"], writes=[f"sg{b}"])
                    hs = f % 4
                    P.add("dve", lambda e, b=b, hs=hs, NT=NT: e.tensor_tensor(out=hto[hs][:, 0:NT], in0=sg[b][:, 0:NT],
                                                                              in1=gu[b][:, 1, 0:NT], op=ALU.mult),
                          reads=[f"sg{b}", f"gu{b}"], writes=[f"hto{hs}"])
                    DMA("sp", f"hto{hs}", hT_d[c0:c0 + n, :, f, :].rearrange("c p t -> p c t"),
                        hto[hs][:, 0:NT].rearrange("p (c t) -> p c t", t=128),
                        reads=[f"hto{hs}"], writes=[("hT", c0, f)])
                    if f == 8 and ti + 1 < len(tiles):
                        prep(ti + 1)
            return done(f"FA{layer}{j}")

    def phase_B(name, nf, w2, layer, gate_mod, ln_idx, xsrc, dst, chunk_list, dst_rows, prefetch=None):
        with ExitStack() as es:
            tl = Tail(es, layer, gate_mod, ln_idx)
            xc = [sbuf(es, f"xc{s}", [128, D], F32) for s in range(2)]
            hin = [sbuf(es, f"hin{s}", [128, nf, 128], BF16) for s in range(2)]
            yps = [psum(es, f"yps{s}", [128, D], F32) for s in range(2)]
            if prefetch:
                prefetch()
            nl = [0]

            def load_next():
                if nl[0] < len(chunk_list):
                    c = chunk_list[nl[0]]
                    s = nl[0] % 2
                    DMA("sp", f"xc{s}", xc[s][:], xsrc[c * 128:(c + 1) * 128, :], reads=[("x", id(xsrc), c)], writes=[f"xc{s}"])
                    DMA("sp", f"hin{s}", hin[s][:], hT_d[c, :, 0:nf, :], reads=[("hT", c)], writes=[f"hin{s}"])
                    nl[0] += 1
            for _ in range(2):
                load_next()
            for k, c in enumerate(chunk_list):
                s, b = k % 2, k % 2
                cnd = 0 if c < NLAT else 1
                for half in range(2):
                    def pe(e, s=s, b=b, half=half):
                        r = None
                        for f in range(nf):
                            r = e.matmul(yps[b][:, half * 512:(half + 1) * 512], lhsT=hin[s][:, f, :],
                                         rhs=w2[:, f, half * 512:(half + 1) * 512], start=(f == 0), stop=(f == nf - 1))
                        return r
                    P.add("pe", pe, reads=["R2", f"hin{s}"], writes=[(f"yps{b}", half)])
                    tl.piece(b, yps[b][:, half * 512:(half + 1) * 512], (f"yps{b}", half), half * 512, (half + 1) * 512, cnd)
                r0 = dst_rows(c)
                tl.rest(b, xc[s][:], f"xc{s}", dst[r0:r0 + 128, :], ("x", id(dst), c), before_store=load_next)
            return done(name)


    def gen_loader(name_prefix, nslots, slots, chunk_ids, src, res_prefix):
        st_ = [0]

        def load_next():
            if st_[0] < len(chunk_ids):
                c = chunk_ids[st_[0]]
                s_ = st_[0] % nslots
                DMA("sp", f"{res_prefix}{s_}", slots[s_][:], src[c * 128:(c + 1) * 128, :],
                    reads=[("x", id(src), c)], writes=[f"{res_prefix}{s_}"])
                st_[0] += 1
        return load_next

    def phase_G1(src, prefetch=None):
        layer, shift_mod, scale_mod = 0, 3, 4
        tiles = [(2 * t, 2, 0) for t in range(16)] + [(32, 2, 1)]
        w_in = R1.rearrange("p (k f) -> p k f", k=8)
        with ExitStack() as es:
            NX = 4
            xc = [sbuf(es, "xc", [128, D], F32) for s_ in range(NX)]
            xmT = [sbuf(es, "xmT", [128, 8, 256], BF16) for s_ in range(2)]
            vh = sbuf(es, "vh", [128, 3072], BF16)
            hho = [sbuf(es, "hho", [128, 24, 128], BF16) for s_ in range(2)]
            wsT = sbuf(es, "wsT", [128, 8, 128], BF16)
            bs_bc = sbuf(es, "bs_bc", [128, 8, 128], F32)
            tmp = [sbuf(es, "tmp", [128, 128], F32) for s_ in range(2)]
            gcol = sbuf(es, "gcol", [128, 24], F32)
            bcol = sbuf(es, "bcol", [128, 24], F32)
            ones_b = sbuf(es, "ones_b", [128, 128], BF16)
            st6 = sbuf(es, "st6", [128, 6, 6], F32)
            mv = sbuf(es, "mvg", [128, 2], F32)
            rs = sbuf(es, "rsg", [128, 2], F32)
            epsc = sbuf(es, "epsg", [128, 1], F32)
            uT = [R2[:, i * 6144:(i + 1) * 6144].rearrange("p (f t) -> p f t", f=24) for i in range(2)]
            v_sb = R2[:, 12288:18432].bitcast(F32)
            Bt = R2[:, 18432:24576].bitcast(F32).rearrange("p (f t) -> p f t", f=24)
            trp = psum(es, "trp", [128, 8, 128], F32)
            ups_ = [psum(es, "ups", [128, 512], F32) for s_ in range(2)]
            vps = [psum(es, "vps", [128, 512], F32) for s_ in range(2)]
            sps = [psum(es, "sps", [128, 4, 128], F32) for s_ in range(2)]
            if prefetch:
                prefetch()
            DMA("pool", "wsT", wsT[:], wsT_d, writes=["wsT"])
            DMA("sp", "gcol", gcol[:], glng, writes=["gcol"])
            DMA("sp", "bcol", bcol[:], glnb, writes=["bcol"])
            DMA("sp", "bsbc", bs_bc[:].rearrange("p g t -> p (g t)"), gbs.partition_broadcast(128), writes=["bs_bc"])
            P.add("pool", lambda e: e.memset(ones_b[:], 1.0), writes=["ones_b"])
            P.add("pool", lambda e: e.memset(epsc[:], LN_EPS), writes=["epsg"])
            for g in range(8):
                P.add("pe", lambda e, g=g: e.matmul(sps[g // 4][:, g % 4, :], lhsT=ones_b[:], rhs=wsT[:, g, :], start=True, stop=True),
                      reads=["ones_b", "wsT"], writes=[f"sps{g // 4}"])
            for cc in range(24):
                g = cc // 3
                P.add("dve", lambda e, cc=cc, g=g: e.scalar_tensor_tensor(
                    out=Bt[:, cc, :], in0=sps[g // 4][:, g % 4, :], scalar=bcol[:, cc:cc + 1], in1=bs_bc[:, g, :],
                    op0=ALU.mult, op1=ALU.add), reads=[f"sps{g // 4}", "bcol", "bs_bc"], writes=["Bt"])
            chunks = [c0 + ci for (c0, n, cnd) in tiles for ci in range(n)]
            load_next = gen_loader("xc", NX, xc, chunks, src, "xc")
            for _ in range(NX):
                load_next()
            kidx = [0]

            def prep(ti):
                c0, n, cnd = tiles[ti]
                for ci in range(n):
                    k = kidx[0]
                    prep_chunk(xc[k % NX], f"xc{k % NX}", trp, "trp", xmT[ti % 2], f"xmT{ti % 2}", ci * 128,
                               layer, shift_mod, scale_mod, cnd)
                    kidx[0] += 1
                    load_next()
            prep(0)

            def emit_uT(ti, lo, hi):
                xm_ = xmT[ti % 2]
                uu = uT[ti % 2]
                for cc in range(lo, hi):
                    def pe(e, cc=cc, xm_=xm_):
                        r = None
                        for kc in range(8):
                            r = e.matmul(ups_[cc % 2][:, 0:256], lhsT=w_in[:, kc, cc * 128:(cc + 1) * 128], rhs=xm_[:, kc, :],
                                         start=(kc == 0), stop=(kc == 7))
                        return r
                    P.add("pe", pe, reads=["R1", f"xmT{ti % 2}"], writes=[f"ups{cc % 2}"])
                    P.add("act", lambda e, cc=cc, uu=uu: e.activation(out=uu[:, cc, :], in_=ups_[cc % 2][:, 0:256], func=AF.Gelu),
                          reads=[f"ups{cc % 2}"], writes=[f"uT{ti % 2}"])
            emit_uT(0, 0, 24)
            for ti, (c0, n, cnd) in enumerate(tiles):
                xm = xmT[ti % 2]
                u_ = uT[ti % 2]
                has_next = ti + 1 < len(tiles)
                if has_next:
                    prep(ti + 1)
                for ci in range(n):
                    c = c0 + ci
                    for blk in range(6):
                        def pe(e, blk=blk, xm=xm, ci=ci):
                            r = None
                            for kc in range(8):
                                r = e.matmul(vps[blk % 2][:, :], lhsT=xm[:, kc, ci * 128:(ci + 1) * 128],
                                             rhs=w_in[:, kc, 3072 + blk * 512:3072 + (blk + 1) * 512], start=(kc == 0), stop=(kc == 7))
                            return r
                        P.add("pe", pe, reads=["R1", f"xmT{ti % 2}"], writes=[f"vps{blk % 2}"])
                        P.add("act", lambda e, blk=blk: e.activation(out=v_sb[:, blk * 512:(blk + 1) * 512], in_=vps[blk % 2][:, :], func=AF.Gelu),
                              reads=[f"vps{blk % 2}"], writes=["v_sb"])

                    def stats(e):
                        r = None
                        for q in range(6):
                            r = e.bn_stats(out=st6[:, q, :], in_=v_sb[:, q * 512:(q + 1) * 512])
                        return r
                    P.add("dve", stats, reads=["v_sb"], writes=["st6"])
                    P.add("dve", lambda e: e.bn_aggr(out=mv[:], in_=st6[:].rearrange("p a b -> p (a b)")), reads=["st6"], writes=["mvg"])
                    P.add("act", lambda e: e.activation(out=rs[:, 0:1], in_=mv[:, 1:2], func=AF.Sqrt, bias=epsc[:, 0:1], scale=1.0),
                          reads=["mvg", "epsg"], writes=["rsga"])
                    P.add("dve", lambda e: e.reciprocal(out=rs[:, 0:1], in_=rs[:, 0:1]), reads=["rsga"], writes=["rsga"])
                    P.add("dve", lambda e: e.tensor_scalar(out=rs[:, 1:2], in0=mv[:, 0:1], scalar1=rs[:, 0:1], scalar2=-1.0,
                                                           op0=ALU.mult, op1=ALU.mult), reads=["mvg", "rsga"], writes=["rsgb"])
                    P.add("act", lambda e: e.activation(out=vh[:], in_=v_sb[:], func=AF.Identity, scale=rs[:, 0:1], bias=rs[:, 1:2]),
                          reads=["v_sb", "rsga", "rsgb"], writes=["vh"])
                    usched = [3, 2, 2, 2, 2, 1]
                    unext = [ci * 12]
                    for cg in range(6):
                        if has_next:
                            emit_uT(ti + 1, unext[0], unext[0] + usched[cg])
                            unext[0] += usched[cg]

                        def pe(e, cg=cg):
                            r = None
                            for i in range(4):
                                cc = cg * 4 + i
                                r = e.matmul(sps[cg % 2][:, i, :], lhsT=vh[:, cc * 128:(cc + 1) * 128], rhs=wsT[:, cc // 3, :],
                                             start=True, stop=True)
                            return r
                        P.add("pe", pe, reads=["vh", "wsT"], writes=[f"sps{cg % 2}"])
                        for i in range(4):
                            cc = cg * 4 + i
                            P.add("dve", lambda e, cc=cc, cg=cg, i=i: e.scalar_tensor_tensor(
                                out=tmp[cc % 2][:], in0=sps[cg % 2][:, i, :], scalar=gcol[:, cc:cc + 1], in1=Bt[:, cc, :],
                                op0=ALU.mult, op1=ALU.add), reads=[f"sps{cg % 2}", "gcol", "Bt"], writes=[f"tmp{cc % 2}"])
                            P.add("pool", lambda e, cc=cc, c=c, ci=ci, u_=u_: e.tensor_tensor(
                                out=hho[c % 2][:, cc, :], in0=tmp[cc % 2][:], in1=u_[:, cc, ci * 128:(ci + 1) * 128], op=ALU.mult),
                                reads=[f"tmp{cc % 2}", f"uT{ti % 2}"], writes=[f"hho{c % 2}"])
                    DMA("sp", f"hho{c % 2}", hT_d[c], hho[c % 2][:], reads=[f"hho{c % 2}"], writes=[("hT", c)])
            return done("G1")

    def phase_QKV(src, prefetch=None):
        layer, shift_mod, scale_mod = 1, 3, 4
        wq = R1[:, 0:8 * 1536].rearrange("p (k f) -> p k f", k=8)
        with ExitStack() as es:
            NX = 3
            xc = [sbuf(es, "xc", [128, D], F32) for s_ in range(NX)]
            xmT = [sbuf(es, "xmT", [128, 8, 128], BF16) for s_ in range(2)]
            sq = [sbuf(es, "sq", [128, 512], F32) for s_ in range(3)]
            ss = sbuf(es, "ss", [128, 12], F32)
            rst = sbuf(es, "rst", [128, 12], F32)
            qn = sbuf(es, "qnb", [128, 10, 128], F32)
            ra = sbuf(es, "ra", [128, 10, 64], F32)
            rb = sbuf(es, "rb", [128, 10, 64], F32)
            rc = sbuf(es, "rc", [128, 10, 64], F32)
            rd = sbuf(es, "rd", [128, 10, 64], F32)
            qr = sbuf(es, "qr", [128, 10, 128], BF16)
            cosT = R2[:, 12288:16384].bitcast(F32).rearrange("p (c j) -> p c j", c=NLAT)
            sinT = R2[:, 16384:20480].bitcast(F32).rearrange("p (c j) -> p c j", c=NLAT)
            ang = R2[:, 0:4096].bitcast(F32).rearrange("p (c j) -> p c j", c=NLAT)
            angi = R2[:, 4096:8192].bitcast(I32).rearrange("p (c j) -> p c j", c=NLAT)
            ang2 = R2[:, 8192:12288].bitcast(F32).rearrange("p (c j) -> p c j", c=NLAT)
            prc = sbuf(es, "prc", [128, NLAT, 2], F32)
            frq = sbuf(es, "frq", [128, 64], F32)
            gq_bc = sbuf(es, "gq_bc", [128, 128], F32)
            gk_bc = sbuf(es, "gk_bc", [128, 128], F32)
            qTs = [sbuf(es, "qTs", [128, 8, 128], BF16) for s_ in range(2)]
            kTs = [sbuf(es, "kTs", [128, 2, 128], BF16) for s_ in range(2)]
            vs = [sbuf(es, "vs", [128, 256], BF16) for s_ in range(2)]
            trp = psum(es, "trp", [128, 8, 128], F32)
            qkv = [psum(es, "qkv", [128, 512], F32) for s_ in range(3)]
            tq = psum(es, "tq", [128, 8, 128], BF16)
            tk = psum(es, "tk", [128, 8, 128], BF16)
            if prefetch:
                prefetch()

            def emit_w(e):
                srcw = w_qkv.rearrange("(k p) f -> p k f", p=128)
                return [e.dma_start(out=wq[:, kc, :], in_=srcw[:, kc, :]) for kc in range(8)]
            P.add("pool", emit_w, writes=["R1"], dma="wR1", ndma=8)
            DMA("sp", "prc", prc[:], posrc, writes=["prc"])
            DMA("sp", "frq", frq[:], freq2, writes=["frq"])
            DMA("sp", "gqbc", gq_bc[:], qn_d.partition_broadcast(128), writes=["gq_bc"])
            DMA("sp", "gkbc", gk_bc[:], kn_d.partition_broadcast(128), writes=["gk_bc"])
            for hh_ in range(2):
                P.add("dve", lambda e, hh_=hh_: e.tensor_tensor(
                    out=ang[:, :, hh_ * 32:(hh_ + 1) * 32], in0=prc[:, :, hh_:hh_ + 1].broadcast_to([128, NLAT, 32]),
                    in1=frq[:, hh_ * 32:(hh_ + 1) * 32].unsqueeze(1).broadcast_to([128, NLAT, 32]), op=ALU.mult),
                    reads=["prc", "frq"], writes=["ang"])
            P.add("dve", lambda e: e.tensor_scalar(out=ang[:], in0=ang[:], scalar1=1.0 / (2.0 * math.pi), scalar2=None, op0=ALU.mult),
                  reads=["ang"], writes=["ang"])
            P.add("dve", lambda e: e.tensor_copy(out=angi[:], in_=ang[:]), reads=["ang"], writes=["angi"])
            P.add("dve", lambda e: e.tensor_copy(out=ang2[:], in_=angi[:]), reads=["angi"], writes=["ang2"])
            P.add("dve", lambda e: e.tensor_tensor(out=ang[:], in0=ang[:], in1=ang2[:], op=ALU.subtract), reads=["ang", "ang2"], writes=["ang"])
            P.add("dve", lambda e: e.tensor_scalar(out=ang2[:], in0=ang[:], scalar1=-1.0, scalar2=None, op0=ALU.mult), reads=["ang"], writes=["ang2"])
            P.add("dve", lambda e: e.tensor_tensor(out=ang2[:], in0=ang2[:], in1=ang[:], op=ALU.max), reads=["ang", "ang2"], writes=["ang2"])
            P.add("act", lambda e: e.activation(out=sinT[:], in_=ang[:], func=AF.Sin, scale=math.pi), reads=["ang"], writes=["sinT"])
            P.add("dve", lambda e: e.tensor_scalar(out=ang2[:], in0=ang2[:], scalar1=-math.pi, scalar2=math.pi / 2, op0=ALU.mult, op1=ALU.add),
                  reads=["ang2"], writes=["ang2"])
            P.add("act", lambda e: e.activation(out=cosT[:], in_=ang2[:], func=AF.Sin), reads=["ang2"], writes=["cosT"])
            P.add("dve", lambda e: e.tensor_tensor(out=ang[:], in0=sinT[:], in1=sinT[:], op=ALU.mult), reads=["sinT"], writes=["ang"])
            P.add("dve", lambda e: e.scalar_tensor_tensor(out=sinT[:], in0=sinT[:], scalar=2.0, in1=cosT[:], op0=ALU.mult, op1=ALU.mult),
                  reads=["sinT", "cosT", "ang"], writes=["sinT"])
            P.add("dve", lambda e: e.tensor_scalar(out=cosT[:], in0=ang[:], scalar1=-2.0, scalar2=1.0, op0=ALU.mult, op1=ALU.add),
                  reads=["ang", "sinT"], writes=["cosT"])
            chunks = list(range(NCH))
            load_next = gen_loader("xc", NX, xc, chunks, src, "xc")
            for _ in range(NX):
                load_next()
            pending = [None]
            for k, c in enumerate(chunks):
                lat = c < NLAT
                cnd = 0 if lat else 1
                b = k % 2
                prep_chunk(xc[k % NX], f"xc{k % NX}", trp, "trp", xmT[b], f"xmT{b}", 0, layer, shift_mod, scale_mod, cnd)
                load_next()
                blks = (0, 1, 2) if lat else (2,)
                for blk in blks:
                    def pe(e, blk=blk, b=b):
                        r = None
                        for kc in range(8):
                            r = e.matmul(qkv[blk][:, :], lhsT=xmT[b][:, kc, :], rhs=wq[:, kc, blk * 512:(blk + 1) * 512],
                                         start=(kc == 0), stop=(kc == 7))
                        return r
                    P.add("pe", pe, reads=["R1", f"xmT{b}"], writes=[f"qkv{blk}"])
                    P.add("act", lambda e, blk=blk: e.activation(out=sq[blk][:], in_=qkv[blk][:, :], func=AF.Square),
                          reads=[f"qkv{blk}"], writes=[f"sq{blk}"])
                    P.add("dve", lambda e, blk=blk: e.tensor_reduce(out=ss[:, blk * 4:(blk + 1) * 4],
                                                                    in_=sq[blk][:].rearrange("p (h d) -> p h d", h=4), axis=AX.X, op=ALU.add),
                          reads=[f"sq{blk}"], writes=["ss"])
                P.add("dve", lambda e: e.tensor_scalar(out=rst[:], in0=ss[:], scalar1=1.0 / 128.0, scalar2=RMS_EPS, op0=ALU.mult, op1=ALU.add),
                      reads=["ss"], writes=["rst"])
                P.add("act", lambda e: e.activation(out=rst[:], in_=rst[:], func=AF.Sqrt), reads=["rst"], writes=["rst"])
                P.add("dve", lambda e: e.reciprocal(out=rst[:], in_=rst[:]), reads=["rst"], writes=["rst"])
                if pending[0] is not None:
                    pending[0]()
                    pending[0] = None
                heads = list(range(10)) if lat else [8, 9]
                for h in heads:
                    blk, off = (h // 4, (h % 4) * 128) if h < 8 else (2, (h - 8) * 128)
                    gb = gq_bc if h < 8 else gk_bc
                    P.add("dve", lambda e, h=h, blk=blk, off=off, gb=gb: e.scalar_tensor_tensor(
                        out=qn[:, h, :], in0=qkv[blk][:, off:off + 128], scalar=rst[:, (blk * 4 + (off // 128)):(blk * 4 + (off // 128)) + 1],
                        in1=gb[:], op0=ALU.mult, op1=ALU.mult), reads=[f"qkv{blk}", "rst", "gq_bc", "gk_bc"], writes=["qn"])
                if lat:
                    x1, x2 = qn[:, :, 0:64], qn[:, :, 64:128]
                    cb = cosT[:, c, :].unsqueeze(1).broadcast_to([128, 10, 64])
                    sb_ = sinT[:, c, :].unsqueeze(1).broadcast_to([128, 10, 64])
                    P.add("pool", lambda e, x1=x1, cb=cb: e.tensor_tensor(out=ra[:], in0=x1, in1=cb, op=ALU.mult), reads=["qn", "cosT"], writes=["ra"])
                    P.add("pool", lambda e, x2=x2, sb_=sb_: e.tensor_tensor(out=rb[:], in0=x2, in1=sb_, op=ALU.mult), reads=["qn", "sinT"], writes=["rb"])
                    P.add("dve", lambda e: e.tensor_tensor(out=qr[:, :, 0:64], in0=ra[:], in1=rb[:], op=ALU.subtract), reads=["ra", "rb"], writes=["qr"])
                    P.add("pool", lambda e, x2=x2, cb=cb: e.tensor_tensor(out=rc[:], in0=x2, in1=cb, op=ALU.mult), reads=["qn", "cosT"], writes=["rc"])
                    P.add("pool", lambda e, x1=x1, sb_=sb_: e.tensor_tensor(out=rd[:], in0=x1, in1=sb_, op=ALU.mult), reads=["qn", "sinT"], writes=["rd"])
                    P.add("dve", lambda e: e.tensor_tensor(out=qr[:, :, 64:128], in0=rc[:], in1=rd[:], op=ALU.add), reads=["rc", "rd"], writes=["qr"])
                else:
                    P.add("dve", lambda e: e.tensor_copy(out=qr[:, 8:10, :], in_=qn[:, 8:10, :]), reads=["qn"], writes=["qr"])

                P.add("act", lambda e, b=b: e.activation(out=vs[b][:], in_=qkv[2][:, 256:512], func=AF.Copy), reads=["qkv2"], writes=[f"vs{b}"])

                def stage_b(c=c, lat=lat, b=b):
                  def pe_t(e, lat=lat):
                      r = None
                      if lat:
                          for h in range(8):
                              r = e.transpose(out=tq[:, h, :], in_=qr[:, h, :], identity=ident_b[:])
                      for kv in range(2):
                          r = e.transpose(out=tk[:, kv, :], in_=qr[:, 8 + kv, :], identity=ident_b[:])
                      return r
                  P.add("pe", pe_t, reads=["qr", "ident_b"], writes=["tq", "tk"])
                  if lat:
                      P.add("act", lambda e, b=b: e.activation(out=qTs[b][:], in_=tq[:], func=AF.Copy), reads=["tq"], writes=[f"qTs{b}"])
                      DMA("sp", f"qTs{b}", qT_d[c].rearrange("p (h t) -> p h t", h=8), qTs[b][:], reads=[f"qTs{b}"], writes=[("qT", c)])
                  P.add("dve", lambda e, b=b: e.tensor_copy(out=kTs[b][:], in_=tk[:, 0:2, :]), reads=["tk"], writes=[f"kTs{b}"])
                  if lat:
                      def st_k(e, b=b, c=c):
                          return [e.dma_start(out=kin[kv].ap()[:, c * 128:(c + 1) * 128], in_=kTs[b][:, kv, :]) for kv in range(2)]
                      P.add("sp", st_k, reads=[f"kTs{b}"], writes=["kin"], dma=f"kTs{b}", ndma=2)
                      hv, j16 = c // 16, c % 16
                      DMA("sp", f"vs{b}", vin[hv].ap()[j16 * 8:(j16 + 1) * 8, :].rearrange("a (b c) -> (a b) c", c=256), vs[b][:],
                          reads=[f"vs{b}"], writes=["vin"])
                  else:
                      cc_ = c - NLAT

                      def st_k(e, b=b, cc_=cc_):
                          return [e.dma_start(out=kc_d[kv, :, cc_ * 128:(cc_ + 1) * 128], in_=kTs[b][:, kv, :]) for kv in range(2)]
                      P.add("sp", st_k, reads=[f"kTs{b}"], writes=["kc_d"], dma=f"kTs{b}", ndma=2)
                      DMA("sp", f"vs{b}", vc_d[cc_ * 128:(cc_ + 1) * 128, :], vs[b][:], reads=[f"vs{b}"], writes=["vc_d"])
                pending[0] = stage_b
            pending[0]()
            P.barrier()
            for i in range(2):
                P.add("pool", lambda e, i=i: e.collective_compute("AllGather", ALU.bypass, replica_groups=GROUPS,
                                                                   ins=[kin[i].ap().opt()], outs=[kout[i].ap().opt()]),
                      reads=["kin"], writes=[("kout", i)], dma=f"cck{i}", inc=1)
                P.add("pool", lambda e, i=i: e.collective_compute("AllGather", ALU.bypass, replica_groups=GROUPS,
                                                                   ins=[vin[i].ap().opt()], outs=[vout[i].ap().opt()]),
                      reads=["vin"], writes=[("vout", i)], dma=f"ccv{i}", inc=1)
            return done("QKV")

    def phase_ATT(xsrc, dst):
        layer = 1
        KT = BIG[:, 0:33280].rearrange("p (k t) -> p k t", k=2)
        V = BIG[:, 33280:66560].rearrange("p (c k d) -> p c k d", c=NKC, k=2)
        wo = BIG[:, 66560:74752].rearrange("p (h d) -> p h d", h=8)
        NP = NKC // 2
        with ExitStack() as es:
            tl = Tail(es, layer, 5, 1, nslots=1, conds=(0,))
            xc = sbuf(es, "xc", [128, D], F32)
            qTt = [sbuf(es, "qTt", [128, 8, 128], BF16) for s_ in range(2)]
            NPT = 5
            pt = [sbuf(es, "pt", [128, 1024], BF16) for s_ in range(NPT)]
            acc = {"dve": sbuf(es, "accD", [128, 1024], F32), "pool": sbuf(es, "accP", [128, 1024], F32)}
            accb = sbuf(es, "accb", [128, 2048], BF16)
            ones_b = sbuf(es, "ones_b", [128, 128], BF16)
            rinv = sbuf(es, "rinv", [128, 512], F32)
            oT = sbuf(es, "oT", [128, 8, 128], BF16)
            spsm = [psum(es, "spsm", [128, 1024], F32) for s_ in range(3)]
            OT = [psum(es, "OT", [128, 512], F32)]
            sump = psum(es, "sump", [128, 512], F32)
            yps = sump
            P.add("pool", lambda e: e.memset(ones_b[:], 1.0), writes=["ones_b"])
            vres = [("Vld", hv, rk) for hv in range(2) for rk in range(4)] + ["Vc"]
            for kv in range(2):
                def ld_k(e, kv=kv):
                    r = [e.dma_start(out=KT[:, kv, rk * 4096:(rk + 1) * 4096], in_=kout[kv].ap()[rk * 128:(rk + 1) * 128, :]) for rk in range(4)]
                    r.append(e.dma_start(out=KT[:, kv, 16384:16640], in_=kc_d[kv]))
                    return r
                P.add("sp", ld_k, reads=[("kout", kv), "kc_d", "R1"], writes=[("Kld", kv)], dma=f"ldk{kv}", ndma=5)
            for hv in range(2):
                for rk in range(4):
                    def ld_v(e, hv=hv, rk=rk):
                        srcv = vout[hv].ap()[rk * 128:(rk + 1) * 128, :].rearrange("(j a) (b k d) -> (a b) j k d", a=8, b=16, k=2)
                        c0 = rk * 32 + hv * 16
                        r = []
                        for kv in range(2):
                            for jh in range(2):
                                r.append(e.dma_start(out=V[:, c0 + jh * 8:c0 + (jh + 1) * 8, kv, :], in_=srcv[:, jh * 8:(jh + 1) * 8, kv, :]))
                        return r
                    P.add("sp", ld_v, reads=[("vout", hv), "R1", "R2"], writes=[("Vld", hv, rk)], dma=f"ldv{hv}{rk}", ndma=4)

            def ld_vc(e):
                srcv = vc_d.rearrange("(j p) (k d) -> p j k d", p=128, k=2)
                return [e.dma_start(out=V[:, 128:130, kv, :], in_=srcv[:, :, kv, :]) for kv in range(2)]
            P.add("sp", ld_vc, reads=["vc_d", "R2"], writes=["Vc"], dma="ldvc", ndma=2)

            def emit_wo(e):
                srcw = w_o_d.rearrange("(h p) d -> p h d", p=128)
                return [e.dma_start(out=wo[:, h, :], in_=srcw[:, h, :]) for h in range(8)]
            P.add("pool", emit_wo, reads=["R2"], writes=["wo"], dma="wo", ndma=8)
            P.add("pe", lambda e: None, reads=vres + [("Kld", 0), ("Kld", 1)], writes=["KVready"])
            qts = list(range(NLAT))
            nl = [0]

            def load_q():
                if nl[0] < len(qts):
                    c = qts[nl[0]]
                    s_ = nl[0] % 2
                    DMA("sp", f"qTt{s_}", qTt[s_][:].rearrange("p h t -> p (h t)"), qT_d[c], reads=[("qT", c)], writes=[f"qTt{s_}"])
                    nl[0] += 1
            load_q()
            load_q()
            first = {"dve": True, "pool": True, "pe": True}
            SUMENG = ["dve", "pool", "dve", "pe", "dve", "pool", "dve", "pe"]
            for qi, c in enumerate(qts):
                s_ = qi % 2
                DMA("sp", "xc0", xc[:], xsrc[c * 128:(c + 1) * 128, :], reads=[("x", id(xsrc), c)], writes=["xc0"])
                for kv in range(2):
                    g = qi * 2 + kv
                    ot = OT[0]
                    otres = "OT0"
                    rhs = qTt[s_][:, kv * 4:(kv + 1) * 4, :].rearrange("p h t -> p (h t)")
                    first["dve"] = True
                    first["pool"] = True
                    first["pe"] = True

                    def S(p, kv=kv, rhs=rhs):
                        sp_ = spsm[p % 3]

                        def pe(e):
                            e.matmul(sp_[:, 0:512], lhsT=KT[:, kv, (2 * p) * 128:(2 * p + 1) * 128], rhs=rhs, start=True, stop=True)
                            return e.matmul(sp_[:, 512:1024], lhsT=KT[:, kv, (2 * p + 1) * 128:(2 * p + 2) * 128], rhs=rhs, start=True, stop=True)
                        P.add("pe", pe, reads=["KVready", f"qTt{s_}"], writes=[f"spsm{p % 3}"])
                        P.add("act", lambda e: e.activation(out=pt[p % NPT][:], in_=sp_[:, :], func=AF.Exp, scale=SCALE),
                              reads=[f"spsm{p % 3}"], writes=[f"pt{p % NPT}"])

                    def PV(p, kv=kv, ot=ot, otres=otres):
                        def pe(e):
                            e.matmul(ot[:, :], lhsT=V[:, 2 * p, kv, :], rhs=pt[p % NPT][:, 0:512], start=(p == 0), stop=False)
                            return e.matmul(ot[:, :], lhsT=V[:, 2 * p + 1, kv, :], rhs=pt[p % NPT][:, 512:1024], start=False, stop=(p == NP - 1))
                        P.add("pe", pe, reads=[f"pt{p % NPT}", "KVready"], writes=[otres])
                        en = SUMENG[p % 8]
                        if en == "pe":
                            st0 = first["pe"]
                            first["pe"] = False

                            def pes(e):
                                e.matmul(sump[:, :], lhsT=ones_b[:], rhs=pt[p % NPT][:, 0:512], start=st0, stop=False)
                                return e.matmul(sump[:, :], lhsT=ones_b[:], rhs=pt[p % NPT][:, 512:1024], start=False, stop=False)
                            P.add("pe", pes, reads=[f"pt{p % NPT}", "ones_b"], writes=["sump"])
                            return
                        a_ = acc[en]
                        if first[en]:
                            first[en] = False
                            P.add(en, lambda e: e.tensor_copy(out=a_[:], in_=pt[p % NPT][:]), reads=[f"pt{p % NPT}"], writes=[f"acc_{en}"])
                        else:
                            P.add(en, lambda e: e.tensor_tensor(out=a_[:], in0=a_[:], in1=pt[p % NPT][:], op=ALU.add),
                                  reads=[f"pt{p % NPT}", f"acc_{en}"], writes=[f"acc_{en}"])
                    S(0)
                    S(1)
                    for p in range(NP):
                        if p + 2 < NP:
                            S(p + 2)
                        PV(p)
                    P.add("dve", lambda e: e.tensor_copy(out=accb[:, 0:1024], in_=acc["dve"][:]), reads=["acc_dve"], writes=["accbD"])
                    P.add("pool", lambda e: e.tensor_copy(out=accb[:, 1024:2048], in_=acc["pool"][:]), reads=["acc_pool"], writes=["accbP"])

                    def pe_sum(e):
                        r = None
                        for i in range(4):
                            r = e.matmul(sump[:, :], lhsT=ones_b[:], rhs=accb[:, i * 512:(i + 1) * 512], start=False, stop=(i == 3))
                        return r
                    P.add("pe", pe_sum, reads=["ones_b", "accbD", "accbP"], writes=["sump"])
                    P.add("dve", lambda e: e.reciprocal(out=rinv[:], in_=sump[:, :]), reads=["sump"], writes=["rinv"])
                    P.add("dve", lambda e, kv=kv, ot=ot: e.tensor_tensor(
                        out=oT[:, kv * 4:(kv + 1) * 4, :].rearrange("p h t -> p (h t)"), in0=ot[:, :], in1=rinv[:], op=ALU.mult),
                        reads=[otres, "rinv"], writes=["oT"])
                for half in range(2):
                    def pe(e, half=half):
                        r = None
                        for h in range(8):
                            r = e.matmul(yps[:, :], lhsT=oT[:, h, :], rhs=wo[:, h, half * 512:(half + 1) * 512], start=(h == 0), stop=(h == 7))
                        return r
                    P.add("pe", pe, reads=["oT", "wo"], writes=["sump"])
                    tl.piece(0, yps[:, :], "sump", half * 512, (half + 1) * 512, 0)
                tl.rest(0, xc[:], "xc0", dst[c * 128:(c + 1) * 128, :], ("x", id(dst), c))
                load_q()
            return done("ATT")

    w2v = R2[:, 0:22 * 1024].rearrange("p (f d) -> p f d", f=22)
    gwo = R2[:, 0:24 * 1024].rearrange("p (f d) -> p f d", f=24)

    def load_gm_in():
        srcw = gm_w_in.rearrange("(k p) f -> p k f", p=128)
        w_in = R1.rearrange("p (k f) -> p k f", k=8)

        def emit(e):
            return [e.dma_start(out=w_in[:, kc, j * 2048:(j + 1) * 2048], in_=srcw[:, kc, j * 2048:(j + 1) * 2048]) for kc in range(8) for j in range(3)]
        P.add("pool", emit, writes=["R1"], dma="wR1", ndma=24, nobar=True)

    def load_gm_out():
        srcw = gm_w_out.rearrange("(f p) d -> p f d", p=128)

        def emit(e):
            return [e.dma_start(out=gwo[:, f, :], in_=srcw[:, f, :]) for f in range(24)]
        P.add("pool", emit, writes=["R2"], dma="wR2", ndma=24, nobar=True)

    allc = list(range(NCH))
    latc = list(range(NLAT))
    rowf = lambda c: c * 128
    stop = fin
    cur = None
    sched = [
        ("FA00", lambda: phase_FA(0, 0, xin, True)),
        ("FB00", lambda: phase_B("FB00", 22, w2v, 0, 2, 0, xin, xa, allc, rowf, prefetch=load_gm_in)),
        ("G1", lambda: phase_G1(xa)),
        ("G2", lambda: phase_B("G2", 24, gwo, 0, 5, 1, xa, xb, allc, rowf, prefetch=lambda: (load_gm_out(), load_w1(0, 1)))),
        ("FA01", lambda: phase_FA(0, 1, xb, True, prefetch=lambda: load_w2(0, 1))),
        ("FB01", lambda: phase_B("FB01", 22, w2v, 0, 8, 2, xb, xa, allc, rowf, prefetch=lambda: load_w1(1, 0))),
        ("FA10", lambda: phase_FA(1, 0, xa, True, prefetch=lambda: load_w2(1, 0))),
        ("FB10", lambda: phase_B("FB10", 22, w2v, 1, 2, 0, xa, xb, allc, rowf)),
        ("QKV", lambda: phase_QKV(xb)),
        ("ATT", lambda: phase_ATT(xb, xa)),
        ("FA11", lambda: phase_FA(1, 1, xa, False, prefetch=lambda: (load_w1(1, 1, nobar=False), load_w2(1, 1, nobar=False)))),
        ("FB11", lambda: phase_B("FB11", 22, w2v, 1, 8, 2, xa, out, latc, rowf)),
    ]
    streams = {"FB00": xa, "G2": xb, "FB01": xa, "FB10": xb, "ATT": xa}
    for name, fn in sched:
        if stop:
            break
        stop = fn()
        if name in streams:
            cur = streams[name]
        elif name != "FB11":
            cur = None if name in ("FA00",) else cur

    if stop_after == "P0":
        DMA("sp", "dbgm", out[0:36, :].rearrange("(a r) c -> a (r c)", a=4), mrow_d.rearrange("l c n -> (l c) n"), reads=["mrow_d"], writes=[("out", 0)])
        DMA("sp", "dbgc", out[128:256, 0:288], mcol[:].rearrange("p l a b -> p (l a b)"), reads=["mcol"], writes=[("out", 1)])
    if stop_after == "FA00":
        DMA("pool", "dbgh", out[0:384, :].rearrange("(p r) c -> p (r c)", r=3), hT_d[0].rearrange("p f t -> p (f t)"), writes=[("out", 0)])
    if stop and cur is not None:
        for i in range(8):
            DMA("sp", f"dbg{i}", out[i * 512:(i + 1) * 512, :], cur[i * 512:(i + 1) * 512, :], writes=[("out", i)])
    P.add("sp", lambda e: None, reads=[("out", i) for i in range(8)] + [("x", id(out), c) for c in range(NLAT)])

    with nc.Block() as block:
        P.emit_all(nc, block, lambda name: root.enter_context(nc.semaphore(name)))
    root.close()
    return nc, P


def _prep_inputs(inp):
    f = lambda a: np.ascontiguousarray(np.asarray(a, dtype=np.float32))
    x, c, ctx, c_ctx = f(inp["x"]), f(inp["c"]), f(inp["ctx"]), f(inp["c_ctx"])
    shared = dict(
        ident=np.eye(128, dtype=np.float32),
        w_mod=f(inp["w_mod"]),
        bmod2=f(np.broadcast_to(f(inp["b_mod"])[None], (2, 2, 9216))),
        ln_g=f(inp["ln_g"]), ln_b=f(inp["ln_b"]),
        ffn_w_in=f(inp["ffn_w_in"]), ffn_w_out=f(inp["ffn_w_out"]),
        gm_w_in=f(inp["gmlp_w_in"][0]),
        glng=f(f(inp["gmlp_ln_g"])[0].reshape(24, 128).T),
        glnb=f(f(inp["gmlp_ln_b"])[0].reshape(24, 128).T),
        wsT=f(f(inp["gmlp_w_s"])[0].transpose(2, 0, 1)),
        gbs=f(f(inp["gmlp_b_s"])[0].reshape(-1)),
        gm_w_out=f(inp["gmlp_w_out"][0]),
        w_qkv=f(inp["attn_w_qkv"][0]),
        qn=f(inp["attn_q_norm"][0]), kn=f(inp["attn_k_norm"][0]),
        w_o=f(inp["attn_w_o"][0]),
    )
    quarter = 32
    fr = (np.float32(10000.0) ** (-np.arange(quarter, dtype=np.float32) / np.float32(quarter))).astype(np.float32)
    shared["freq2"] = f(np.broadcast_to(np.concatenate([fr, fr])[None], (128, 64)))
    in_maps = []
    for i in range(8):
        b, j = i // 4, i % 4
        m = dict(shared)
        m["xin"] = f(np.concatenate([x[b, j * 4096:(j + 1) * 4096], ctx[b]], axis=0))
        cd = np.stack([c[b], c_ctx], axis=-1)
        m["cond"] = f(cd.reshape(8, 128, 2).transpose(1, 0, 2))
        t = j * 4096 + np.arange(4096)
        rc = np.stack([t // 64, t % 64], axis=-1).astype(np.float32)
        m["posrc"] = f(rc.reshape(32, 128, 2).transpose(1, 0, 2))
        in_maps.append(m)
    return in_maps


_CACHE = {}


def run(inp, stop_after=None, trace=False):
    key = stop_after
    if key not in _CACHE:
        _CACHE[key] = build(stop_after)
    nc, P = _CACHE[key]
    in_maps = _prep_inputs(inp)
    res = run_bass_kernel_spmd(nc, in_maps, core_ids=list(range(8)), trace=trace)
    outs = [np.asarray(r["out"], dtype=np.float32) for r in res.results]
    full = np.stack([np.concatenate(outs[0:4], axis=0), np.concatenate(outs[4:8], axis=0)], axis=0)
    return full, res


def kernel(**inputs):
    full, _ = run(inputs)
    return full
```

```python
from contextlib import ExitStack
import math
import os
import numpy as np
G1MODE = int(os.environ.get('G1MODE', '9'))
G1GELU = int(os.environ.get('G1GELU', '1'))
import concourse.bass as bass
import concourse.mybir as mybir
from concourse.bass_utils import run_bass_kernel_spmd

F32 = mybir.dt.float32
BF16 = mybir.dt.bfloat16
I32 = mybir.dt.int32
AF = mybir.ActivationFunctionType
ALU = mybir.AluOpType
AX = mybir.AxisListType

ENGS = ["pe", "act", "dve", "pool", "sp"]
SEM_LIMIT = 60000


class Prog:
    def __init__(self):
        self.ops = []
        self.last_w = {}
        self.readers = {}
        self.bar_from = 0

    def add(self, eng, emit, reads=(), writes=(), dma=None, ndma=1, inc=16, nobar=False):
        i = len(self.ops)
        deps = {}
        for r in reads:
            d = self.last_w.get(r)
            if d is not None:
                deps[d] = "raw"
        for w in writes:
            d = self.last_w.get(w)
            if d is not None:
                deps.setdefault(d, "waw")
            for rd in self.readers.get(w, ()):
                if rd != i:
                    deps.setdefault(rd, "war")
        for r in reads:
            self.readers.setdefault(r, []).append(i)
        for w in writes:
            self.last_w[w] = i
            self.readers[w] = []
        self.ops.append(dict(eng=eng, emit=emit, deps=deps, dma=dma, ndma=ndma, inc=inc, nobar=nobar))
        return i

    def barrier(self):
        lo, hi = self.bar_from, len(self.ops)
        last = {}
        for i in range(lo, hi):
            op = self.ops[i]
            if op["nobar"]:
                continue
            if op["dma"] is not None:
                last[("dma", op["dma"])] = i
            else:
                last[("eng", op["eng"])] = i
        deps = {i: "raw" for i in last.values()}
        for e in ENGS:
            self.ops.append(dict(eng=e, emit=lambda e_: None, deps=dict(deps), dma=None, ndma=1, inc=16,
                                 nobar=True, isbar=True))
        self.bar_from = len(self.ops)

    def _needs_wait(self, op, dop, kind):
        if dop["dma"] is not None:
            return True
        if dop["eng"] != op["eng"]:
            return True
        if op["dma"] is not None or op.get("isbar"):
            return True
        if op["eng"] == "pe":
            return False
        return kind == "raw"

    def emit_all(self, nc, block, sems):
        ops = self.ops
        n = len(ops)
        need_sig = [False] * n
        for op in ops:
            for d, kind in op["deps"].items():
                dop = ops[d]
                if dop["dma"] is None and self._needs_wait(op, dop, kind):
                    need_sig[d] = True
        cnt = {e: 0 for e in ENGS}
        dcnt = {}
        for i, op in enumerate(ops):
            if op["dma"] is not None:
                k = op["dma"]
                dcnt[k] = dcnt.get(k, 0) + op["inc"] * op["ndma"]
                op["dval"] = dcnt[k]
            elif need_sig[i]:
                cnt[op["eng"]] += 1
                op["sig"] = cnt[op["eng"]]
        self.stats = dict(cnt=cnt, dma_keys=len(dcnt), nops=n, maxdma=max(dcnt.values()) if dcnt else 0)
        assert max(cnt.values()) < SEM_LIMIT, cnt
        assert self.stats["maxdma"] < SEM_LIMIT, self.stats
        eng_sem = {e: sems(f"prog_{e}") for e in ENGS if cnt[e] > 0}
        dma_sem = {k: sems(f"dma_{k}") for k in dcnt}

        def run_engine(ename, e):
            seen = {}
            for i, op in enumerate(ops):
                if op["eng"] != ename:
                    continue
                waits = {}
                for d, kind in op["deps"].items():
                    dop = ops[d]
                    if not self._needs_wait(op, dop, kind):
                        continue
                    if dop["dma"] is not None:
                        s, v = dma_sem[dop["dma"]], dop["dval"]
                    else:
                        s, v = eng_sem[dop["eng"]], dop["sig"]
                    key = id(s)
                    if waits.get(key, (None, 0))[1] < v:
                        waits[key] = (s, v)
                for key, (s, v) in waits.items():
                    if seen.get(key, 0) < v:
                        e.wait_ge(s, v)
                        seen[key] = v
                r = op["emit"](e)
                if op["dma"] is not None:
                    insts = r if isinstance(r, (list, tuple)) else [r]
                    assert len(insts) == op["ndma"], (len(insts), op["ndma"], op["dma"])
                    for ins in insts:
                        ins.then_inc(dma_sem[op["dma"]], op["inc"])
                elif need_sig[i]:
                    assert r is not None, f"op {i} on {ename} must return an instruction"
                    r.then_inc(eng_sem[ename], 1)

        @block.tensor
        def _(e):
            run_engine("pe", e)

        @block.scalar
        def _(e):
            run_engine("act", e)

        @block.vector
        def _(e):
            run_engine("dve", e)

        @block.gpsimd
        def _(e):
            run_engine("pool", e)

        @block.sync
        def _(e):
            run_engine("sp", e)


D = 1024
NCH = 34
NLAT = 32
DFF = 2816
ALPHA = 4.0 ** 0.25
LN_EPS = 1e-5
RMS_EPS = 1e-6
SCALE = 128.0 ** -0.5
NKC = 130
GROUPS = [[0, 1, 2, 3], [4, 5, 6, 7]]
PHASES = ["P0", "FA00", "FB00", "G1", "G2", "FA01", "FB01", "FA10", "FB10", "QKV", "ATT", "FA11", "FB11"]


def build(stop_after=None):
    nc = bass.Bass("TRN2", target_bir_lowering=False)

    def din(name, shape, dt=F32):
        return nc.dram_tensor(name, list(shape), dt, kind="ExternalInput").ap()

    def dscr(name, shape, dt):
        return nc.dram_tensor(name, list(shape), dt).ap()

    xin = din("xin", [NCH * 128, D])
    cond = din("cond", [128, 8, 2])
    posrc = din("posrc", [128, NLAT, 2])
    freq2 = din("freq2", [128, 64])
    identd = din("ident", [128, 128])
    w_mod = din("w_mod", [2, D, 9 * D])
    bmod2 = din("bmod2", [2, 2, 9 * D])
    ln_g = din("ln_g", [2, 3, D])
    ln_b = din("ln_b", [2, 3, D])
    ffn_w_in = din("ffn_w_in", [2, 2, D, 2 * DFF])
    ffn_w_out = din("ffn_w_out", [2, 2, DFF, D])
    gm_w_in = din("gm_w_in", [D, 6144])
    glng = din("glng", [128, 24])
    glnb = din("glnb", [128, 24])
    wsT_d = din("wsT", [128, 8, 128])
    gbs = din("gbs", [8 * 128])
    gm_w_out = din("gm_w_out", [3072, D])
    w_qkv = din("w_qkv", [D, 1536])
    qn_d = din("qn", [128])
    kn_d = din("kn", [128])
    w_o_d = din("w_o", [D, D])
    out = nc.dram_tensor("out", [NLAT * 128, D], F32, kind="ExternalOutput").ap()

    xa = dscr("xa", [NCH * 128, D], F32)
    xb = dscr("xb", [NCH * 128, D], F32)
    hT_d = dscr("hT_d", [NCH, 128, 24, 128], BF16)
    qT_d = dscr("qT_d", [NLAT, 128, 1024], BF16)
    kin = [nc.dram_tensor(f"kin{i}", [128, 4096], BF16) for i in range(2)]
    vin = [nc.dram_tensor(f"vin{i}", [128, 4096], BF16) for i in range(2)]
    kout = [nc.dram_tensor(f"kout{i}", [512, 4096], BF16) for i in range(2)]
    vout = [nc.dram_tensor(f"vout{i}", [512, 4096], BF16) for i in range(2)]
    kc_d = dscr("kc_d", [2, 128, 256], BF16)
    vc_d = dscr("vc_d", [256, 256], BF16)
    mrow_d = dscr("mrow_d", [2, 2, 9 * D], F32)

    P = Prog()
    root = ExitStack()

    uid = [0]

    def sbuf(es, name, shape, dt):
        uid[0] += 1
        return es.enter_context(nc.sbuf_tensor(f"{name}_{uid[0]}", list(shape), dt))

    def psum(es, name, shape, dt):
        uid[0] += 1
        return es.enter_context(nc.psum_tensor(f"{name}_{uid[0]}", list(shape), dt))

    ident_f = sbuf(root, "ident_f", [128, 128], F32)
    ident_b = sbuf(root, "ident_b", [128, 128], BF16)
    mcol = sbuf(root, "mcol", [128, 2, 72, 2], F32)
    BIG = sbuf(root, "BIG", [128, 77824], BF16)
    R1 = BIG[:, 0:49152]
    R2 = BIG[:, 49152:77824]

    def DMA(eng, key, out_, in_, reads=(), writes=(), nobar=False):
        return P.add(eng, lambda e: e.dma_start(out=out_, in_=in_), reads=reads, writes=writes, dma=key, nobar=nobar)

    DMA("sp", "c_ident", ident_f[:], identd, writes=["ident_f"])
    P.add("dve", lambda e: e.tensor_copy(out=ident_b[:], in_=ident_f[:]), reads=["ident_f"], writes=["ident_b"])

    def done(ph):
        P.barrier()
        return stop_after == ph

    def load_w1(layer, j, nobar=True):
        w1 = R1[:, 0:8 * 5632].rearrange("p (k f) -> p k f", k=8)
        src = ffn_w_in[layer, j].rearrange("(k p) f -> p k f", p=128)
        pieces = [(0, 2048), (2048, 4096), (4096, 5632)]

        def emit(e):
            r = []
            for kc in range(8):
                for lo, hi in pieces:
                    r.append(e.dma_start(out=w1[:, kc, lo:hi], in_=src[:, kc, lo:hi]))
            return r
        P.add("pool", emit, writes=["R1"], dma="wR1", ndma=24, nobar=nobar)
        return w1

    def load_w2(layer, j, nobar=True):
        w2 = R2[:, 0:22 * 1024].rearrange("p (f d) -> p f d", f=22)
        src = ffn_w_out[layer, j].rearrange("(f p) d -> p f d", p=128)

        def emit(e):
            return [e.dma_start(out=w2[:, f, :], in_=src[:, f, :]) for f in range(22)]
        P.add("pool", emit, writes=["R2"], dma="wR2", ndma=22, nobar=nobar)
        return w2

    def prep_chunk(xc_t, xc_res, trp, trp_res, xmT, xm_res, col0, layer, shift_mod, scale_mod, cnd):
        def pe(e):
            r = None
            for kc in range(8):
                r = e.transpose(out=trp[:, kc, :], in_=xc_t[:, kc * 128:(kc + 1) * 128], identity=ident_f[:])
            return r
        P.add("pe", pe, reads=[xc_res, "ident_f"], writes=[trp_res])
        for kc in range(8):
            P.add("act", lambda e, kc=kc: e.activation(
                out=xmT[:, kc, col0:col0 + 128], in_=trp[:, kc, :], func=AF.Identity,
                scale=mcol[:, layer, scale_mod * 8 + kc, cnd:cnd + 1],
                bias=mcol[:, layer, shift_mod * 8 + kc, cnd:cnd + 1]),
                reads=[trp_res, "mcol"], writes=[xm_res])

    class Tail:
        def __init__(self, es, layer, gate_mod, ln_idx, nslots=2, conds=(0, 1)):
            self.gate = [sbuf(es, f"gate_bc{c}", [128, D], F32) if c in conds else None for c in range(2)]
            self.g_bc = sbuf(es, "g_bc", [128, D], F32)
            self.b_bc = sbuf(es, "b_bc", [128, D], F32)
            self.t1 = [sbuf(es, f"t1_{s}", [128, D], F32) for s in range(nslots)]
            self.st = [sbuf(es, f"st_{s}", [128, 2, 6], F32) for s in range(nslots)]
            self.mv = [sbuf(es, f"mv_{s}", [128, 2], F32) for s in range(nslots)]
            self.rs = [sbuf(es, f"rs_{s}", [128, 2], F32) for s in range(nslots)]
            self.n = nslots
            self.epsc = sbuf(es, "epsc", [128, 1], F32)
            P.add("pool", lambda e: e.memset(self.epsc[:], LN_EPS), writes=["epsc"])
            for c in conds:
                DMA("sp", f"gatebc{c}", self.gate[c][:],
                    mrow_d[layer, c, gate_mod * D:(gate_mod + 1) * D].partition_broadcast(128),
                    reads=["mrow_d"], writes=[f"gate_bc{c}"])
            DMA("sp", "gbc", self.g_bc[:], ln_g[layer, ln_idx].partition_broadcast(128), writes=["g_bc"])
            DMA("sp", "bbc", self.b_bc[:], ln_b[layer, ln_idx].partition_broadcast(128), writes=["b_bc"])

        def piece(self, s, yap, yres, lo, hi, cnd):
            t1 = self.t1[s]
            P.add("dve", lambda e: e.tensor_tensor(out=t1[:, lo:hi], in0=yap, in1=self.gate[cnd][:, lo:hi], op=ALU.mult),
                  reads=[yres, f"gate_bc{cnd}"], writes=[f"t1_{s}"])

        def rest(self, s, xold, xres, dst, dst_res, before_store=None):
            t1, st, mv, rs = self.t1[s], self.st[s], self.mv[s], self.rs[s]
            r = f"t1_{s}"
            P.add("dve", lambda e: e.scalar_tensor_tensor(out=t1[:], in0=xold, scalar=ALPHA, in1=t1[:],
                                                          op0=ALU.mult, op1=ALU.add), reads=[xres, r], writes=[r])

            def stats(e):
                e.bn_stats(out=st[:, 0, :], in_=t1[:, 0:512])
                return e.bn_stats(out=st[:, 1, :], in_=t1[:, 512:1024])
            P.add("dve", stats, reads=[r], writes=[f"st_{s}"])
            P.add("dve", lambda e: e.bn_aggr(out=mv[:], in_=st[:].rearrange("p a b -> p (a b)")),
                  reads=[f"st_{s}"], writes=[f"mv_{s}"])
            P.add("act", lambda e: e.activation(out=rs[:, 0:1], in_=mv[:, 1:2], func=AF.Sqrt, bias=self.epsc[:, 0:1], scale=1.0),
                  reads=[f"mv_{s}", "epsc"], writes=[f"rsa_{s}"])
            P.add("dve", lambda e: e.reciprocal(out=rs[:, 0:1], in_=rs[:, 0:1]), reads=[f"rsa_{s}"], writes=[f"rsa_{s}"])
            P.add("dve", lambda e: e.tensor_scalar(out=rs[:, 1:2], in0=mv[:, 0:1], scalar1=rs[:, 0:1], scalar2=-1.0,
                                                   op0=ALU.mult, op1=ALU.mult), reads=[f"mv_{s}", f"rsa_{s}"], writes=[f"rsb_{s}"])
            P.add("act", lambda e: e.activation(out=t1[:], in_=t1[:], func=AF.Identity, scale=rs[:, 0:1], bias=rs[:, 1:2]),
                  reads=[r, f"rsa_{s}", f"rsb_{s}"], writes=[r])
            P.add("pool", lambda e: e.tensor_tensor(out=t1[:], in0=t1[:], in1=self.g_bc[:], op=ALU.mult),
                  reads=[r, "g_bc"], writes=[r])
            P.add("pool", lambda e: e.tensor_tensor(out=t1[:], in0=t1[:], in1=self.b_bc[:], op=ALU.add),
                  reads=[r, "b_bc"], writes=[r])
            if before_store:
                before_store()
            DMA("sp", f"st_t1_{s}", dst, t1[:], reads=[r], writes=[dst_res])

    with ExitStack() as es:
        scf = sbuf(es, "scf", [128, 8, 2], F32)
        scb = sbuf(es, "scb", [128, 8, 2], BF16)
        mrow_ = R2[0:2, 0:18432].bitcast(F32)
        mrow = [mrow_, mrow_]
        wblk = [sbuf(es, f"wblk{s}", [128, 8, 512], BF16) for s in range(3)]
        brow = [sbuf(es, f"brow{s}", [2, 512], F32) for s in range(3)]
        mps = [psum(es, f"mps{s}", [128, 512], F32) for s in range(2)]
        tps = psum(es, "tps0", [128, 512], F32)
        DMA("sp", "scf", scf[:], cond, writes=["scf"])
        P.add("act", lambda e: e.activation(out=scb[:], in_=scf[:], func=AF.Silu), reads=["scf"], writes=["scb"])
        for layer in range(2):
            for blk in range(18):
                s3, s2 = blk % 3, blk % 2
                DMA("pool", f"wblk{s3}", wblk[s3][:],
                    w_mod[layer, :, blk * 512:(blk + 1) * 512].rearrange("(k p) c -> p k c", p=128),
                    writes=[f"wblk{s3}"])
                DMA("sp", f"brow{s3}", brow[s3][:], bmod2[:, layer, blk * 512:(blk + 1) * 512], writes=[f"brow{s3}"])

                def pe(e, s3=s3, s2=s2):
                    r = None
                    for kc in range(8):
                        r = e.matmul(mps[s2][0:2, :], lhsT=scb[:, kc, :], rhs=wblk[s3][:, kc, :], start=(kc == 0), stop=(kc == 7))
                    return r
                P.add("pe", pe, reads=["scb", f"wblk{s3}"], writes=[f"mps{s2}"])
                P.add("dve", lambda e, s3=s3, s2=s2, blk=blk, layer=layer: e.tensor_tensor(
                    out=mrow[layer][:, blk * 512:(blk + 1) * 512], in0=mps[s2][0:2, :], in1=brow[s3][:], op=ALU.add),
                    reads=[f"mps{s2}", f"brow{s3}"], writes=["R2"])
            for m in (1, 4, 7):
                P.add("dve", lambda e, m=m, layer=layer: e.tensor_scalar(
                    out=mrow[layer][:, m * D:(m + 1) * D], in0=mrow[layer][:, m * D:(m + 1) * D], scalar1=1.0, scalar2=None, op0=ALU.add),
                    reads=["R2"], writes=["R2"])
            for m in (2, 8):
                P.add("dve", lambda e, m=m, layer=layer: e.tensor_scalar(
                    out=mrow[layer][:, m * D:(m + 1) * D], in0=mrow[layer][:, m * D:(m + 1) * D], scalar1=0.5, scalar2=None, op0=ALU.mult),
                    reads=["R2"], writes=["R2"])
            DMA("sp", f"mrowst{layer}", mrow_d[layer], mrow[layer][:], reads=["R2"], writes=["mrow_d"])

            def pe_t(e, layer=layer):
                r = None
                for jj in range(72):
                    r = e.transpose(out=tps[:, 2 * jj:2 * jj + 2], in_=mrow[layer][0:2, jj * 128:(jj + 1) * 128], identity=ident_f[0:2, 0:2])
                return r
            P.add("pe", pe_t, reads=["R2", "ident_f"], writes=["tps0"])
            P.add("dve", lambda e, layer=layer: e.tensor_copy(
                out=mcol[:, layer, :, :].rearrange("p a b -> p (a b)"), in_=tps[:, 0:144]), reads=["tps0"], writes=["mcol"])
        load_w1(0, 0)
        load_w2(0, 0)
        fin = done("P0")

    def phase_FA(layer, j, src, with_ctx, prefetch=None):
        shift_mod, scale_mod = (0, 1) if j == 0 else (6, 7)
        tiles = [(4 * t, 4, 0) for t in range(8)] + ([(32, 2, 1)] if with_ctx else [])
        w1 = R1[:, 0:8 * 5632].rearrange("p (k f) -> p k f", k=8)
        with ExitStack() as es:
            NX = 6
            xc = [sbuf(es, f"xc{s}", [128, D], F32) for s in range(NX)]
            xmT = [sbuf(es, f"xmT{s}", [128, 8, 512], BF16) for s in range(2)]
            sg = [sbuf(es, f"sg{s}", [128, 512], F32) for s in range(2)]
            hto = [sbuf(es, f"hto{s}", [128, 512], BF16) for s in range(4)]
            trp = [psum(es, f"trp{s}", [128, 8, 128], F32) for s in range(2)]
            gu = [psum(es, f"gu{s}", [128, 2, 512], F32) for s in range(2)]
            if prefetch:
                prefetch()
            chunks = [(c0 + ci, cnd) for (c0, n, cnd) in tiles for ci in range(n)]
            nload = [0]

            def load_next():
                if nload[0] < len(chunks):
                    c, _ = chunks[nload[0]]
                    s = nload[0] % NX
                    DMA("sp", f"xc{s}", xc[s][:], src[c * 128:(c + 1) * 128, :], reads=[("x", id(src), c)], writes=[f"xc{s}"])
                    nload[0] += 1
            for _ in range(NX):
                load_next()
            cidx = [0]

            def prep(ti):
                c0, n, cnd = tiles[ti]
                for ci in range(n):
                    k = cidx[0]
                    s = k % NX
                    prep_chunk(xc[s], f"xc{s}", trp[k % 2], f"trp{k % 2}", xmT[ti % 2], f"xmT{ti % 2}", ci * 128,
                               layer, shift_mod, scale_mod, cnd)
                    cidx[0] += 1
                    load_next()
            prep(0)
            for ti, (c0, n, cnd) in enumerate(tiles):
                NT = n * 128
                xm = xmT[ti % 2]
                for f in range(22):
                    b = f % 2

                    def pe(e, f=f, b=b, xm=xm, NT=NT):
                        r = None
                        for half in range(2):
                            col = half * DFF + f * 128
                            for kc in range(8):
                                r = e.matmul(gu[b][:, half, 0:NT], lhsT=w1[:, kc, col:col + 128], rhs=xm[:, kc, 0:NT],
                                             start=(kc == 0), stop=(kc == 7))
                        return r
                    P.add("pe", pe, reads=["R1", f"xmT{ti % 2}"], writes=[f"gu{b}"])
                    P.add("act", lambda e, b=b, NT=NT: e.activation(out=sg[b][:, 0:NT], in_=gu[b][:, 0, 0:NT], func=AF.Silu),
                          reads=[f"gu{b}"], writes=[f"sg{b}"])
                    hs = f % 4
                    P.add("dve", lambda e, b=b, hs=hs, NT=NT: e.tensor_tensor(out=hto[hs][:, 0:NT], in0=sg[b][:, 0:NT],
                                                                              in1=gu[b][:, 1, 0:NT], op=ALU.mult),
                          reads=[f"sg{b}", f"gu{b}"], writes=[f"hto{hs}"])
                    DMA("sp", f"hto{hs}", hT_d[c0:c0 + n, :, f, :].rearrange("c p t -> p c t"),
                        hto[hs][:, 0:NT].rearrange("p (c t) -> p c t", t=128),
                        reads=[f"hto{hs}"], writes=[("hT", c0, f)])
                    if f == 8 and ti + 1 < len(tiles):
                        prep(ti + 1)
            return done(f"FA{layer}{j}")

    def phase_B(name, nf, w2, layer, gate_mod, ln_idx, xsrc, dst, chunk_list, dst_rows, prefetch=None):
        with ExitStack() as es:
            tl = Tail(es, layer, gate_mod, ln_idx)
            xc = [sbuf(es, f"xc{s}", [128, D], F32) for s in range(2)]
            hin = [sbuf(es, f"hin{s}", [128, nf, 128], BF16) for s in range(2)]
            yps = [psum(es, f"yps{s}", [128, D], F32) for s in range(2)]
            if prefetch:
                prefetch()
            nl = [0]

            def load_next():
                if nl[0] < len(chunk_list):
                    c = chunk_list[nl[0]]
                    s = nl[0] % 2
                    DMA("sp", f"xc{s}", xc[s][:], xsrc[c * 128:(c + 1) * 128, :], reads=[("x", id(xsrc), c)], writes=[f"xc{s}"])
                    DMA("sp", f"hin{s}", hin[s][:], hT_d[c, :, 0:nf, :], reads=[("hT", c)], writes=[f"hin{s}"])
                    nl[0] += 1
            for _ in range(2):
                load_next()
            for k, c in enumerate(chunk_list):
                s, b = k % 2, k % 2
                cnd = 0 if c < NLAT else 1
                for half in range(2):
                    def pe(e, s=s, b=b, half=half):
                        r = None
                        for f in range(nf):
                            r = e.matmul(yps[b][:, half * 512:(half + 1) * 512], lhsT=hin[s][:, f, :],
                                         rhs=w2[:, f, half * 512:(half + 1) * 512], start=(f == 0), stop=(f == nf - 1))
                        return r
                    P.add("pe", pe, reads=["R2", f"hin{s}"], writes=[(f"yps{b}", half)])
                    tl.piece(b, yps[b][:, half * 512:(half + 1) * 512], (f"yps{b}", half), half * 512, (half + 1) * 512, cnd)
                r0 = dst_rows(c)
                tl.rest(b, xc[s][:], f"xc{s}", dst[r0:r0 + 128, :], ("x", id(dst), c), before_store=load_next)
            return done(name)


    def gen_loader(name_prefix, nslots, slots, chunk_ids, src, res_prefix):
        st_ = [0]

        def load_next():
            if st_[0] < len(chunk_ids):
                c = chunk_ids[st_[0]]
                s_ = st_[0] % nslots
                DMA("sp", f"{res_prefix}{s_}", slots[s_][:], src[c * 128:(c + 1) * 128, :],
                    reads=[("x", id(src), c)], writes=[f"{res_prefix}{s_}"])
                st_[0] += 1
        return load_next

    def phase_G1(src, prefetch=None):
        layer, shift_mod, scale_mod = 0, 3, 4
        tiles = [(2 * t, 2, 0) for t in range(16)] + [(32, 2, 1)]
        w_in = R1.rearrange("p (k f) -> p k f", k=8)
        with ExitStack() as es:
            NX = 4
            xc = [sbuf(es, "xc", [128, D], F32) for s_ in range(NX)]
            xmT = [sbuf(es, "xmT", [128, 8, 256], BF16) for s_ in range(2)]
            vh = sbuf(es, "vh", [128, 3072], BF16)
            hho = [sbuf(es, "hho", [128, 24, 128], BF16) for s_ in range(2)]
            wsT = sbuf(es, "wsT", [128, 8, 128], BF16)
            bs_bc = sbuf(es, "bs_bc", [128, 8, 128], F32)
            tmp = [sbuf(es, "tmp", [128, 128], F32) for s_ in range(2)]
            gcol = sbuf(es, "gcol", [128, 24], F32)
            bcol = sbuf(es, "bcol", [128, 24], F32)
            ones_b = sbuf(es, "ones_b", [128, 128], BF16)
            st6 = sbuf(es, "st6", [128, 6, 6], F32)
            mv = sbuf(es, "mvg", [128, 2], F32)
            rs = sbuf(es, "rsg", [128, 2], F32)
            epsc = sbuf(es, "epsg", [128, 1], F32)
            uT = [R2[:, i * 6144:(i + 1) * 6144].rearrange("p (f t) -> p f t", f=24) for i in range(2)]
            v_sb = R2[:, 12288:18432].bitcast(F32)
            Bt = R2[:, 18432:24576].bitcast(F32).rearrange("p (f t) -> p f t", f=24)
            trp = psum(es, "trp", [128, 8, 128], F32)
            ups_ = [psum(es, "ups", [128, 512], F32) for s_ in range(2)]
            vps = [psum(es, "vps", [128, 512], F32) for s_ in range(2)]
            sps = [psum(es, "sps", [128, 4, 128], F32) for s_ in range(2)]
            if prefetch:
                prefetch()
            DMA("pool", "wsT", wsT[:], wsT_d, writes=["wsT"])
            DMA("sp", "gcol", gcol[:], glng, writes=["gcol"])
            DMA("sp", "bcol", bcol[:], glnb, writes=["bcol"])
            DMA("sp", "bsbc", bs_bc[:].rearrange("p g t -> p (g t)"), gbs.partition_broadcast(128), writes=["bs_bc"])
            P.add("pool", lambda e: e.memset(ones_b[:], 1.0), writes=["ones_b"])
            P.add("pool", lambda e: e.memset(epsc[:], LN_EPS), writes=["epsg"])
            for g in range(8):
                P.add("pe", lambda e, g=g: e.matmul(sps[g // 4][:, g % 4, :], lhsT=ones_b[:], rhs=wsT[:, g, :], start=True, stop=True),
                      reads=["ones_b", "wsT"], writes=[f"sps{g // 4}"])
            for cc in range(24):
                g = cc // 3
                P.add("dve", lambda e, cc=cc, g=g: e.scalar_tensor_tensor(
                    out=Bt[:, cc, :], in0=sps[g // 4][:, g % 4, :], scalar=bcol[:, cc:cc + 1], in1=bs_bc[:, g, :],
                    op0=ALU.mult, op1=ALU.add), reads=[f"sps{g // 4}", "bcol", "bs_bc"], writes=["Bt"])
            chunks = [c0 + ci for (c0, n, cnd) in tiles for ci in range(n)]
            load_next = gen_loader("xc", NX, xc, chunks, src, "xc")
            for _ in range(NX):
                load_next()
            kidx = [0]

            def prep(ti):
                c0, n, cnd = tiles[ti]
                for ci in range(n):
                    k = kidx[0]
                    prep_chunk(xc[k % NX], f"xc{k % NX}", trp, "trp", xmT[ti % 2], f"xmT{ti % 2}", ci * 128,
                               layer, shift_mod, scale_mod, cnd)
                    kidx[0] += 1
                    load_next()
            prep(0)

            def emit_uT(ti, lo, hi):
                xm_ = xmT[ti % 2]
                uu = uT[ti % 2]
                for cc in range(lo, hi):
                    def pe(e, cc=cc, xm_=xm_):
                        r = None
                        for kc in range(8):
                            r = e.matmul(ups_[cc % 2][:, 0:256], lhsT=w_in[:, kc, cc * 128:(cc + 1) * 128], rhs=xm_[:, kc, :],
                                         start=(kc == 0), stop=(kc == 7))
                        return r
                    P.add("pe", pe, reads=["R1", f"xmT{ti % 2}"], writes=[f"ups{cc % 2}"])
                    P.add("act", lambda e, cc=cc, uu=uu: e.activation(out=uu[:, cc, :], in_=ups_[cc % 2][:, 0:256], func=AF.Gelu),
                          reads=[f"ups{cc % 2}"], writes=[f"uT{ti % 2}"])
            emit_uT(0, 0, 24)
            for ti, (c0, n, cnd) in enumerate(tiles):
                xm = xmT[ti % 2]
                u_ = uT[ti % 2]
                has_next = ti + 1 < len(tiles)
                if has_next:
                    prep(ti + 1)
                for ci in range(n):
                    c = c0 + ci
                    for blk in range(6):
                        def pe(e, blk=blk, xm=xm, ci=ci):
                            r = None
                            for kc in range(8):
                                r = e.matmul(vps[blk % 2][:, :], lhsT=xm[:, kc, ci * 128:(ci + 1) * 128],
                                             rhs=w_in[:, kc, 3072 + blk * 512:3072 + (blk + 1) * 512], start=(kc == 0), stop=(kc == 7))
                            return r
                        P.add("pe", pe, reads=["R1", f"xmT{ti % 2}"], writes=[f"vps{blk % 2}"])
                        P.add("act", lambda e, blk=blk: e.activation(out=v_sb[:, blk * 512:(blk + 1) * 512], in_=vps[blk % 2][:, :], func=AF.Gelu),
                              reads=[f"vps{blk % 2}"], writes=["v_sb"])

                    def stats(e):
                        r = None
                        for q in range(6):
                            r = e.bn_stats(out=st6[:, q, :], in_=v_sb[:, q * 512:(q + 1) * 512])
                        return r
                    P.add("dve", stats, reads=["v_sb"], writes=["st6"])
                    P.add("dve", lambda e: e.bn_aggr(out=mv[:], in_=st6[:].rearrange("p a b -> p (a b)")), reads=["st6"], writes=["mvg"])
                    P.add("act", lambda e: e.activation(out=rs[:, 0:1], in_=mv[:, 1:2], func=AF.Sqrt, bias=epsc[:, 0:1], scale=1.0),
                          reads=["mvg", "epsg"], writes=["rsga"])
                    P.add("dve", lambda e: e.reciprocal(out=rs[:, 0:1], in_=rs[:, 0:1]), reads=["rsga"], writes=["rsga"])
                    P.add("dve", lambda e: e.tensor_scalar(out=rs[:, 1:2], in0=mv[:, 0:1], scalar1=rs[:, 0:1], scalar2=-1.0,
                                                           op0=ALU.mult, op1=ALU.mult), reads=["mvg", "rsga"], writes=["rsgb"])
                    P.add("act", lambda e: e.activation(out=vh[:], in_=v_sb[:], func=AF.Identity, scale=rs[:, 0:1], bias=rs[:, 1:2]),
                          reads=["v_sb", "rsga", "rsgb"], writes=["vh"])
                    usched = [3, 2, 2, 2, 2, 1]
                    unext = [ci * 12]
                    for cg in range(6):
                        if has_next:
                            emit_uT(ti + 1, unext[0], unext[0] + usched[cg])
                            unext[0] += usched[cg]

                        def pe(e, cg=cg):
                            r = None
                            for i in range(4):
                                cc = cg * 4 + i
                                r = e.matmul(sps[cg % 2][:, i, :], lhsT=vh[:, cc * 128:(cc + 1) * 128], rhs=wsT[:, cc // 3, :],
                                             start=True, stop=True)
                            return r
                        P.add("pe", pe, reads=["vh", "wsT"], writes=[f"sps{cg % 2}"])
                        for i in range(4):
                            cc = cg * 4 + i
                            P.add("dve", lambda e, cc=cc, cg=cg, i=i: e.scalar_tensor_tensor(
                                out=tmp[cc % 2][:], in0=sps[cg % 2][:, i, :], scalar=gcol[:, cc:cc + 1], in1=Bt[:, cc, :],
                                op0=ALU.mult, op1=ALU.add), reads=[f"sps{cg % 2}", "gcol", "Bt"], writes=[f"tmp{cc % 2}"])
                            P.add("pool", lambda e, cc=cc, c=c, ci=ci, u_=u_: e.tensor_tensor(
                                out=hho[c % 2][:, cc, :], in0=tmp[cc % 2][:], in1=u_[:, cc, ci * 128:(ci + 1) * 128], op=ALU.mult),
                                reads=[f"tmp{cc % 2}", f"uT{ti % 2}"], writes=[f"hho{c % 2}"])
                    DMA("sp", f"hho{c % 2}", hT_d[c], hho[c % 2][:], reads=[f"hho{c % 2}"], writes=[("hT", c)])
            return done("G1")

    def phase_QKV(src, prefetch=None):
        layer, shift_mod, scale_mod = 1, 3, 4
        wq = R1[:, 0:8 * 1536].rearrange("p (k f) -> p k f", k=8)
        with ExitStack() as es:
            NX = 3
            xc = [sbuf(es, "xc", [128, D], F32) for s_ in range(NX)]
            xmT = [sbuf(es, "xmT", [128, 8, 128], BF16) for s_ in range(2)]
            sq = [sbuf(es, "sq", [128, 512], F32) for s_ in range(3)]
            ss = sbuf(es, "ss", [128, 12], F32)
            rst = sbuf(es, "rst", [128, 12], F32)
            qn = sbuf(es, "qnb", [128, 10, 128], F32)
            ra = sbuf(es, "ra", [128, 10, 64], F32)
            rb = sbuf(es, "rb", [128, 10, 64], F32)
            rc = sbuf(es, "rc", [128, 10, 64], F32)
            rd = sbuf(es, "rd", [128, 10, 64], F32)
            qr = sbuf(es, "qr", [128, 10, 128], BF16)
            cosT = R2[:, 12288:16384].bitcast(F32).rearrange("p (c j) -> p c j", c=NLAT)
            sinT = R2[:, 16384:20480].bitcast(F32).rearrange("p (c j) -> p c j", c=NLAT)
            ang = R2[:, 0:4096].bitcast(F32).rearrange("p (c j) -> p c j", c=NLAT)
            angi = R2[:, 4096:8192].bitcast(I32).rearrange("p (c j) -> p c j", c=NLAT)
            ang2 = R2[:, 8192:12288].bitcast(F32).rearrange("p (c j) -> p c j", c=NLAT)
            prc = sbuf(es, "prc", [128, NLAT, 2], F32)
            frq = sbuf(es, "frq", [128, 64], F32)
            gq_bc = sbuf(es, "gq_bc", [128, 128], F32)
            gk_bc = sbuf(es, "gk_bc", [128, 128], F32)
            qTs = [sbuf(es, "qTs", [128, 8, 128], BF16) for s_ in range(2)]
            kTs = [sbuf(es, "kTs", [128, 2, 128], BF16) for s_ in range(2)]
            vs = [sbuf(es, "vs", [128, 256], BF16) for s_ in range(2)]
            trp = psum(es, "trp", [128, 8, 128], F32)
            qkv = [psum(es, "qkv", [128, 512], F32) for s_ in range(3)]
            tq = psum(es, "tq", [128, 8, 128], BF16)
            tk = psum(es, "tk", [128, 8, 128], BF16)
            if prefetch:
                prefetch()

            def emit_w(e):
                srcw = w_qkv.rearrange("(k p) f -> p k f", p=128)
                return [e.dma_start(out=wq[:, kc, :], in_=srcw[:, kc, :]) for kc in range(8)]
            P.add("pool", emit_w, writes=["R1"], dma="wR1", ndma=8)
            DMA("sp", "prc", prc[:], posrc, writes=["prc"])
            DMA("sp", "frq", frq[:], freq2, writes=["frq"])
            DMA("sp", "gqbc", gq_bc[:], qn_d.partition_broadcast(128), writes=["gq_bc"])
            DMA("sp", "gkbc", gk_bc[:], kn_d.partition_broadcast(128), writes=["gk_bc"])
            for hh_ in range(2):
                P.add("dve", lambda e, hh_=hh_: e.tensor_tensor(
                    out=ang[:, :, hh_ * 32:(hh_ + 1) * 32], in0=prc[:, :, hh_:hh_ + 1].broadcast_to([128, NLAT, 32]),
                    in1=frq[:, hh_ * 32:(hh_ + 1) * 32].unsqueeze(1).broadcast_to([128, NLAT, 32]), op=ALU.mult),
                    reads=["prc", "frq"], writes=["ang"])
            P.add("dve", lambda e: e.tensor_scalar(out=ang[:], in0=ang[:], scalar1=1.0 / (2.0 * math.pi), scalar2=None, op0=ALU.mult),
                  reads=["ang"], writes=["ang"])
            P.add("dve", lambda e: e.tensor_copy(out=angi[:], in_=ang[:]), reads=["ang"], writes=["angi"])
            P.add("dve", lambda e: e.tensor_copy(out=ang2[:], in_=angi[:]), reads=["angi"], writes=["ang2"])
            P.add("dve", lambda e: e.tensor_tensor(out=ang[:], in0=ang[:], in1=ang2[:], op=ALU.subtract), reads=["ang", "ang2"], writes=["ang"])
            P.add("dve", lambda e: e.tensor_scalar(out=ang2[:], in0=ang[:], scalar1=-1.0, scalar2=None, op0=ALU.mult), reads=["ang"], writes=["ang2"])
            P.add("dve", lambda e: e.tensor_tensor(out=ang2[:], in0=ang2[:], in1=ang[:], op=ALU.max), reads=["ang", "ang2"], writes=["ang2"])
            P.add("act", lambda e: e.activation(out=sinT[:], in_=ang[:], func=AF.Sin, scale=math.pi), reads=["ang"], writes=["sinT"])
            P.add("dve", lambda e: e.tensor_scalar(out=ang2[:], in0=ang2[:], scalar1=-math.pi, scalar2=math.pi / 2, op0=ALU.mult, op1=ALU.add),
                  reads=["ang2"], writes=["ang2"])
            P.add("act", lambda e: e.activation(out=cosT[:], in_=ang2[:], func=AF.Sin), reads=["ang2"], writes=["cosT"])
            P.add("dve", lambda e: e.tensor_tensor(out=ang[:], in0=sinT[:], in1=sinT[:], op=ALU.mult), reads=["sinT"], writes=["ang"])
            P.add("dve", lambda e: e.scalar_tensor_tensor(out=sinT[:], in0=sinT[:], scalar=2.0, in1=cosT[:], op0=ALU.mult, op1=ALU.mult),
                  reads=["sinT", "cosT", "ang"], writes=["sinT"])
            P.add("dve", lambda e: e.tensor_scalar(out=cosT[:], in0=ang[:], scalar1=-2.0, scalar2=1.0, op0=ALU.mult, op1=ALU.add),
                  reads=["ang", "sinT"], writes=["cosT"])
            chunks = list(range(NCH))
            load_next = gen_loader("xc", NX, xc, chunks, src, "xc")
            for _ in range(NX):
                load_next()
            pending = [None]
            for k, c in enumerate(chunks):
                lat = c < NLAT
                cnd = 0 if lat else 1
                b = k % 2
                prep_chunk(xc[k % NX], f"xc{k % NX}", trp, "trp", xmT[b], f"xmT{b}", 0, layer, shift_mod, scale_mod, cnd)
                load_next()
                blks = (0, 1, 2) if lat else (2,)
                for blk in blks:
                    def pe(e, blk=blk, b=b):
                        r = None
                        for kc in range(8):
                            r = e.matmul(qkv[blk][:, :], lhsT=xmT[b][:, kc, :], rhs=wq[:, kc, blk * 512:(blk + 1) * 512],
                                         start=(kc == 0), stop=(kc == 7))
                        return r
                    P.add("pe", pe, reads=["R1", f"xmT{b}"], writes=[f"qkv{blk}"])
                    P.add("act", lambda e, blk=blk: e.activation(out=sq[blk][:], in_=qkv[blk][:, :], func=AF.Square),
                          reads=[f"qkv{blk}"], writes=[f"sq{blk}"])
                    P.add("dve", lambda e, blk=blk: e.tensor_reduce(out=ss[:, blk * 4:(blk + 1) * 4],
                                                                    in_=sq[blk][:].rearrange("p (h d) -> p h d", h=4), axis=AX.X, op=ALU.add),
                          reads=[f"sq{blk}"], writes=["ss"])
                P.add("dve", lambda e: e.tensor_scalar(out=rst[:], in0=ss[:], scalar1=1.0 / 128.0, scalar2=RMS_EPS, op0=ALU.mult, op1=ALU.add),
                      reads=["ss"], writes=["rst"])
                P.add("act", lambda e: e.activation(out=rst[:], in_=rst[:], func=AF.Sqrt), reads=["rst"], writes=["rst"])
                P.add("dve", lambda e: e.reciprocal(out=rst[:], in_=rst[:]), reads=["rst"], writes=["rst"])
                if pending[0] is not None:
                    pending[0]()
                    pending[0] = None
                heads = list(range(10)) if lat else [8, 9]
                for h in heads:
                    blk, off = (h // 4, (h % 4) * 128) if h < 8 else (2, (h - 8) * 128)
                    gb = gq_bc if h < 8 else gk_bc
                    P.add("dve", lambda e, h=h, blk=blk, off=off, gb=gb: e.scalar_tensor_tensor(
                        out=qn[:, h, :], in0=qkv[blk][:, off:off + 128], scalar=rst[:, (blk * 4 + (off // 128)):(blk * 4 + (off // 128)) + 1],
                        in1=gb[:], op0=ALU.mult, op1=ALU.mult), reads=[f"qkv{blk}", "rst", "gq_bc", "gk_bc"], writes=["qn"])
                if lat:
                    x1, x2 = qn[:, :, 0:64], qn[:, :, 64:128]
                    cb = cosT[:, c, :].unsqueeze(1).broadcast_to([128, 10, 64])
                    sb_ = sinT[:, c, :].unsqueeze(1).broadcast_to([128, 10, 64])
                    P.add("pool", lambda e, x1=x1, cb=cb: e.tensor_tensor(out=ra[:], in0=x1, in1=cb, op=ALU.mult), reads=["qn", "cosT"], writes=["ra"])
                    P.add("pool", lambda e, x2=x2, sb_=sb_: e.tensor_tensor(out=rb[:], in0=x2, in1=sb_, op=ALU.mult), reads=["qn", "sinT"], writes=["rb"])
                    P.add("dve", lambda e: e.tensor_tensor(out=qr[:, :, 0:64], in0=ra[:], in1=rb[:], op=ALU.subtract), reads=["ra", "rb"], writes=["qr"])
                    P.add("pool", lambda e, x2=x2, cb=cb: e.tensor_tensor(out=rc[:], in0=x2, in1=cb, op=ALU.mult), reads=["qn", "cosT"], writes=["rc"])
                    P.add("pool", lambda e, x1=x1, sb_=sb_: e.tensor_tensor(out=rd[:], in0=x1, in1=sb_, op=ALU.mult), reads=["qn", "sinT"], writes=["rd"])
                    P.add("dve", lambda e: e.tensor_tensor(out=qr[:, :, 64:128], in0=rc[:], in1=rd[:], op=ALU.add), reads=["rc", "rd"], writes=["qr"])
                else:
                    P.add("dve", lambda e: e.tensor_copy(out=qr[:, 8:10, :], in_=qn[:, 8:10, :]), reads=["qn"], writes=["qr"])

                P.add("act", lambda e, b=b: e.activation(out=vs[b][:], in_=qkv[2][:, 256:512], func=AF.Copy), reads=["qkv2"], writes=[f"vs{b}"])

                def stage_b(c=c, lat=lat, b=b):
                  def pe_t(e, lat=lat):
                      r = None
                      if lat:
                          for h in range(8):
                              r = e.transpose(out=tq[:, h, :], in_=qr[:, h, :], identity=ident_b[:])
                      for kv in range(2):
                          r = e.transpose(out=tk[:, kv, :], in_=qr[:, 8 + kv, :], identity=ident_b[:])
                      return r
                  P.add("pe", pe_t, reads=["qr", "ident_b"], writes=["tq", "tk"])
                  if lat:
                      P.add("act", lambda e, b=b: e.activation(out=qTs[b][:], in_=tq[:], func=AF.Copy), reads=["tq"], writes=[f"qTs{b}"])
                      DMA("sp", f"qTs{b}", qT_d[c].rearrange("p (h t) -> p h t", h=8), qTs[b][:], reads=[f"qTs{b}"], writes=[("qT", c)])
                  P.add("dve", lambda e, b=b: e.tensor_copy(out=kTs[b][:], in_=tk[:, 0:2, :]), reads=["tk"], writes=[f"kTs{b}"])
                  if lat:
                      def st_k(e, b=b, c=c):
                          return [e.dma_start(out=kin[kv].ap()[:, c * 128:(c + 1) * 128], in_=kTs[b][:, kv, :]) for kv in range(2)]
                      P.add("sp", st_k, reads=[f"kTs{b}"], writes=["kin"], dma=f"kTs{b}", ndma=2)
                      hv, j16 = c // 16, c % 16
                      DMA("sp", f"vs{b}", vin[hv].ap()[j16 * 8:(j16 + 1) * 8, :].rearrange("a (b c) -> (a b) c", c=256), vs[b][:],
                          reads=[f"vs{b}"], writes=["vin"])
                  else:
                      cc_ = c - NLAT

                      def st_k(e, b=b, cc_=cc_):
                          return [e.dma_start(out=kc_d[kv, :, cc_ * 128:(cc_ + 1) * 128], in_=kTs[b][:, kv, :]) for kv in range(2)]
                      P.add("sp", st_k, reads=[f"kTs{b}"], writes=["kc_d"], dma=f"kTs{b}", ndma=2)
                      DMA("sp", f"vs{b}", vc_d[cc_ * 128:(cc_ + 1) * 128, :], vs[b][:], reads=[f"vs{b}"], writes=["vc_d"])
                pending[0] = stage_b
            pending[0]()
            P.barrier()
            for i in range(2):
                P.add("pool", lambda e, i=i: e.collective_compute("AllGather", ALU.bypass, replica_groups=GROUPS,
                                                                   ins=[kin[i].ap().opt()], outs=[kout[i].ap().opt()]),
                      reads=["kin"], writes=[("kout", i)], dma=f"cck{i}", inc=1)
                P.add("pool", lambda e, i=i: e.collective_compute("AllGather", ALU.bypass, replica_groups=GROUPS,
                                                                   ins=[vin[i].ap().opt()], outs=[vout[i].ap().opt()]),
                      reads=["vin"], writes=[("vout", i)], dma=f"ccv{i}", inc=1)
            return done("QKV")

    def phase_ATT(xsrc, dst):
        layer = 1
        KT = BIG[:, 0:33280].rearrange("p (k t) -> p k t", k=2)
        V = BIG[:, 33280:66560].rearrange("p (c k d) -> p c k d", c=NKC, k=2)
        wo = BIG[:, 66560:74752].rearrange("p (h d) -> p h d", h=8)
        NP = NKC // 2
        with ExitStack() as es:
            tl = Tail(es, layer, 5, 1, nslots=1, conds=(0,))
            xc = sbuf(es, "xc", [128, D], F32)
            qTt = [sbuf(es, "qTt", [128, 8, 128], BF16) for s_ in range(2)]
            NPT = 5
            pt = [sbuf(es, "pt", [128, 1024], BF16) for s_ in range(NPT)]
            acc = {"dve": sbuf(es, "accD", [128, 1024], F32), "pool": sbuf(es, "accP", [128, 1024], F32)}
            accb = sbuf(es, "accb", [128, 2048], BF16)
            ones_b = sbuf(es, "ones_b", [128, 128], BF16)
            rinv = sbuf(es, "rinv", [128, 512], F32)
            oT = sbuf(es, "oT", [128, 8, 128], BF16)
            spsm = [psum(es, "spsm", [128, 1024], F32) for s_ in range(3)]
            OT = [psum(es, "OT", [128, 512], F32)]
            sump = psum(es, "sump", [128, 512], F32)
            yps = sump
            P.add("pool", lambda e: e.memset(ones_b[:], 1.0), writes=["ones_b"])
            vres = [("Vld", hv, rk) for hv in range(2) for rk in range(4)] + ["Vc"]
            for kv in range(2):
                def ld_k(e, kv=kv):
                    r = [e.dma_start(out=KT[:, kv, rk * 4096:(rk + 1) * 4096], in_=kout[kv].ap()[rk * 128:(rk + 1) * 128, :]) for rk in range(4)]
                    r.append(e.dma_start(out=KT[:, kv, 16384:16640], in_=kc_d[kv]))
                    return r
                P.add("sp", ld_k, reads=[("kout", kv), "kc_d", "R1"], writes=[("Kld", kv)], dma=f"ldk{kv}", ndma=5)
            for hv in range(2):
                for rk in range(4):
                    def ld_v(e, hv=hv, rk=rk):
                        srcv = vout[hv].ap()[rk * 128:(rk + 1) * 128, :].rearrange("(j a) (b k d) -> (a b) j k d", a=8, b=16, k=2)
                        c0 = rk * 32 + hv * 16
                        r = []
                        for kv in range(2):
                            for jh in range(2):
                                r.append(e.dma_start(out=V[:, c0 + jh * 8:c0 + (jh + 1) * 8, kv, :], in_=srcv[:, jh * 8:(jh + 1) * 8, kv, :]))
                        return r
                    P.add("sp", ld_v, reads=[("vout", hv), "R1", "R2"], writes=[("Vld", hv, rk)], dma=f"ldv{hv}{rk}", ndma=4)

            def ld_vc(e):
                srcv = vc_d.rearrange("(j p) (k d) -> p j k d", p=128, k=2)
                return [e.dma_start(out=V[:, 128:130, kv, :], in_=srcv[:, :, kv, :]) for kv in range(2)]
            P.add("sp", ld_vc, reads=["vc_d", "R2"], writes=["Vc"], dma="ldvc", ndma=2)

            def emit_wo(e):
                srcw = w_o_d.rearrange("(h p) d -> p h d", p=128)
                return [e.dma_start(out=wo[:, h, :], in_=srcw[:, h, :]) for h in range(8)]
            P.add("pool", emit_wo, reads=["R2"], writes=["wo"], dma="wo", ndma=8)
            P.add("pe", lambda e: None, reads=vres + [("Kld", 0), ("Kld", 1)], writes=["KVready"])
            qts = list(range(NLAT))
            nl = [0]

            def load_q():
                if nl[0] < len(qts):
                    c = qts[nl[0]]
                    s_ = nl[0] % 2
                    DMA("sp", f"qTt{s_}", qTt[s_][:].rearrange("p h t -> p (h t)"), qT_d[c], reads=[("qT", c)], writes=[f"qTt{s_}"])
                    nl[0] += 1
            load_q()
            load_q()
            pending_tail = [None]
            first = {"dve": True, "pool": True, "pe": True}
            SUMENG = ["dve", "pool", "dve", "pe", "dve", "pool", "dve", "pe"]
            for qi, c in enumerate(qts):
                s_ = qi % 2
                if qi == 0:
                    DMA("sp", "xc0", xc[:], xsrc[c * 128:(c + 1) * 128, :], reads=[("x", id(xsrc), c)], writes=["xc0"])
                for kv in range(2):
                    g = qi * 2 + kv
                    ot = OT[0]
                    otres = "OT0"
                    rhs = qTt[s_][:, kv * 4:(kv + 1) * 4, :].rearrange("p h t -> p (h t)")
                    first["dve"] = True
                    first["pool"] = True
                    first["pe"] = True

                    def S(p, kv=kv, rhs=rhs):
                        sp_ = spsm[p % 3]

                        def pe(e):
                            e.matmul(sp_[:, 0:512], lhsT=KT[:, kv, (2 * p) * 128:(2 * p + 1) * 128], rhs=rhs, start=True, stop=True)
                            return e.matmul(sp_[:, 512:1024], lhsT=KT[:, kv, (2 * p + 1) * 128:(2 * p + 2) * 128], rhs=rhs, start=True, stop=True)
                        P.add("pe", pe, reads=["KVready", f"qTt{s_}"], writes=[f"spsm{p % 3}"])
                        P.add("act", lambda e: e.activation(out=pt[p % NPT][:], in_=sp_[:, :], func=AF.Exp, scale=SCALE),
                              reads=[f"spsm{p % 3}"], writes=[f"pt{p % NPT}"])

                    def PV(p, kv=kv, ot=ot, otres=otres):
                        def pe(e):
                            e.matmul(ot[:, :], lhsT=V[:, 2 * p, kv, :], rhs=pt[p % NPT][:, 0:512], start=(p == 0), stop=False)
                            return e.matmul(ot[:, :], lhsT=V[:, 2 * p + 1, kv, :], rhs=pt[p % NPT][:, 512:1024], start=False, stop=(p == NP - 1))
                        P.add("pe", pe, reads=[f"pt{p % NPT}", "KVready"], writes=[otres])
                        en = SUMENG[p % 8]
                        if en == "pe":
                            st0 = first["pe"]
                            first["pe"] = False

                            def pes(e):
                                e.matmul(sump[:, :], lhsT=ones_b[:], rhs=pt[p % NPT][:, 0:512], start=st0, stop=False)
                                return e.matmul(sump[:, :], lhsT=ones_b[:], rhs=pt[p % NPT][:, 512:1024], start=False, stop=False)
                            P.add("pe", pes, reads=[f"pt{p % NPT}", "ones_b"], writes=["sump"])
                            return
                        a_ = acc[en]
                        if first[en]:
                            first[en] = False
                            P.add(en, lambda e: e.tensor_copy(out=a_[:], in_=pt[p % NPT][:]), reads=[f"pt{p % NPT}"], writes=[f"acc_{en}"])
                        else:
                            P.add(en, lambda e: e.tensor_tensor(out=a_[:], in0=a_[:], in1=pt[p % NPT][:], op=ALU.add),
                                  reads=[f"pt{p % NPT}", f"acc_{en}"], writes=[f"acc_{en}"])
                    S(0)
                    S(1)
                    for p in range(NP):
                        if p + 2 < NP:
                            S(p + 2)
                        PV(p)
                        if kv == 0 and p == 1 and pending_tail[0] is not None:
                            pending_tail[0]()
                            pending_tail[0] = None
                    P.add("dve", lambda e: e.tensor_copy(out=accb[:, 0:1024], in_=acc["dve"][:]), reads=["acc_dve"], writes=["accbD"])
                    P.add("dve", lambda e: e.tensor_copy(out=accb[:, 1024:2048], in_=acc["pool"][:]), reads=["acc_pool"], writes=["accbP"])

                    def pe_sum(e):
                        r = None
                        for i in range(4):
                            r = e.matmul(sump[:, :], lhsT=ones_b[:], rhs=accb[:, i * 512:(i + 1) * 512], start=False, stop=(i == 3))
                        return r
                    P.add("pe", pe_sum, reads=["ones_b", "accbD", "accbP"], writes=["sump"])
                    P.add("act", lambda e: e.activation(out=rinv[:], in_=sump[:, :], func=AF.Ln), reads=["sump"], writes=["rinv"])
                    P.add("act", lambda e: e.activation(out=rinv[:], in_=rinv[:], func=AF.Exp, scale=-1.0), reads=["rinv"], writes=["rinv"])
                    P.add("dve", lambda e, kv=kv, ot=ot: e.tensor_tensor(
                        out=oT[:, kv * 4:(kv + 1) * 4, :].rearrange("p h t -> p (h t)"), in0=ot[:, :], in1=rinv[:], op=ALU.mult),
                        reads=[otres, "rinv"], writes=["oT"])
                def tail_fn(c=c, qi=qi):
                    for half in range(2):
                        def pe(e, half=half):
                            r = None
                            for h in range(8):
                                r = e.matmul(yps[:, :], lhsT=oT[:, h, :], rhs=wo[:, h, half * 512:(half + 1) * 512], start=(h == 0), stop=(h == 7))
                            return r
                        P.add("pe", pe, reads=["oT", "wo"], writes=["sump"])
                        tl.piece(0, yps[:, :], "sump", half * 512, (half + 1) * 512, 0)
                    tl.rest(0, xc[:], "xc0", dst[c * 128:(c + 1) * 128, :], ("x", id(dst), c))
                    if qi + 1 < len(qts):
                        c2 = qts[qi + 1]
                        DMA("sp", "xc0", xc[:], xsrc[c2 * 128:(c2 + 1) * 128, :], reads=[("x", id(xsrc), c2)], writes=["xc0"])
                pending_tail[0] = tail_fn
                load_q()
            pending_tail[0]()
            return done("ATT")

    w2v = R2[:, 0:22 * 1024].rearrange("p (f d) -> p f d", f=22)
    gwo = R2[:, 0:24 * 1024].rearrange("p (f d) -> p f d", f=24)

    def load_gm_in():
        srcw = gm_w_in.rearrange("(k p) f -> p k f", p=128)
        w_in = R1.rearrange("p (k f) -> p k f", k=8)

        def emit(e):
            return [e.dma_start(out=w_in[:, kc, j * 2048:(j + 1) * 2048], in_=srcw[:, kc, j * 2048:(j + 1) * 2048]) for kc in range(8) for j in range(3)]
        P.add("pool", emit, writes=["R1"], dma="wR1", ndma=24, nobar=True)

    def load_gm_out():
        srcw = gm_w_out.rearrange("(f p) d -> p f d", p=128)

        def emit(e):
            return [e.dma_start(out=gwo[:, f, :], in_=srcw[:, f, :]) for f in range(24)]
        P.add("pool", emit, writes=["R2"], dma="wR2", ndma=24, nobar=True)

    allc = list(range(NCH))
    latc = list(range(NLAT))
    rowf = lambda c: c * 128
    stop = fin
    cur = None
    sched = [
        ("FA00", lambda: phase_FA(0, 0, xin, True)),
        ("FB00", lambda: phase_B("FB00", 22, w2v, 0, 2, 0, xin, xa, allc, rowf, prefetch=load_gm_in)),
        ("G1", lambda: phase_G1(xa)),
        ("G2", lambda: phase_B("G2", 24, gwo, 0, 5, 1, xa, xb, allc, rowf, prefetch=lambda: (load_gm_out(), load_w1(0, 1)))),
        ("FA01", lambda: phase_FA(0, 1, xb, True, prefetch=lambda: load_w2(0, 1))),
        ("FB01", lambda: phase_B("FB01", 22, w2v, 0, 8, 2, xb, xa, allc, rowf, prefetch=lambda: load_w1(1, 0))),
        ("FA10", lambda: phase_FA(1, 0, xa, True, prefetch=lambda: load_w2(1, 0))),
        ("FB10", lambda: phase_B("FB10", 22, w2v, 1, 2, 0, xa, xb, allc, rowf)),
        ("QKV", lambda: phase_QKV(xb)),
        ("ATT", lambda: phase_ATT(xb, xa)),
        ("FA11", lambda: phase_FA(1, 1, xa, False, prefetch=lambda: (load_w1(1, 1, nobar=False), load_w2(1, 1, nobar=False)))),
        ("FB11", lambda: phase_B("FB11", 22, w2v, 1, 8, 2, xa, out, latc, rowf)),
    ]
    streams = {"FB00": xa, "G2": xb, "FB01": xa, "FB10": xb, "ATT": xa}
    for name, fn in sched:
        if stop:
            break
        stop = fn()
        if name in streams:
            cur = streams[name]
        elif name != "FB11":
            cur = None if name in ("FA00",) else cur

    if stop_after == "P0":
        DMA("sp", "dbgm", out[0:36, :].rearrange("(a r) c -> a (r c)", a=4), mrow_d.rearrange("l c n -> (l c) n"), reads=["mrow_d"], writes=[("out", 0)])
        DMA("sp", "dbgc", out[128:256, 0:288], mcol[:].rearrange("p l a b -> p (l a b)"), reads=["mcol"], writes=[("out", 1)])
    if stop_after == "FA00":
        DMA("pool", "dbgh", out[0:384, :].rearrange("(p r) c -> p (r c)", r=3), hT_d[0].rearrange("p f t -> p (f t)"), writes=[("out", 0)])
    if stop and cur is not None:
        for i in range(8):
            DMA("sp", f"dbg{i}", out[i * 512:(i + 1) * 512, :], cur[i * 512:(i + 1) * 512, :], writes=[("out", i)])
    P.add("sp", lambda e: None, reads=[("out", i) for i in range(8)] + [("x", id(out), c) for c in range(NLAT)])

    with nc.Block() as block:
        P.emit_all(nc, block, lambda name: root.enter_context(nc.semaphore(name)))
    root.close()
    return nc, P


def _prep_inputs(inp):
    f = lambda a: np.ascontiguousarray(np.asarray(a, dtype=np.float32))
    x, c, ctx, c_ctx = f(inp["x"]), f(inp["c"]), f(inp["ctx"]), f(inp["c_ctx"])
    shared = dict(
        ident=np.eye(128, dtype=np.float32),
        w_mod=f(inp["w_mod"]),
        bmod2=f(np.broadcast_to(f(inp["b_mod"])[None], (2, 2, 9216))),
        ln_g=f(inp["ln_g"]), ln_b=f(inp["ln_b"]),
        ffn_w_in=f(inp["ffn_w_in"]), ffn_w_out=f(inp["ffn_w_out"]),
        gm_w_in=f(inp["gmlp_w_in"][0]),
        glng=f(f(inp["gmlp_ln_g"])[0].reshape(24, 128).T),
        glnb=f(f(inp["gmlp_ln_b"])[0].reshape(24, 128).T),
        wsT=f(f(inp["gmlp_w_s"])[0].transpose(2, 0, 1)),
        gbs=f(f(inp["gmlp_b_s"])[0].reshape(-1)),
        gm_w_out=f(inp["gmlp_w_out"][0]),
        w_qkv=f(inp["attn_w_qkv"][0]),
        qn=f(inp["attn_q_norm"][0]), kn=f(inp["attn_k_norm"][0]),
        w_o=f(inp["attn_w_o"][0]),
    )
    quarter = 32
    fr = (np.float32(10000.0) ** (-np.arange(quarter, dtype=np.float32) / np.float32(quarter))).astype(np.float32)
    shared["freq2"] = f(np.broadcast_to(np.concatenate([fr, fr])[None], (128, 64)))
    in_maps = []
    for i in range(8):
        b, j = i // 4, i % 4
        m = dict(shared)
        m["xin"] = f(np.concatenate([x[b, j * 4096:(j + 1) * 4096], ctx[b]], axis=0))
        cd = np.stack([c[b], c_ctx], axis=-1)
        m["cond"] = f(cd.reshape(8, 128, 2).transpose(1, 0, 2))
        t = j * 4096 + np.arange(4096)
        rc = np.stack([t // 64, t % 64], axis=-1).astype(np.float32)
        m["posrc"] = f(rc.reshape(32, 128, 2).transpose(1, 0, 2))
        in_maps.append(m)
    return in_maps


_CACHE = {}


def run(inp, stop_after=None, trace=False):
    key = stop_after
    if key not in _CACHE:
        _CACHE[key] = build(stop_after)
    nc, P = _CACHE[key]
    in_maps = _prep_inputs(inp)
    res = run_bass_kernel_spmd(nc, in_maps, core_ids=list(range(8)), trace=trace)
    outs = [np.asarray(r["out"], dtype=np.float32) for r in res.results]
    full = np.stack([np.concatenate(outs[0:4], axis=0), np.concatenate(outs[4:8], axis=0)], axis=0)
    return full, res


def kernel(**inputs):
    full, _ = run(inputs)
    return full
```
